# Optimizing a Trainium2 kernel written in Bass

```python
import math
import jax, jax.numpy as jnp
from jax import lax
import numpy as np

D_MODEL = 1024
BATCH = 8
SEQ = 8192
DEPTH = 4

CHUNK = 64
Q_BLOCK = 128
RET_HEADS = 4
RET_QK_DIM = 128
RET_V_DIM = 256
DIFF_HEADS = 8
DIFF_HEAD_DIM = 64
DIFF_V_DIM = 2 * DIFF_HEAD_DIM
D_FF = 2816
N_SUB = 3
NORM_EPS = 1e-6

RET_QK = RET_HEADS * RET_QK_DIM
RET_V = RET_HEADS * RET_V_DIM
DIFF_QK = DIFF_HEADS * 2 * DIFF_HEAD_DIM
DIFF_V = DIFF_HEADS * DIFF_V_DIM
SPLITS = (RET_QK, RET_QK, RET_V, RET_V, DIFF_QK, DIFF_QK, DIFF_V, D_MODEL, D_MODEL)
IN_WIDTH = sum(SPLITS)
SPLIT_IDX = [int(i) for i in np.cumsum(SPLITS)[:-1]]

kernel_name = "hybrid_retention_diffattn_macaron_adaln"


def rmsnorm(x, w):
    xf = x.astype(jnp.float32)
    y = xf * lax.rsqrt(jnp.mean(xf * xf, axis=-1, keepdims=True) + NORM_EPS)
    return (y * w.astype(jnp.float32)).astype(x.dtype)


def modulate(h, shift, scale):
    return h * (1 + scale[:, None, :]) + shift[:, None, :]


def swiglu(h, w_up, w_down):
    a, b = jnp.split(h @ w_up, 2, axis=-1)
    return (jax.nn.silu(a) * b) @ w_down


def retention(q, k, v):
    B, S, H, dk = q.shape
    dv = v.shape[-1]
    n_chunks = S // CHUNK
    f32 = jnp.float32
    gamma = 1.0 - 2.0 ** (-5.0 - jnp.arange(H, dtype=f32))
    log_g = jnp.log(gamma)
    r = jnp.arange(CHUNK, dtype=f32)
    intra = jnp.exp(log_g[:, None, None] * jnp.abs(r[:, None] - r[None, :]))
    q_dec = jnp.exp(log_g[:, None] * r[None, :])
    k_dec = jnp.exp(log_g[:, None] * (CHUNK - r)[None, :])
    chunk_dec = jnp.exp(log_g * CHUNK)

    def to_chunks(t):
        return t.astype(f32).reshape(B, n_chunks, CHUNK, H, t.shape[-1]).transpose(1, 0, 3, 2, 4)

    qc, kc, vc = to_chunks(q), to_chunks(k) * (dk ** -0.5), to_chunks(v)

    def step(state, inp):
        qi, ki, vi = inp
        scores = jnp.einsum('bhqd,bhkd->bhqk', qi, ki) * intra
        y = jnp.einsum('bhqk,bhkv->bhqv', scores, vi)
        y = y + jnp.einsum('bhqd,bhdv->bhqv', qi * q_dec[:, :, None], state)
        state = state * chunk_dec[:, None, None] + jnp.einsum('bhkd,bhkv->bhdv', ki * k_dec[:, :, None], vi)
        return state, y

    state0 = jnp.zeros((B, H, dk, dv), f32)
    _, y = lax.scan(step, state0, (qc, kc, vc))
    return y.transpose(1, 0, 3, 2, 4).reshape(B, S, H, dv)


def alibi_slopes(n_heads):
    return 2.0 ** (-8.0 * jnp.arange(1, n_heads + 1, dtype=jnp.float32) / n_heads)


def diff_attention(q, k, v, lam):
    B, S, H, _, d = q.shape
    nb = S // Q_BLOCK
    f32 = jnp.float32
    qb = q.reshape(B, nb, Q_BLOCK, H, 2, d).transpose(1, 0, 3, 4, 2, 5)
    kt = k.transpose(0, 2, 3, 1, 4)
    vt = v.transpose(0, 2, 1, 3)
    slopes = alibi_slopes(H)
    key_pos = jnp.arange(S)
    scale = d ** -0.5

    def block(inp):
        qi, bi = inp
        q_pos = bi * Q_BLOCK + jnp.arange(Q_BLOCK)
        allowed = (key_pos[None, :] // CHUNK) <= (q_pos[:, None] // CHUNK)
        dist = jnp.abs(q_pos[:, None] - key_pos[None, :]).astype(f32)
        bias = jnp.where(allowed[None], -slopes[:, None, None] * dist[None], -jnp.inf)
        s = jnp.einsum('bhiqd,bhikd->bhiqk', qi, kt).astype(f32) * scale + bias[None, :, None]
        p = jax.nn.softmax(s, axis=-1)
        a = p[:, :, 0] - lam * p[:, :, 1]
        return jnp.einsum('bhqk,bhkv->bhqv', a.astype(vt.dtype), vt)

    out = lax.map(block, (qb, jnp.arange(nb)))
    return out.transpose(1, 0, 3, 2, 4).reshape(B, S, H, 2 * d)


def hybrid_mixer(h, w_in, ret_gn, lam_q1, lam_k1, lam_q2, lam_k2, subln, w_ret_branch, w_diff_branch, w_out, layer_idx):
    B, S, _ = h.shape
    f32 = jnp.float32
    proj = h @ w_in
    rq, rk, rv, rg, dq, dk, dv, g_ret, g_diff = jnp.split(proj, SPLIT_IDX, axis=-1)

    y_ret = retention(rq.reshape(B, S, RET_HEADS, RET_QK_DIM),
                      rk.reshape(B, S, RET_HEADS, RET_QK_DIM),
                      rv.reshape(B, S, RET_HEADS, RET_V_DIM))
    y_ret = rmsnorm(y_ret, ret_gn.reshape(RET_HEADS, RET_V_DIM)).reshape(B, S, RET_V).astype(h.dtype)
    y_ret = y_ret * jax.nn.silu(rg)

    lam_init = 0.8 - 0.6 * math.exp(-0.3 * layer_idx)
    lam = (jnp.exp(jnp.sum(lam_q1.astype(f32) * lam_k1.astype(f32)))
           - jnp.exp(jnp.sum(lam_q2.astype(f32) * lam_k2.astype(f32))) + lam_init)
    y_diff = diff_attention(dq.reshape(B, S, DIFF_HEADS, 2, DIFF_HEAD_DIM),
                            dk.reshape(B, S, DIFF_HEADS, 2, DIFF_HEAD_DIM),
                            dv.reshape(B, S, DIFF_HEADS, DIFF_V_DIM), lam)
    y_diff = (rmsnorm(y_diff, subln) * (1.0 - lam_init)).reshape(B, S, DIFF_V)

    merged = (jax.nn.sigmoid(g_ret) * (y_ret @ w_ret_branch)
              + jax.nn.sigmoid(g_diff) * (y_diff @ w_diff_branch))
    return merged @ w_out


def setup_inputs(seed: int = 0) -> dict:
    key = jax.random.key(seed)
    ks = jax.random.split(key, 20)
    nrm = jax.random.normal
    f32 = jnp.float32
    return {
        "x": nrm(ks[0], (BATCH, SEQ, D_MODEL), f32),
        "c": nrm(ks[1], (BATCH, D_MODEL), f32),
        "w_ada": nrm(ks[2], (DEPTH, D_MODEL, N_SUB * 3 * D_MODEL), f32) * (0.5 * D_MODEL ** -0.5),
        "b_ada": nrm(ks[3], (DEPTH, N_SUB * 3 * D_MODEL), f32) * 0.02,
        "norm_w": 1.0 + 0.02 * nrm(ks[4], (DEPTH, N_SUB, D_MODEL), f32),
        "w_ffn_up": nrm(ks[5], (DEPTH, 2, D_MODEL, 2 * D_FF), f32) * D_MODEL ** -0.5,
        "w_ffn_down": nrm(ks[6], (DEPTH, 2, D_FF, D_MODEL), f32) * D_FF ** -0.5,
        "w_in": nrm(ks[7], (DEPTH, D_MODEL, IN_WIDTH), f32) * D_MODEL ** -0.5,
        "ret_gn": 1.0 + 0.02 * nrm(ks[8], (DEPTH, RET_V), f32),
        "lambda_q1": 0.1 * nrm(ks[9], (DEPTH, DIFF_HEAD_DIM), f32),
        "lambda_k1": 0.1 * nrm(ks[10], (DEPTH, DIFF_HEAD_DIM), f32),
        "lambda_q2": 0.1 * nrm(ks[11], (DEPTH, DIFF_HEAD_DIM), f32),
        "lambda_k2": 0.1 * nrm(ks[12], (DEPTH, DIFF_HEAD_DIM), f32),
        "diff_subln": 1.0 + 0.02 * nrm(ks[13], (DEPTH, DIFF_V_DIM), f32),
        "w_ret_branch": nrm(ks[14], (DEPTH, RET_V, D_MODEL), f32) * RET_V ** -0.5,
        "w_diff_branch": nrm(ks[15], (DEPTH, DIFF_V, D_MODEL), f32) * DIFF_V ** -0.5,
        "w_out": nrm(ks[16], (DEPTH, D_MODEL, D_MODEL), f32) * D_MODEL ** -0.5,
        "final_norm": 1.0 + 0.02 * nrm(ks[17], (D_MODEL,), f32),
    }


def reference(x, c, w_ada, b_ada, norm_w, w_ffn_up, w_ffn_down, w_in, ret_gn, lambda_q1, lambda_k1, lambda_q2, lambda_k2, diff_subln, w_ret_branch, w_diff_branch, w_out, final_norm):
    B = x.shape[0]
    cond = jax.nn.silu(c)
    for l in range(DEPTH):
        mod = (cond @ w_ada[l] + b_ada[l]).reshape(B, N_SUB, 3, D_MODEL)
        shift, scale, gate = mod[:, :, 0], mod[:, :, 1], mod[:, :, 2]
        h = modulate(rmsnorm(x, norm_w[l, 0]), shift[:, 0], scale[:, 0])
        x = x + 0.5 * gate[:, 0, None, :] * swiglu(h, w_ffn_up[l, 0], w_ffn_down[l, 0])
        h = modulate(rmsnorm(x, norm_w[l, 1]), shift[:, 1], scale[:, 1])
        x = x + gate[:, 1, None, :] * hybrid_mixer(h, w_in[l], ret_gn[l], lambda_q1[l], lambda_k1[l],
                                                    lambda_q2[l], lambda_k2[l], diff_subln[l],
                                                    w_ret_branch[l], w_diff_branch[l], w_out[l], l)
        h = modulate(rmsnorm(x, norm_w[l, 2]), shift[:, 2], scale[:, 2])
        x = x + 0.5 * gate[:, 2, None, :] * swiglu(h, w_ffn_up[l, 1], w_ffn_down[l, 1])
    return rmsnorm(x, final_norm)
```

```python
import math
import numpy as np
import concourse.bass as bass
import concourse.mybir as mybir
from concourse.bass_utils import run_bass_kernel_spmd

F32 = mybir.dt.float32
BF16 = mybir.dt.bfloat16
AF = mybir.ActivationFunctionType
ALU = mybir.AluOpType
AX = mybir.AxisListType

D = 1024
KC = 8
T = 512
DFF = 2816
NF = 22
EPS = 1e-6
KDMA = 8
SKIP_T = 128.0
NLAYERS = 4
import os
DBG = int(os.environ.get('KDBG', '9'))
DBG2 = int(os.environ.get('KDBG2', '9'))
SEQ = 8192


class Op:
    __slots__ = ("eng", "fn", "deps", "dma", "sig", "cnt", "semkey", "qn", "bar")


class Prog:
    def __init__(self):
        self.streams = {e: [] for e in ("pe", "act", "dve", "pool", "sp")}
        self.lastw = {}
        self.readers = {}
        self.nbar = 0

    def add(self, eng, fn, reads=(), writes=(), dma=False):
        op = Op()
        op.eng = eng; op.fn = fn; op.dma = dma; op.sig = False; op.bar = 0
        op.cnt = 0; op.semkey = None; op.qn = 0
        deps = {}
        for r in reads:
            w = self.lastw.get(r)
            if w is not None:
                deps[id(w)] = (w, 0)
            if r.startswith("ps"):
                rd = self.readers.get(r)
                if rd:
                    for k, o in rd.items():
                        if k != "dma" and k != eng and id(o) not in deps:
                            deps[id(o)] = (o, 3)
        for r in writes:
            w = self.lastw.get(r)
            if w is not None and id(w) not in deps:
                deps[id(w)] = (w, 1)
            rd = self.readers.get(r)
            if rd:
                for k, o in rd.items():
                    if k == "dma":
                        for oo in o:
                            if id(oo) not in deps:
                                deps[id(oo)] = (oo, 2)
                    elif id(o) not in deps:
                        deps[id(o)] = (o, 2)
        dl = []
        for w, kind in deps.values():
            if (not w.dma) and (not dma) and w.eng == eng:
                if eng == "pe":
                    continue
            dl.append(w)
            w.sig = True
        op.deps = dl
        for r in writes:
            self.lastw[r] = op
            self.readers[r] = {}
        for r in reads:
            rd = self.readers.setdefault(r, {})
            if dma:
                rd.setdefault("dma", []).append(op)
            else:
                rd[eng] = op
        self.streams[eng].append(op)
        return op

    def barrier(self):
        self.nbar += 1
        for e, st in self.streams.items():
            for o in reversed(st):
                if o.bar:
                    break
                if not o.dma:
                    o.sig = True
                    break
            op = Op()
            op.eng = e; op.fn = None; op.dma = False; op.sig = False; op.bar = self.nbar
            op.deps = []; op.cnt = 0; op.semkey = None; op.qn = 0
            st.append(op)
        self.lastw = {}
        self.readers = {}

    def finalize(self):
        for e, st in self.streams.items():
            c = 0
            q = 0
            for op in st:
                if op.bar:
                    continue
                if op.dma:
                    op.semkey = "%s_d%d" % (e, q % KDMA)
                    op.cnt = 16 * (q // KDMA + 1)
                    op.qn = q
                    q += 1
                elif op.sig:
                    c += 1
                    op.cnt = c
                    op.semkey = e

    def run_stream(self, ename, eng, sems):
        waited = {}

        def wait(key, val):
            if waited.get(key, 0) < val:
                eng.wait_ge(sems[key], val)
                waited[key] = val

        own = 0
        dtot = {}
        for op in self.streams[ename]:
            if op.bar:
                if ename != "sp" and own > 0:
                    wait(ename, own)
                for k, v in dtot.items():
                    wait(k, v)
                eng.sem_inc(sems["bar"], 1)
                wait("bar", 5 * op.bar)
                continue
            for d in op.deps:
                wait(d.semkey, d.cnt)
            if op.dma and op.qn >= KDMA:
                wait(op.semkey, op.cnt - 16)
            ins = op.fn(eng)
            if op.dma:
                ins.then_inc(sems[op.semkey], 16)
                dtot[op.semkey] = op.cnt
            elif op.sig:
                ins.then_inc(sems[ename], 1)
                own = op.cnt


class Mem:
    def __init__(self, nc):
        self.nc = nc
        self.off = 16576
        self.n = 0
        self.base = 16576

    def alloc(self, shape, dtype):
        nb = 1
        for s in shape[1:]:
            nb *= s
        nb *= 4 if dtype == F32 else 2
        nb = (nb + 63) // 64 * 64
        h = self.nc.alloc_sbuf_tensor_at("sb%d" % self.n, list(shape), dtype, offset=self.off)
        self.n += 1
        self.off += nb
        assert self.off <= 229376, self.off
        return h

    def set_base(self):
        self.base = self.off

    def reset(self):
        self.off = self.base


def lam_init_of(l):
    return 0.8 - 0.6 * math.exp(-0.3 * l)


def build_program(S, depth, upto=99):
    NT = S // T
    NKT = S // 128
    nc = bass.Bass("TRN2", target_bir_lowering=False)
    P = Prog()
    mem = Mem(nc)

    def din(name, shape, dt=F32):
        return nc.dram_tensor(name, list(shape), dt, kind="ExternalInput")

    xT = din("xT", [D, S])
    cT = din("cT", [128, 8])
    w_ada = din("w_ada", [depth, D, 9216])
    b_adaT = din("b_adaT", [128, depth * 72])
    norm_wT = din("norm_wT", [128, depth * 24])
    w_up = din("w_up", [depth, 2, D, 2 * DFF])
    w_down = din("w_down", [depth, 2, DFF, D])
    w_in = din("w_in", [depth, D, 8192])
    ret_gnT = din("ret_gnT", [128, depth * 8])
    lam_in = din("lam_in", [128, 4 * depth * 64])
    sublnT = din("sublnT", [128, depth])
    w_rb = din("w_rb", [depth, D, D])
    w_db = din("w_db", [depth, D, D])
    w_out = din("w_out", [depth, D, D])
    fnT = din("fnT", [128, 8])
    qaug = din("qaug", [4, S])
    kaug = din("kaug", [8, 4, S])
    c0d = din("c0d", [128, 4, 512])
    dtd = din("dtd", [128, 4, 4, 512])
    qdecd = din("qdecd", [128, 4, 512])
    kdecd = din("kdecd", [128, 16])
    yT = nc.dram_tensor("yT", [D, S], F32, kind="ExternalOutput")

    def dscr(name, shape):
        return nc.dram_tensor(name, list(shape), BF16, kind="Internal")

    rqT = dscr("rqT", [4, 128, S]); rqdT = dscr("rqdT", [4, 128, S]); rkT = dscr("rkT", [4, 128, S])
    rkd = dscr("rkd", [128, NKT, 512]); rv = dscr("rv", [128, NKT, 1024])
    rgT = dscr("rgT", [D, S]); dqT = dscr("dqT", [8, 128, S]); dkT = dscr("dkT", [8, 128, S])
    dvs = dscr("dvs", [128, NKT, 1024]); sgrT = dscr("sgrT", [D, S]); sgdT = dscr("sgdT", [D, S])
    yrT = dscr("yrT", [D, S]); ydT = dscr("ydT", [D, S])

    ps = [nc.alloc_psum_tensor("ps%d" % i, [128, 512], F32) for i in range(8)]

    MOD = mem.alloc([128, depth * 72], F32)
    ACO = mem.alloc([128, depth * 24], F32)
    GCO = mem.alloc([128, depth * 24], F32)
    ones_b = mem.alloc([128, 128], BF16)
    ones_f = mem.alloc([128, 128], F32)
    neglam = mem.alloc([128, depth], F32)
    cdiff = mem.alloc([128, depth], F32)
    cret = mem.alloc([128, depth * 8], F32)
    fnc = mem.alloc([128, 8], F32)
    kdec = mem.alloc([128, 16], F32)
    epsc = mem.alloc([128, 4], F32)
    mem.set_base()

    x_view = xT.rearrange("(kc p) s -> p kc s", p=128)
    y_view = yT.rearrange("(kc p) s -> p kc s", p=128)

    def act(fn, r, w):
        return P.add("act", fn, r, w)

    def dve(fn, r, w):
        return P.add("dve", fn, r, w)

    def pool(fn, r, w):
        return P.add("pool", fn, r, w)

    def pe(fn, r, w):
        return P.add("pe", fn, r, w)

    def dma_sp(fn, r, w):
        return P.add("sp", fn, r, w, dma=True)

    def dma_pool(fn, r, w):
        return P.add("pool", fn, r, w, dma=True)

    def phase_pre():
        mem.reset()
        cnd = mem.alloc([128, 8], F32)
        cnd2 = mem.alloc([128, 8], F32)
        bad = mem.alloc([128, depth * 72], F32)
        nwt = mem.alloc([128, depth * 24], F32)
        rgn = mem.alloc([128, depth * 8], F32)
        sbl = mem.alloc([128, depth], F32)
        fnt = mem.alloc([128, 8], F32)
        lmi = mem.alloc([128, 4 * depth * 64], F32)
        lpr = mem.alloc([128, 2 * depth * 64], F32)
        lsum = mem.alloc([128, 2 * depth], F32)
        lexp = mem.alloc([128, 2 * depth], F32)
        ldif = mem.alloc([128, depth], F32)
        wa = [mem.alloc([128, 8, 1152], F32) for _ in range(2)]

        dma_sp(lambda e: e.dma_start(out=cnd[:, :], in_=cT[:, :]), [], ["cnd"])
        dma_sp(lambda e: e.dma_start(out=bad[:, :], in_=b_adaT[:, :]), [], ["bad"])
        dma_sp(lambda e: e.dma_start(out=nwt[:, :], in_=norm_wT[:, :]), [], ["nwt"])
        dma_sp(lambda e: e.dma_start(out=rgn[:, :], in_=ret_gnT[:, :]), [], ["rgn"])
        dma_sp(lambda e: e.dma_start(out=sbl[:, :], in_=sublnT[:, :]), [], ["sbl"])
        dma_sp(lambda e: e.dma_start(out=fnt[:, :], in_=fnT[:, :]), [], ["fnt"])
        dma_sp(lambda e: e.dma_start(out=lmi[:, :], in_=lam_in[:, :]), [], ["lmi"])
        dma_sp(lambda e: e.dma_start(out=kdec[:, :], in_=kdecd[:, :]), [], ["kdec"])
        dve(lambda e: e.memset(ones_b[:, :], 1.0), [], ["ones_b"])
        dve(lambda e: e.memset(ones_f[:, :], 1.0), [], ["ones_f"])
        dve(lambda e: e.memset(epsc[:, 0:1], float(D * EPS)), [], ["epsc"])
        dve(lambda e: e.memset(epsc[:, 1:2], float(128 * EPS)), [], ["epsc"])
        dve(lambda e: e.memset(epsc[:, 2:3], float(256 * EPS)), [], ["epsc"])
        act(lambda e: e.activation(cnd2[:, :], cnd[:, :], AF.Sigmoid), ["cnd"], ["cnd2"])
        dve(lambda e: e.tensor_tensor(cnd2[:, :], cnd2[:, :], cnd[:, :], ALU.mult), ["cnd", "cnd2"], ["cnd2"])
        LD = depth * 64
        dve(lambda e: e.tensor_tensor(lpr[:, 0:LD], lmi[:, 0:LD], lmi[:, LD:2 * LD], ALU.mult), ["lmi"], ["lpr0"])
        dve(lambda e: e.tensor_tensor(lpr[:, LD:2 * LD], lmi[:, 2 * LD:3 * LD], lmi[:, 3 * LD:4 * LD], ALU.mult), ["lmi"], ["lpr1"])
        dve(lambda e: e.reduce_sum(lsum[:, :], lpr[:, :].rearrange("p (a d) -> p a d", d=64), AX.X), ["lpr0", "lpr1"], ["lsum"])
        act(lambda e: e.activation(lexp[:, :], lsum[:, :], AF.Exp), ["lsum"], ["lexp"])
        dve(lambda e: e.tensor_tensor(ldif[:, :], lexp[:, depth:2 * depth], lexp[:, 0:depth], ALU.subtract), ["lexp"], ["ldif"])
        for l in range(depth):
            li = lam_init_of(l)
            dve(lambda e, l=l, li=li: e.tensor_scalar_add(neglam[:, l:l + 1], ldif[:, l:l + 1], -li), ["ldif"], ["neglam"])
            dve(lambda e, l=l, li=li: e.tensor_scalar_mul(cdiff[:, l:l + 1], sbl[:, l:l + 1], (1.0 - li) * math.sqrt(128.0)), ["sbl"], ["cdiff"])
        dve(lambda e: e.tensor_scalar_mul(cret[:, :], rgn[:, :], 16.0), ["rgn"], ["cret"])
        dve(lambda e: e.tensor_scalar_mul(fnc[:, :], fnt[:, :], 32.0), ["fnt"], ["fnc"])
        nb = 0
        for l in range(depth):
            wv = w_ada[l].rearrange("(kc p) n -> p kc n", p=128)
            for j in range(8):
                buf = wa[nb % 2]
                rn = "wa%d" % (nb % 2)
                nb += 1
                for kc in range(8):
                    dma_sp(lambda e, buf=buf, kc=kc, j=j, wv=wv: e.dma_start(out=buf[:, kc, :], in_=wv[:, kc, j * 1152:(j + 1) * 1152]), [], [rn + "_%d" % kc])
                for m in range(9):
                    col = l * 72 + j * 9 + m
                    for kc in range(8):
                        pe(lambda e, buf=buf, kc=kc, m=m, col=col: e.matmul(ps[0][:, col:col + 1], buf[:, kc, m * 128:(m + 1) * 128], cnd2[:, kc:kc + 1], start=(kc == 0), stop=(kc == 7)),
                           [rn + "_%d" % kc, "cnd2"], ["psM"])
        dve(lambda e: e.tensor_tensor(MOD[:, :], ps[0][:, 0:depth * 72], bad[:, :], ALU.add), ["psM", "bad"], ["MOD"])
        for l in range(depth):
            for s in range(3):
                o = (l * 3 + s) * 8
                sc = l * 72 + s * 24 + 8
                gc = l * 72 + s * 24 + 16
                dve(lambda e, o=o, sc=sc: e.scalar_tensor_tensor(ACO[:, o:o + 8], MOD[:, sc:sc + 8], 1.0, nwt[:, o:o + 8], ALU.add, ALU.mult), ["MOD", "nwt"], ["ACO%d" % o])
                dve(lambda e, o=o: e.tensor_scalar_mul(ACO[:, o:o + 8], ACO[:, o:o + 8], 32.0), ["ACO%d" % o], ["ACO%d" % o])
                dve(lambda e, o=o, gc=gc, s=s: e.tensor_scalar_mul(GCO[:, o:o + 8], MOD[:, gc:gc + 8], 1.0 if s == 1 else 0.5), ["MOD"], ["GCO"])
        P.barrier()

    def emit_norm(xt, xr, ht, hr, fs, rstd, l, s, tag):
        for c in range(8):
            k = c % 3
            act(lambda e, c=c, k=k: e.activation(fs[k][:, :], xt[:, c, :], AF.Square), [xr], ["fs%d" % k])
            pe(lambda e, c=c, k=k: e.matmul(ps[0][:, :], ones_f[:, :], fs[k][:, :], start=(c == 0), stop=(c == 7)), ["fs%d" % k], ["ps0"])
        act(lambda e: e.activation(rstd[:, :], ps[0][:, :], AF.Ln, bias=epsc[:, 0:1], scale=1.0), ["ps0"], ["rstd"])
        act(lambda e: e.activation(rstd[:, :], rstd[:, :], AF.Exp, scale=-0.5), ["rstd"], ["rstd"])
        o = (l * 3 + s) * 8
        sh = l * 72 + s * 24
        for c in range(8):
            k = c % 3
            dve(lambda e, c=c, k=k: e.tensor_tensor(fs[k][:, :], xt[:, c, :], rstd[:, :], ALU.mult), [xr, "rstd"], ["fs%d" % k])
            act(lambda e, c=c, k=k: e.activation(ht[:, c, :], fs[k][:, :], AF.Identity, bias=MOD[:, sh + c:sh + c + 1], scale=ACO[:, o + c:o + c + 1]),
                ["fs%d" % k], [hr])

    def phase_ffn(l, fi, s, src_view):
        mem.reset()
        wup = mem.alloc([128, 8, 2 * DFF], BF16)
        wdn = mem.alloc([128, NF, D], BF16)
        xt = [mem.alloc([128, 8, T], F32) for _ in range(2)]
        ht = mem.alloc([128, 8, T], BF16)
        gt = mem.alloc([128, NF, T], BF16)
        fs = [mem.alloc([128, T], F32) for _ in range(3)]
        rstd = mem.alloc([128, T], F32)
        upv = w_up[l, fi].rearrange("(kc p) n -> p kc n", p=128)
        dnv = w_down[l, fi].rearrange("(fc p) n -> p fc n", p=128)
        for kc in range(8):
            dma_pool(lambda e, kc=kc: e.dma_start(out=wup[:, kc, :], in_=upv[:, kc, :]), [], ["wup%d" % kc])
        for q in range(0, NF, 2):
            dma_pool(lambda e, q=q: e.dma_start(out=wdn[:, q:q + 2, :], in_=dnv[:, q:q + 2, :]), [], ["wdn%d" % q])
        wup_r = ["wup%d" % kc for kc in range(8)]
        go = (l * 3 + s) * 8

        def load_x(t):
            b = t % 2
            dma_sp(lambda e, t=t, b=b: e.dma_start(out=xt[b][:, :, :], in_=src_view[:, :, t * T:(t + 1) * T]), [], ["x%d" % b])

        def norm(t):
            b = t % 2
            emit_norm(xt[b], "x%d" % b, ht, "ht", fs, rstd, l, s, "f")

        def up(t):
            for f in range(NF):
                pa = ps[1 + f % 2]
                pb = ps[3 + f % 2]
                ra = "ps%d" % (1 + f % 2)
                rb = "ps%d" % (3 + f % 2)
                for kc in range(8):
                    pe(lambda e, f=f, kc=kc, pa=pa: e.matmul(pa[:, :], wup[:, kc, f * 128:(f + 1) * 128], ht[:, kc, :], start=(kc == 0), stop=(kc == 7)),
                       ["wup%d" % kc, "ht"], [ra])
                for kc in range(8):
                    pe(lambda e, f=f, kc=kc, pb=pb: e.matmul(pb[:, :], wup[:, kc, DFF + f * 128:DFF + (f + 1) * 128], ht[:, kc, :], start=(kc == 0), stop=(kc == 7)),
                       ["wup%d" % kc, "ht"], [rb])
                k = f % 2
                act(lambda e, pa=pa, k=k: e.activation(fs[k][:, :], pa[:, :], AF.Sigmoid), [ra], ["fs%d" % k])
                dve(lambda e, pa=pa, k=k: e.tensor_tensor(fs[k][:, :], fs[k][:, :], pa[:, :], ALU.mult), ["fs%d" % k, ra], ["fs%d" % k])
                dve(lambda e, pb=pb, k=k, f=f: e.tensor_tensor(gt[:, f, :], fs[k][:, :], pb[:, :], ALU.mult), ["fs%d" % k, rb], ["gt%d" % f])

        def down(t):
            b = t % 2
            for dc in range(8):
                py = ps[5 + dc % 2]
                ry = "ps%d" % (5 + dc % 2)
                for f in range(NF):
                    pe(lambda e, f=f, dc=dc, py=py: e.matmul(py[:, :], wdn[:, f, dc * 128:(dc + 1) * 128], gt[:, f, :], start=(f == 0), stop=(f == NF - 1)),
                       ["wdn%d" % (f // 2 * 2), "gt%d" % f], [ry])
                dve(lambda e, dc=dc, py=py, b=b: e.scalar_tensor_tensor(xt[b][:, dc, :], py[:, :], GCO[:, go + dc:go + dc + 1], xt[b][:, dc, :], ALU.mult, ALU.add),
                    [ry, "x%d" % b], ["x%d" % b])
            dma_sp(lambda e, t=t, b=b: e.dma_start(out=y_view[:, :, t * T:(t + 1) * T], in_=xt[b][:, :, :]), ["x%d" % b], [])

        load_x(0)
        norm(0)
        for t in range(NT):
            if t + 1 < NT:
                load_x(t + 1)
            up(t)
            if t + 1 < NT:
                norm(t + 1)
            down(t)
        P.barrier()

    def phase_m1(l):
        mem.reset()
        win = mem.alloc([128, 8, 8192], BF16)
        xt = mem.alloc([128, 8, T], F32)
        ht = [mem.alloc([128, 8, T], BF16) for _ in range(2)]
        stg = [mem.alloc([128, 2048], BF16) for _ in range(6)]
        fs = [mem.alloc([128, T], F32) for _ in range(3)]
        rstd = mem.alloc([128, T], F32)
        qdec = mem.alloc([128, 4, T], F32)
        wv = w_in[l].rearrange("(kc p) n -> p kc n", p=128)
        for kc in range(8):
            for hh in range(2):
                dma_pool(lambda e, kc=kc, hh=hh: e.dma_start(out=win[:, kc, hh * 4096:(hh + 1) * 4096], in_=wv[:, kc, hh * 4096:(hh + 1) * 4096]), [], ["win%d_%d" % (kc, hh)])
        dma_sp(lambda e: e.dma_start(out=qdec[:, :, :], in_=qdecd[:, :, :]), [], ["qdec"])
        st = {"bank": 0, "slot": 0, "ev": 0}

        def nbank():
            b = 1 + st["bank"] % 6
            st["bank"] += 1
            return ps[b], "ps%d" % b

        def nslot():
            k = st["slot"] % 6
            st["slot"] += 1
            return stg[k], "stg%d" % k

        def load_x(t):
            dma_sp(lambda e, t=t: e.dma_start(out=xt[:, :, :], in_=y_view[:, :, t * T:(t + 1) * T]), [], ["x"])

        def norm(t):
            emit_norm(xt, "x", ht[t % 2], "ht%d" % (t % 2), fs, rstd, l, 1, "m")

        def fm_group(t, col, evac):
            h_ = ht[t % 2]
            hr = "ht%d" % (t % 2)
            pb, rb = nbank()
            hh = col // 4096
            for kc in range(8):
                pe(lambda e, kc=kc, pb=pb, col=col, h_=h_: e.matmul(pb[:, :], win[:, kc, col:col + 128], h_[:, kc, :], start=(kc == 0), stop=(kc == 7)),
                   ["win%d_%d" % (kc, hh), hr], [rb])
            evac(pb, rb)

        def tm_group(t, j, col, evac):
            h_ = ht[t % 2]
            hr = "ht%d" % (t % 2)
            pb, rb = nbank()
            hh = col // 4096
            for kc in range(8):
                pe(lambda e, kc=kc, pb=pb, col=col, h_=h_, j=j: e.matmul(pb[:, :], h_[:, kc, j * 128:(j + 1) * 128], win[:, kc, col:col + 512], start=(kc == 0), stop=(kc == 7)),
                   ["win%d_%d" % (kc, hh), hr], [rb])
            evac(pb, rb)

        def proj_a(t):
            t0 = t * T
            sq_, rq_ = nslot()
            sqd, rqd_ = nslot()
            for h in range(4):
                def ev(pb, rb, h=h):
                    act(lambda e: e.activation(sq_[:, h * T:(h + 1) * T], pb[:, :], AF.Identity), [rb], [rq_ + "a"])
                    dve(lambda e: e.tensor_tensor(sqd[:, h * T:(h + 1) * T], pb[:, :], qdec[:, h, :], ALU.mult), [rb, "qdec"], [rqd_ + "d"])
                fm_group(t, h * 128, ev)
            if DBG2 >= 2:
                dma_sp(lambda e: e.dma_start(out=rqT[:, :, t0:t0 + T].rearrange("h d s -> d h s"), in_=sq_[:, :].rearrange("p (h s) -> p h s", h=4)), [rq_ + "a", rq_ + "d"], [])
                dma_sp(lambda e: e.dma_start(out=rqdT[:, :, t0:t0 + T].rearrange("h d s -> d h s"), in_=sqd[:, :].rearrange("p (h s) -> p h s", h=4)), [rqd_ + "a", rqd_ + "d"], [])
            if DBG2 <= 2:
                return None
            sk, rk_ = nslot()
            for h in range(4):
                def ev(pb, rb, h=h):
                    act(lambda e: e.activation(sk[:, h * T:(h + 1) * T], pb[:, :], AF.Identity, scale=float(128.0 ** -0.5)), [rb], [rk_ + "a"])
                fm_group(t, 512 + h * 128, ev)
            dma_sp(lambda e: e.dma_start(out=rkT[:, :, t0:t0 + T].rearrange("h d s -> d h s"), in_=sk[:, :].rearrange("p (h s) -> p h s", h=4)), [rk_ + "a", rk_ + "d"], [])

            def chunked(colbase, dst, mode):
                dview = dst.rearrange("(c p) s -> p c s", p=128)
                for c0 in range(0, 8, 4):
                    sl, rs = nslot()
                    for cc in range(4):
                        c = c0 + cc

                        def ev(pb, rb, cc=cc):
                            o = sl[:, cc * T:(cc + 1) * T]
                            if mode == "silu":
                                k = st["ev"] % 3
                                st["ev"] += 1
                                act(lambda e: e.activation(fs[k][:, :], pb[:, :], AF.Sigmoid), [rb], ["fs%d" % k])
                                dve(lambda e: e.tensor_tensor(o, fs[k][:, :], pb[:, :], ALU.mult), [rb, "fs%d" % k], [rs + "d"])
                            elif mode == "sig":
                                act(lambda e: e.activation(o, pb[:, :], AF.Sigmoid), [rb], [rs + "a"])
                            elif mode == "q":
                                dve(lambda e: e.tensor_scalar_mul(o, pb[:, :], 0.125), [rb], [rs + "d"])
                            else:
                                dve(lambda e: e.tensor_copy(o, pb[:, :]), [rb], [rs + "d"])
                        fm_group(t, colbase + c * 128, ev)
                    dma_sp(lambda e, sl=sl, c0=c0: e.dma_start(out=dview[:, c0:c0 + 4, t0:t0 + T], in_=sl[:, :].rearrange("p (c s) -> p c s", c=4)), [rs + "a", rs + "d"], [])
            return chunked

        def proj_a2(t, chunked):
            chunked(2048, rgT, "silu")
            chunked(3072, dqT.rearrange("h r s -> (h r) s"), "q")
            chunked(4096, dkT.rearrange("h r s -> (h r) s"), "k")
            chunked(6144, sgrT, "sig")
            chunked(7168, sgdT, "sig")

        def proj_b(t):
            kt0 = t * 4
            skd, rkd_ = nslot()
            for j in range(4):
                def ev(pb, rb, j=j):
                    for h in range(4):
                        dve(lambda e, h=h: e.tensor_scalar_mul(skd[:, j * 512 + h * 128:j * 512 + (h + 1) * 128], pb[:, h * 128:(h + 1) * 128], kdec[:, h * 4 + j:h * 4 + j + 1]),
                            [rb, "kdec"], [rkd_ + "d"])
                tm_group(t, j, 512, ev)
            dma_sp(lambda e: e.dma_start(out=rkd[:, kt0:kt0 + 4, :], in_=skd[:, :].rearrange("p (j n) -> p j n", j=4)), [rkd_ + "a", rkd_ + "d"], [])
            for colbase, dst in ((1024, rv), (5120, dvs)):
                for jp in range(2):
                    sl, rs = nslot()
                    for jj in range(2):
                        j = jp * 2 + jj
                        for half in range(2):
                            def ev(pb, rb, jj=jj, half=half, sl=sl, rs=rs):
                                o = sl[:, jj * 1024 + half * 512:jj * 1024 + (half + 1) * 512]
                                if half == 0:
                                    act(lambda e: e.activation(o, pb[:, :], AF.Identity), [rb], [rs + "a"])
                                else:
                                    dve(lambda e: e.tensor_copy(o, pb[:, :]), [rb], [rs + "d"])
                            tm_group(t, j, colbase + half * 512, ev)
                    dma_sp(lambda e, sl=sl, jp=jp, dst=dst: e.dma_start(out=dst[:, kt0 + jp * 2:kt0 + jp * 2 + 2, :], in_=sl[:, :].rearrange("p (j n) -> p j n", j=2)), [rs + "a", rs + "d"], [])

        load_x(0)
        norm(0)
        for t in range(NT):
            if t + 1 < NT:
                load_x(t + 1)
            if DBG >= 2:
                ch = proj_a(t)
            if DBG >= 3:
                proj_b(t)
            if t + 1 < NT:
                norm(t + 1)
            if DBG >= 4:
                proj_a2(t, ch)
        P.barrier()

    def phase_m2d(l):
        mem.reset()
        KT = [[mem.alloc([128, S], BF16) for _ in range(2)] for _ in range(2)]
        VV = [mem.alloc([128, NKT, 128], BF16) for _ in range(2)]
        QT = [mem.alloc([128, 2, T], BF16) for _ in range(2)]
        PT = [mem.alloc([128, T], BF16) for _ in range(4)]
        C0 = mem.alloc([128, 4, T], F32)
        sfix = [mem.alloc([128, T], F32) for _ in range(2)]
        EP = [[mem.alloc([128, T], F32) for _ in range(2)] for _ in range(7)]
        ydst = [mem.alloc([128, T], BF16) for _ in range(2)]
        dma_sp(lambda e: e.dma_start(out=C0[:, :, :], in_=c0d[:, :, :]), [], ["C0"])
        cnt = {"i": 0, "q": 0, "o": 0}
        slopes = [2.0 ** (-(h + 1)) for h in range(8)]

        def kres(b):
            return ["K%d_%d%s" % (b, s, x) for s in range(2) for x in "ra"]

        def qres(qb):
            return ["Q%d_%d%s" % (qb, s, x) for s in range(2) for x in "ra"]

        def load_head(h):
            b = h % 2
            for s in range(2):
                dma_sp(lambda e, s=s: e.dma_start(out=KT[b][s][0:64, :], in_=dkT[h, s * 64:(s + 1) * 64, :]), [], ["K%d_%dr" % (b, s)])
                dma_pool(lambda e, s=s: e.dma_start(out=KT[b][s][64:68, :], in_=kaug[h, :, :]), [], ["K%d_%da" % (b, s)])
            dma_sp(lambda e: e.dma_start(out=VV[b][:, :, :], in_=dvs[:, :, h * 128:(h + 1) * 128]), [], ["V%d" % b])

        def load_q(h, qi):
            qb = cnt["q"] % 2
            cnt["q"] += 1
            q0 = qi * T
            for s in range(2):
                dma_sp(lambda e, s=s: e.dma_start(out=QT[qb][0:64, s, :], in_=dqT[h, s * 64:(s + 1) * 64, q0:q0 + T]), [], ["Q%d_%dr" % (qb, s)])
                dma_pool(lambda e, s=s: e.dma_start(out=QT[qb][64:68, s, :], in_=qaug[:, q0:q0 + T]), [], ["Q%d_%da" % (qb, s)])
            return qb

        def do_tile(h, qi, b, qb):
            nk = 4 * qi + 4
            lim = qi * T - 127 - SKIP_T / slopes[h]
            kt_lo = max(0, int(math.floor(lim / 128.0)) + 1) if lim >= 0 else 0
            steps = [(kt, s) for kt in range(kt_lo, nk) for s in range(2)]

            def c0_of(kt):
                m_ = kt - 4 * qi
                return 128 * m_ if m_ > 0 else 0
            base = cnt["i"]
            kr = kres(b)
            qr = qres(qb)
            slope = float(slopes[h])

            def s_mm(i):
                kt, s = steps[i]
                g = base + i
                pb = ps[1 + g % 3]
                c0 = c0_of(kt)
                pe(lambda e: e.matmul(pb[:, c0:T], KT[b][s][0:68, kt * 128:(kt + 1) * 128], QT[qb][0:68, s, c0:T], start=True, stop=True),
                   kr + qr, ["ps%d" % (1 + g % 3)])

            def step(i):
                kt, s = steps[i]
                g = base + i
                pb = ps[1 + g % 3]
                rb = "ps%d" % (1 + g % 3)
                pt = PT[g % 4]
                rp = "PT%d" % (g % 4)
                m = kt - 4 * qi
                c0 = c0_of(kt)
                if m >= 0:
                    sf = sfix[g % 2]
                    rsf = "sfix%d" % (g % 2)
                    dve(lambda e: e.scalar_tensor_tensor(sf[:, c0:T], C0[:, m, c0:T], slope, pb[:, c0:T], ALU.mult, ALU.add), ["C0", rb], [rsf])
                    act(lambda e: e.activation(pt[:, c0:T], sf[:, c0:T], AF.Exp), [rsf], [rp])
                else:
                    act(lambda e: e.activation(pt[:, c0:T], pb[:, c0:T], AF.Exp), [rb], [rp])
                po = ps[4 + s]
                pz = ps[6 + s]
                pe(lambda e: e.matmul(po[:, c0:T], VV[b][:, kt, :], pt[:, c0:T], start=(kt == kt_lo), stop=(kt == nk - 1)),
                   ["V%d" % b, rp], ["ps%d" % (4 + s)])
                pe(lambda e: e.matmul(pz[:, c0:T], ones_b[:, :], pt[:, c0:T], start=(kt == kt_lo), stop=(kt == nk - 1)),
                   [rp], ["ps%d" % (6 + s)])

            s_mm(0)
            s_mm(1)
            for i in range(len(steps)):
                if i + 2 < len(steps):
                    s_mm(i + 2)
                step(i)
            cnt["i"] += len(steps)
            ob = cnt["o"] % 2
            cnt["o"] += 1
            R0_, R1_, oo_, t1_, sq_, ln_, rs_ = [x[ob] for x in EP]
            sfx = "_%d" % ob
            act(lambda e: e.activation(ln_[:, :], ps[6][:, :], AF.Ln), ["ps6"], ["lnv" + sfx])
            act(lambda e: e.activation(rs_[:, :], ps[7][:, :], AF.Ln), ["ps7"], ["rstd" + sfx])
            act(lambda e: e.activation(R0_[:, :], ln_[:, :], AF.Exp, scale=-1.0), ["lnv" + sfx], ["R0" + sfx])
            act(lambda e: e.activation(R1_[:, :], rs_[:, :], AF.Exp, scale=-1.0), ["rstd" + sfx], ["R1" + sfx])
            dve(lambda e: e.tensor_tensor(oo_[:, :], ps[4][:, :], R0_[:, :], ALU.mult), ["ps4", "R0" + sfx], ["oo" + sfx])
            dve(lambda e: e.tensor_tensor(t1_[:, :], ps[5][:, :], R1_[:, :], ALU.mult), ["ps5", "R1" + sfx], ["t1" + sfx])
            dve(lambda e: e.scalar_tensor_tensor(oo_[:, :], t1_[:, :], neglam[:, l:l + 1], oo_[:, :], ALU.mult, ALU.add), ["t1" + sfx, "oo" + sfx], ["oo" + sfx])
            pool(lambda e: e.tensor_tensor(sq_[:, :], oo_[:, :], oo_[:, :], ALU.mult), ["oo" + sfx], ["sqo" + sfx])
            pe(lambda e: e.matmul(ps[0][:, :], ones_f[:, :], sq_[:, :], start=True, stop=True), ["sqo" + sfx], ["ps0"])
            act(lambda e: e.activation(ln_[:, :], ps[0][:, :], AF.Ln, bias=epsc[:, 1:2], scale=1.0), ["ps0"], ["lnv" + sfx])
            act(lambda e: e.activation(rs_[:, :], ln_[:, :], AF.Exp, scale=-0.5), ["lnv" + sfx], ["rstd" + sfx])
            dve(lambda e: e.scalar_tensor_tensor(ydst[ob][:, :], oo_[:, :], cdiff[:, l:l + 1], rs_[:, :], ALU.mult, ALU.mult), ["oo" + sfx, "rstd" + sfx], ["yd%d" % ob])
            q0 = qi * T
            dma_pool(lambda e: e.dma_start(out=ydT[h * 128:(h + 1) * 128, q0:q0 + T], in_=ydst[ob][:, :]), ["yd%d" % ob], [])

        seq = [(h, qi) for h in range(8) for qi in range(NT)]
        load_head(0)
        qb_next = load_q(0, 0)
        for idx, (h, qi) in enumerate(seq):
            if qi == 0 and h + 1 < 8:
                load_head(h + 1)
            qb = qb_next
            if idx + 1 < len(seq):
                qb_next = load_q(*seq[idx + 1])
            do_tile(h, qi, h % 2, qb)
        P.barrier()

    def phase_m2r(l):
        mem.reset()
        NB = S // T
        Dt = mem.alloc([128, 16, T], F32)
        qb_ = [mem.alloc([128, 4, T], BF16) for _ in range(2)]
        qdb = [mem.alloc([128, 4, T], BF16) for _ in range(2)]
        kb = [mem.alloc([128, 4, T], BF16) for _ in range(2)]
        kdb = [mem.alloc([128, 4, 512], BF16) for _ in range(2)]
        vb = [mem.alloc([128, 4, 1024], BF16) for _ in range(2)]
        rgb = [mem.alloc([128, 8, T], BF16) for _ in range(2)]
        Pm = [mem.alloc([128, 4, T], BF16) for _ in range(2)]
        st32 = mem.alloc([128, 4, 256], F32)
        stb = mem.alloc([128, 4, 256], BF16)
        y32 = [mem.alloc([128, 2, T], F32) for _ in range(2)]
        sqv = mem.alloc([128, 2, T], F32)
        rstd = mem.alloc([128, T], F32)
        tmp = mem.alloc([128, T], F32)
        ost = [mem.alloc([128, 8, T], BF16) for _ in range(2)]
        dma_sp(lambda e: e.dma_start(out=Dt[:, :, :], in_=dtd.rearrange("p h m s -> p (h m) s")), [], ["Dt"])
        gam = [1.0 - 2.0 ** (-5.0 - h) for h in range(4)]
        g512 = [float(np.float32(np.exp(np.float64(T) * np.log(np.float64(np.float32(g)))))) for g in gam]
        yr_view = yrT.rearrange("(c p) s -> p c s", p=128)
        rg_view = rgT.rearrange("(c p) s -> p c s", p=128)

        def load_blk(bi):
            b = bi % 2
            t0 = bi * T
            dma_sp(lambda e: e.dma_start(out=qb_[b][:, :, :], in_=rqT[:, :, t0:t0 + T].rearrange("h d s -> d h s")), [], ["q%d" % b])
            dma_sp(lambda e: e.dma_start(out=qdb[b][:, :, :], in_=rqdT[:, :, t0:t0 + T].rearrange("h d s -> d h s")), [], ["qd%d" % b])
            dma_sp(lambda e: e.dma_start(out=kb[b][:, :, :], in_=rkT[:, :, t0:t0 + T].rearrange("h d s -> d h s")), [], ["k%d" % b])
            dma_sp(lambda e: e.dma_start(out=kdb[b][:, :, :], in_=rkd[:, bi * 4:bi * 4 + 4, :]), [], ["kd%d" % b])
            dma_sp(lambda e: e.dma_start(out=vb[b][:, :, :], in_=rv[:, bi * 4:bi * 4 + 4, :]), [], ["v%d" % b])
            dma_sp(lambda e: e.dma_start(out=rgb[b][:, :, :], in_=rg_view[:, :, t0:t0 + T]), [], ["rg%d" % b])

        def do_head(bi, b, h, gi):
            pmb = Pm[gi % 2]
            rpm = "Pm%d" % (gi % 2)
            yb = y32[gi % 2]
            ryb = "y32_%d" % (gi % 2)
            osb = ost[b]
            ros = "ost%d" % b

            def smm(m):
                pb = ps[1 + m % 2]
                rb = "ps%d" % (1 + m % 2)
                pe(lambda e: e.matmul(pb[:, :], kb[b][:, h, m * 128:(m + 1) * 128], qb_[b][:, h, :], start=True, stop=True), ["k%d" % b, "q%d" % b], [rb])
                dve(lambda e: e.tensor_tensor(pmb[:, m, :], pb[:, :], Dt[:, h * 4 + m, :], ALU.mult), [rb, "Dt"], [rpm + "_%d" % m])
            for m in range(4):
                smm(m)

            def ymm(vc):
                py = ps[3 + vc]
                ry = "ps%d" % (3 + vc)
                for m in range(4):
                    pe(lambda e, m=m: e.matmul(py[:, :], vb[b][:, m, h * 256 + vc * 128:h * 256 + (vc + 1) * 128], pmb[:, m, :], start=(m == 0), stop=(m == 3 and bi == 0)),
                       ["v%d" % b, rpm + "_%d" % m], [ry])
                if bi > 0:
                    pe(lambda e: e.matmul(py[:, :], stb[:, h, vc * 128:(vc + 1) * 128], qdb[b][:, h, :], start=False, stop=True), ["stb%d" % h, "qd%d" % b], [ry])
            for vc in range(2):
                ymm(vc)
            for m in range(4):
                pe(lambda e, m=m: e.matmul(ps[5][:, 0:256], kdb[b][:, m, h * 128:(h + 1) * 128], vb[b][:, m, h * 256:(h + 1) * 256], start=(m == 0), stop=(m == 3)),
                   ["kd%d" % b, "v%d" % b], ["ps5"])
            if bi == 0:
                dve(lambda e: e.tensor_copy(st32[:, h, :], ps[5][:, 0:256]), ["ps5"], ["st32_%d" % h])
            else:
                dve(lambda e: e.scalar_tensor_tensor(st32[:, h, :], st32[:, h, :], float(g512[h]), ps[5][:, 0:256], ALU.mult, ALU.add), ["ps5", "st32_%d" % h], ["st32_%d" % h])
            if bi + 1 < NB:
                pool(lambda e: e.tensor_copy(stb[:, h, :], st32[:, h, :]), ["st32_%d" % h], ["stb%d" % h])

            def gn(vc):
                py = ps[3 + vc]
                ry = "ps%d" % (3 + vc)
                act(lambda e: e.activation(sqv[:, vc, :], py[:, :], AF.Square), [ry], ["sqv%d" % vc])
                act(lambda e: e.activation(yb[:, vc, :], py[:, :], AF.Identity), [ry], [ryb + "_%d" % vc])
                pe(lambda e: e.matmul(ps[0][:, :], ones_f[:, :], sqv[:, vc, :], start=(vc == 0), stop=(vc == 1)), ["sqv%d" % vc], ["ps0"])
            for vc in range(2):
                gn(vc)
            act(lambda e: e.activation(rstd[:, :], ps[0][:, :], AF.Ln, bias=epsc[:, 2:3], scale=1.0), ["ps0"], ["rstd"])
            act(lambda e: e.activation(rstd[:, :], rstd[:, :], AF.Exp, scale=-0.5), ["rstd"], ["rstd"])

            def fin(vc):
                c = h * 2 + vc
                dve(lambda e: e.scalar_tensor_tensor(tmp[:, :], yb[:, vc, :], cret[:, l * 8 + c:l * 8 + c + 1], rstd[:, :], ALU.mult, ALU.mult), [ryb + "_%d" % vc, "rstd"], ["tmp"])
                pool(lambda e: e.tensor_tensor(osb[:, c, :], tmp[:, :], rgb[b][:, c, :], ALU.mult), ["tmp", "rg%d" % b], [ros])
            for vc in range(2):
                fin(vc)

        def store_blk(bi, b):
            t0 = bi * T
            dma_pool(lambda e: e.dma_start(out=yr_view[:, :, t0:t0 + T], in_=ost[b][:, :, :]), ["ost%d" % b], [])

        load_blk(0)
        gi = 0
        for bi in range(NB):
            if bi + 1 < NB:
                load_blk(bi + 1)
            for h in range(4):
                do_head(bi, bi % 2, h, gi)
                gi += 1
            store_blk(bi, bi % 2)
        P.barrier()

    def phase_m3(l):
        mem.reset()
        wr = mem.alloc([128, 8, D], BF16); wd = mem.alloc([128, 8, D], BF16); wo = mem.alloc([128, 8, D], BF16)
        xt = [mem.alloc([128, 8, T], F32) for _ in range(2)]
        yr = [mem.alloc([128, 8, T], BF16) for _ in range(2)]
        yd = [mem.alloc([128, 8, T], BF16) for _ in range(2)]
        sr = [mem.alloc([128, 8, T], BF16) for _ in range(2)]
        sd = [mem.alloc([128, 8, T], BF16) for _ in range(2)]
        mg = mem.alloc([128, 8, T], BF16)
        ta = [mem.alloc([128, T], F32) for _ in range(2)]
        tb = [mem.alloc([128, T], F32) for _ in range(2)]

        def loadw(w_s, w_d, nm):
            v = w_d[l].rearrange("(kc p) n -> p kc n", p=128)
            for hh in range(2):
                dma_pool(lambda e, hh=hh: e.dma_start(out=w_s[:, hh * 4:(hh + 1) * 4, :], in_=v[:, hh * 4:(hh + 1) * 4, :]), [], ["%s%d" % (nm, hh)])
        loadw(wr, w_rb, "wr"); loadw(wd, w_db, "wd"); loadw(wo, w_out, "wo")
        views = [t_.rearrange("(c p) s -> p c s", p=128) for t_ in (yrT, ydT, sgrT, sgdT)]
        go = (l * 3 + 1) * 8

        def load(t):
            b = t % 2
            t0 = t * T
            dma_sp(lambda e: e.dma_start(out=xt[b][:, :, :], in_=y_view[:, :, t0:t0 + T]), [], ["x%d" % b])
            for buf, v, nm in ((yr, views[0], "yr"), (yd, views[1], "yd"), (sr, views[2], "sr"), (sd, views[3], "sd")):
                dma_sp(lambda e, buf=buf, v=v: e.dma_start(out=buf[b][:, :, :], in_=v[:, :, t0:t0 + T]), [], ["%s%d" % (nm, b)])

        def do_tile(t, b):
            def branch(dc):
                pr = ps[1 + dc % 2]; rr = "ps%d" % (1 + dc % 2)
                pd = ps[3 + dc % 2]; rd = "ps%d" % (3 + dc % 2)
                for kc in range(8):
                    pe(lambda e, kc=kc: e.matmul(pr[:, :], wr[:, kc, dc * 128:(dc + 1) * 128], yr[b][:, kc, :], start=(kc == 0), stop=(kc == 7)), ["wr%d" % (kc // 4), "yr%d" % b], [rr])
                for kc in range(8):
                    pe(lambda e, kc=kc: e.matmul(pd[:, :], wd[:, kc, dc * 128:(dc + 1) * 128], yd[b][:, kc, :], start=(kc == 0), stop=(kc == 7)), ["wd%d" % (kc // 4), "yd%d" % b], [rd])
                k = dc % 2
                dve(lambda e: e.tensor_tensor(ta[k][:, :], pr[:, :], sr[b][:, dc, :], ALU.mult), [rr, "sr%d" % b], ["ta%d" % k])
                dve(lambda e: e.tensor_tensor(tb[k][:, :], pd[:, :], sd[b][:, dc, :], ALU.mult), [rd, "sd%d" % b], ["tb%d" % k])
                pool(lambda e: e.tensor_tensor(mg[:, dc, :], ta[k][:, :], tb[k][:, :], ALU.add), ["ta%d" % k, "tb%d" % k], ["mg%d" % dc])
            for dc in range(8):
                branch(dc)

            def outp(dc):
                po = ps[5 + dc % 2]; ro = "ps%d" % (5 + dc % 2)
                for kc in range(8):
                    pe(lambda e, kc=kc: e.matmul(po[:, :], wo[:, kc, dc * 128:(dc + 1) * 128], mg[:, kc, :], start=(kc == 0), stop=(kc == 7)), ["wo%d" % (kc // 4), "mg%d" % kc], [ro])
                dve(lambda e: e.scalar_tensor_tensor(xt[b][:, dc, :], po[:, :], GCO[:, go + dc:go + dc + 1], xt[b][:, dc, :], ALU.mult, ALU.add), [ro, "x%d" % b], ["x%d" % b])
            for dc in range(8):
                outp(dc)
            t0 = t * T
            dma_sp(lambda e: e.dma_start(out=y_view[:, :, t0:t0 + T], in_=xt[b][:, :, :]), ["x%d" % b], [])

        load(0)
        for t in range(NT):
            if t + 1 < NT:
                load(t + 1)
            do_tile(t, t % 2)
        P.barrier()

    def phase_fin():
        mem.reset()
        xt = [mem.alloc([128, 8, T], F32) for _ in range(2)]
        fs = [mem.alloc([128, T], F32) for _ in range(3)]
        rstd = mem.alloc([128, T], F32)

        def load(t):
            b = t % 2
            dma_sp(lambda e: e.dma_start(out=xt[b][:, :, :], in_=y_view[:, :, t * T:(t + 1) * T]), [], ["x%d" % b])

        def do_tile(t, b):
            def st(c):
                k = c % 3
                act(lambda e: e.activation(fs[k][:, :], xt[b][:, c, :], AF.Square), ["x%d" % b], ["fs%d" % k])
                pe(lambda e: e.matmul(ps[0][:, :], ones_f[:, :], fs[k][:, :], start=(c == 0), stop=(c == 7)), ["fs%d" % k], ["ps0"])
            for c in range(8):
                st(c)
            act(lambda e: e.activation(rstd[:, :], ps[0][:, :], AF.Ln, bias=epsc[:, 0:1], scale=1.0), ["ps0"], ["rstd"])
            act(lambda e: e.activation(rstd[:, :], rstd[:, :], AF.Exp, scale=-0.5), ["rstd"], ["rstd"])

            def sc(c):
                dve(lambda e: e.scalar_tensor_tensor(xt[b][:, c, :], xt[b][:, c, :], fnc[:, c:c + 1], rstd[:, :], ALU.mult, ALU.mult), ["x%d" % b, "rstd"], ["x%d" % b])
            for c in range(8):
                sc(c)
            dma_sp(lambda e: e.dma_start(out=y_view[:, :, t * T:(t + 1) * T], in_=xt[b][:, :, :]), ["x%d" % b], [])

        load(0)
        for t in range(NT):
            if t + 1 < NT:
                load(t + 1)
            do_tile(t, t % 2)
        P.barrier()

    plist = [phase_pre]
    for l in range(depth):
        plist.append(lambda l=l: phase_ffn(l, 0, 0, x_view if l == 0 else y_view))
        plist.append(lambda l=l: phase_m1(l))
        plist.append(lambda l=l: phase_m2d(l))
        plist.append(lambda l=l: phase_m2r(l))
        plist.append(lambda l=l: phase_m3(l))
        plist.append(lambda l=l: phase_ffn(l, 1, 2, y_view))
    plist.append(phase_fin)
    for pf in plist[:upto]:
        pf()
    P.finalize()

    from contextlib import ExitStack
    sems = {}
    with ExitStack() as es:
        for k in ("pe", "act", "dve", "pool", "bar"):
            sems[k] = es.enter_context(nc.semaphore("s_" + k))
        for q in ("sp", "pool"):
            for i in range(KDMA):
                sems["%s_d%d" % (q, i)] = es.enter_context(nc.semaphore("s_%s_d%d" % (q, i)))
        with nc.Block() as block:
            @block.tensor
            def _(e):
                P.run_stream("pe", e, sems)

            @block.scalar
            def _(e):
                P.run_stream("act", e, sems)

            @block.vector
            def _(e):
                P.run_stream("dve", e, sems)

            @block.gpsimd
            def _(e):
                P.run_stream("pool", e, sems)

            @block.sync
            def _(e):
                P.run_stream("sp", e, sems)
    return nc


def make_consts(S):
    pos = np.arange(S)
    a = (pos // 128).astype(np.float64)
    b_ = (pos % 128).astype(np.float64)
    qaug = np.stack([-128.0 * a, -b_, np.ones(S), np.ones(S)]).astype(np.float32)
    kaug = np.zeros((8, 4, S), np.float32)
    for h in range(8):
        sl = 2.0 ** (-(h + 1))
        kaug[h, 0] = sl
        kaug[h, 1] = sl
        kaug[h, 2] = sl * 128.0 * a
        kaug[h, 3] = sl * b_
    i = np.arange(128)[:, None, None]
    m = np.arange(4)[None, :, None]
    j = np.arange(512)[None, None, :]
    kp = 128 * m + i
    allowed = (kp // 64) <= (j // 64)
    c0 = np.where(allowed, np.where(kp > j, -2.0 * (kp - j), 0.0), -1.0e6).astype(np.float32)
    dtd = np.zeros((128, 4, 4, 512), np.float32)
    qdec = np.zeros((128, 4, 512), np.float32)
    kdec = np.zeros((128, 16), np.float32)
    for h in range(4):
        g = np.float64(np.float32(1.0 - 2.0 ** (-5.0 - h)))
        lg = np.log(g)
        dd = np.where(allowed, np.exp(lg * np.abs(j - kp)), 0.0)
        dtd[:, h, :, :] = dd.astype(np.float32)
        qdec[:, h, :] = np.exp(lg * np.arange(512))[None, :].astype(np.float32)
        for jj in range(4):
            r = 128 * jj + np.arange(128)
            kdec[:, h * 4 + jj] = (np.exp(lg * (512 - r)) * (128.0 ** -0.5)).astype(np.float32)
    return dict(qaug=qaug, kaug=kaug, c0d=c0, dtd=dtd, qdecd=qdec, kdecd=kdec)


_CACHE = {}


def run(inputs, S, depth, n_cores, upto=99):
    key = (S, depth, upto)
    if key not in _CACHE:
        _CACHE[key] = build_program(S, depth, upto)
    nc = _CACHE[key]
    f = lambda a: np.ascontiguousarray(np.asarray(a, dtype=np.float32))
    x = f(inputs["x"]); c = f(inputs["c"])
    shared = dict(
        w_ada=f(inputs["w_ada"]),
        b_adaT=f(f(inputs["b_ada"]).reshape(depth, 72, 128).transpose(2, 0, 1).reshape(128, depth * 72)),
        norm_wT=f(f(inputs["norm_w"]).reshape(depth, 3, 8, 128).transpose(3, 0, 1, 2).reshape(128, depth * 24)),
        w_up=f(inputs["w_ffn_up"]), w_down=f(inputs["w_ffn_down"]), w_in=f(inputs["w_in"]),
        ret_gnT=f(f(inputs["ret_gn"]).reshape(depth, 8, 128).transpose(2, 0, 1).reshape(128, depth * 8)),
        lam_in=f(np.broadcast_to(np.stack([f(inputs["lambda_q1"]), f(inputs["lambda_k1"]), f(inputs["lambda_q2"]), f(inputs["lambda_k2"])]).reshape(1, -1), (128, 4 * depth * 64))),
        sublnT=f(f(inputs["diff_subln"]).T),
        w_rb=f(inputs["w_ret_branch"]), w_db=f(inputs["w_diff_branch"]), w_out=f(inputs["w_out"]),
        fnT=f(f(inputs["final_norm"]).reshape(8, 128).T),
    )
    shared.update(make_consts(S))
    in_maps = []
    for b in range(n_cores):
        m = dict(shared)
        m["xT"] = f(x[b].T)
        m["cT"] = f(c[b].reshape(8, 128).T)
        in_maps.append(m)
    res = run_bass_kernel_spmd(nc, in_maps, core_ids=list(range(n_cores)))
    out = np.stack([np.ascontiguousarray(np.asarray(r["yT"]).T) for r in res.results])
    return out.astype(np.float32)


def kernel(**inputs):
    return run(inputs, SEQ, NLAYERS, 8)
```

```python
import math
import numpy as np
import concourse.bass as bass
import concourse.mybir as mybir
from concourse.bass_utils import run_bass_kernel_spmd

F32 = mybir.dt.float32
BF16 = mybir.dt.bfloat16
AF = mybir.ActivationFunctionType
ALU = mybir.AluOpType
AX = mybir.AxisListType

D = 1024
KC = 8
T = 512
DFF = 2816
NF = 22
EPS = 1e-6
KDMA = 8
SKIP_T = 128.0
NLAYERS = 4
import os
DBG = int(os.environ.get('KDBG', '9'))
DBG2 = int(os.environ.get('KDBG2', '9'))
SEQ = 8192


class Op:
    __slots__ = ("eng", "fn", "deps", "dma", "sig", "cnt", "semkey", "qn", "bar")


class Prog:
    def __init__(self):
        self.streams = {e: [] for e in ("pe", "act", "dve", "pool", "sp")}
        self.lastw = {}
        self.readers = {}
        self.nbar = 0

    def add(self, eng, fn, reads=(), writes=(), dma=False):
        op = Op()
        op.eng = eng; op.fn = fn; op.dma = dma; op.sig = False; op.bar = 0
        op.cnt = 0; op.semkey = None; op.qn = 0
        deps = {}
        for r in reads:
            w = self.lastw.get(r)
            if w is not None:
                deps[id(w)] = (w, 0)
            if r.startswith("ps"):
                rd = self.readers.get(r)
                if rd:
                    for k, o in rd.items():
                        if k != "dma" and k != eng and id(o) not in deps:
                            deps[id(o)] = (o, 3)
        for r in writes:
            w = self.lastw.get(r)
            if w is not None and id(w) not in deps:
                deps[id(w)] = (w, 1)
            rd = self.readers.get(r)
            if rd:
                for k, o in rd.items():
                    if k == "dma":
                        for oo in o:
                            if id(oo) not in deps:
                                deps[id(oo)] = (oo, 2)
                    elif id(o) not in deps:
                        deps[id(o)] = (o, 2)
        dl = []
        for w, kind in deps.values():
            if (not w.dma) and (not dma) and w.eng == eng:
                if eng == "pe":
                    continue
            dl.append(w)
            w.sig = True
        op.deps = dl
        for r in writes:
            self.lastw[r] = op
            self.readers[r] = {}
        for r in reads:
            rd = self.readers.setdefault(r, {})
            if dma:
                rd.setdefault("dma", []).append(op)
            else:
                rd[eng] = op
        self.streams[eng].append(op)
        return op

    def barrier(self):
        self.nbar += 1
        for e, st in self.streams.items():
            for o in reversed(st):
                if o.bar:
                    break
                if not o.dma:
                    o.sig = True
                    break
            op = Op()
            op.eng = e; op.fn = None; op.dma = False; op.sig = False; op.bar = self.nbar
            op.deps = []; op.cnt = 0; op.semkey = None; op.qn = 0
            st.append(op)
        self.lastw = {}
        self.readers = {}

    def finalize(self):
        for e, st in self.streams.items():
            c = 0
            q = 0
            for op in st:
                if op.bar:
                    continue
                if op.dma:
                    op.semkey = "%s_d%d" % (e, q % KDMA)
                    op.cnt = 16 * (q // KDMA + 1)
                    op.qn = q
                    q += 1
                elif op.sig:
                    c += 1
                    op.cnt = c
                    op.semkey = e

    def run_stream(self, ename, eng, sems):
        waited = {}

        def wait(key, val):
            if waited.get(key, 0) < val:
                eng.wait_ge(sems[key], val)
                waited[key] = val

        own = 0
        dtot = {}
        for op in self.streams[ename]:
            if op.bar:
                if ename != "sp" and own > 0:
                    wait(ename, own)
                for k, v in dtot.items():
                    wait(k, v)
                eng.sem_inc(sems["bar"], 1)
                wait("bar", 5 * op.bar)
                continue
            for d in op.deps:
                wait(d.semkey, d.cnt)
            if op.dma and op.qn >= KDMA:
                wait(op.semkey, op.cnt - 16)
            ins = op.fn(eng)
            if op.dma:
                ins.then_inc(sems[op.semkey], 16)
                dtot[op.semkey] = op.cnt
            elif op.sig:
                ins.then_inc(sems[ename], 1)
                own = op.cnt


class Mem:
    def __init__(self, nc):
        self.nc = nc
        self.off = 16576
        self.n = 0
        self.base = 16576

    def alloc(self, shape, dtype):
        nb = 1
        for s in shape[1:]:
            nb *= s
        nb *= 4 if dtype == F32 else 2
        nb = (nb + 63) // 64 * 64
        h = self.nc.alloc_sbuf_tensor_at("sb%d" % self.n, list(shape), dtype, offset=self.off)
        self.n += 1
        self.off += nb
        assert self.off <= 229376, self.off
        return h

    def set_base(self):
        self.base = self.off

    def reset(self):
        self.off = self.base


def lam_init_of(l):
    return 0.8 - 0.6 * math.exp(-0.3 * l)


def build_program(S, depth, upto=99):
    NT = S // T
    NKT = S // 128
    nc = bass.Bass("TRN2", target_bir_lowering=False)
    P = Prog()
    mem = Mem(nc)

    def din(name, shape, dt=F32):
        return nc.dram_tensor(name, list(shape), dt, kind="ExternalInput")

    xT = din("xT", [D, S])
    cT = din("cT", [128, 8])
    w_ada = din("w_ada", [depth, D, 9216])
    b_adaT = din("b_adaT", [128, depth * 72])
    norm_wT = din("norm_wT", [128, depth * 24])
    w_up = din("w_up", [depth, 2, D, 2 * DFF])
    w_down = din("w_down", [depth, 2, DFF, D])
    w_in = din("w_in", [depth, D, 8192])
    ret_gnT = din("ret_gnT", [128, depth * 8])
    lam_in = din("lam_in", [128, 4 * depth * 64])
    sublnT = din("sublnT", [128, depth])
    w_rb = din("w_rb", [depth, D, D])
    w_db = din("w_db", [depth, D, D])
    w_out = din("w_out", [depth, D, D])
    fnT = din("fnT", [128, 8])
    qaug = din("qaug", [4, S])
    kaug = din("kaug", [8, 4, S])
    c0d = din("c0d", [128, 4, 512])
    dtd = din("dtd", [128, 4, 4, 512])
    qdecd = din("qdecd", [128, 4, 512])
    kdecd = din("kdecd", [128, 16])
    yT = nc.dram_tensor("yT", [D, S], F32, kind="ExternalOutput")

    def dscr(name, shape):
        return nc.dram_tensor(name, list(shape), BF16, kind="Internal")

    rqT = dscr("rqT", [4, 128, S]); rqdT = dscr("rqdT", [4, 128, S]); rkT = dscr("rkT", [4, 128, S])
    rkd = dscr("rkd", [128, NKT, 512]); rv = dscr("rv", [128, NKT, 1024])
    rgT = dscr("rgT", [D, S]); dqT = dscr("dqT", [8, 128, S]); dkT = dscr("dkT", [8, 128, S])
    dvs = dscr("dvs", [128, NKT, 1024]); sgrT = dscr("sgrT", [D, S]); sgdT = dscr("sgdT", [D, S])
    yrT = dscr("yrT", [D, S]); ydT = dscr("ydT", [D, S])

    ps = [nc.alloc_psum_tensor("ps%d" % i, [128, 512], F32) for i in range(8)]

    MOD = mem.alloc([128, depth * 72], F32)
    ACO = mem.alloc([128, depth * 24], F32)
    GCO = mem.alloc([128, depth * 24], F32)
    ones_b = mem.alloc([128, 128], BF16)
    ones_f = mem.alloc([128, 128], F32)
    neglam = mem.alloc([128, depth], F32)
    cdiff = mem.alloc([128, depth], F32)
    cret = mem.alloc([128, depth * 8], F32)
    fnc = mem.alloc([128, 8], F32)
    kdec = mem.alloc([128, 16], F32)
    epsc = mem.alloc([128, 4], F32)
    mem.set_base()

    x_view = xT.rearrange("(kc p) s -> p kc s", p=128)
    y_view = yT.rearrange("(kc p) s -> p kc s", p=128)

    def act(fn, r, w):
        return P.add("act", fn, r, w)

    def dve(fn, r, w):
        return P.add("dve", fn, r, w)

    def pool(fn, r, w):
        return P.add("pool", fn, r, w)

    def pe(fn, r, w):
        return P.add("pe", fn, r, w)

    def dma_sp(fn, r, w):
        return P.add("sp", fn, r, w, dma=True)

    def dma_pool(fn, r, w):
        return P.add("pool", fn, r, w, dma=True)

    def phase_pre():
        mem.reset()
        cnd = mem.alloc([128, 8], F32)
        cnd2 = mem.alloc([128, 8], F32)
        bad = mem.alloc([128, depth * 72], F32)
        nwt = mem.alloc([128, depth * 24], F32)
        rgn = mem.alloc([128, depth * 8], F32)
        sbl = mem.alloc([128, depth], F32)
        fnt = mem.alloc([128, 8], F32)
        lmi = mem.alloc([128, 4 * depth * 64], F32)
        lpr = mem.alloc([128, 2 * depth * 64], F32)
        lsum = mem.alloc([128, 2 * depth], F32)
        lexp = mem.alloc([128, 2 * depth], F32)
        ldif = mem.alloc([128, depth], F32)
        wa = [mem.alloc([128, 8, 1152], F32) for _ in range(2)]

        dma_sp(lambda e: e.dma_start(out=cnd[:, :], in_=cT[:, :]), [], ["cnd"])
        dma_sp(lambda e: e.dma_start(out=bad[:, :], in_=b_adaT[:, :]), [], ["bad"])
        dma_sp(lambda e: e.dma_start(out=nwt[:, :], in_=norm_wT[:, :]), [], ["nwt"])
        dma_sp(lambda e: e.dma_start(out=rgn[:, :], in_=ret_gnT[:, :]), [], ["rgn"])
        dma_sp(lambda e: e.dma_start(out=sbl[:, :], in_=sublnT[:, :]), [], ["sbl"])
        dma_sp(lambda e: e.dma_start(out=fnt[:, :], in_=fnT[:, :]), [], ["fnt"])
        dma_sp(lambda e: e.dma_start(out=lmi[:, :], in_=lam_in[:, :]), [], ["lmi"])
        dma_sp(lambda e: e.dma_start(out=kdec[:, :], in_=kdecd[:, :]), [], ["kdec"])
        dve(lambda e: e.memset(ones_b[:, :], 1.0), [], ["ones_b"])
        dve(lambda e: e.memset(ones_f[:, :], 1.0), [], ["ones_f"])
        dve(lambda e: e.memset(epsc[:, 0:1], float(D * EPS)), [], ["epsc"])
        dve(lambda e: e.memset(epsc[:, 1:2], float(128 * EPS)), [], ["epsc"])
        dve(lambda e: e.memset(epsc[:, 2:3], float(256 * EPS)), [], ["epsc"])
        act(lambda e: e.activation(cnd2[:, :], cnd[:, :], AF.Sigmoid), ["cnd"], ["cnd2"])
        dve(lambda e: e.tensor_tensor(cnd2[:, :], cnd2[:, :], cnd[:, :], ALU.mult), ["cnd", "cnd2"], ["cnd2"])
        LD = depth * 64
        dve(lambda e: e.tensor_tensor(lpr[:, 0:LD], lmi[:, 0:LD], lmi[:, LD:2 * LD], ALU.mult), ["lmi"], ["lpr0"])
        dve(lambda e: e.tensor_tensor(lpr[:, LD:2 * LD], lmi[:, 2 * LD:3 * LD], lmi[:, 3 * LD:4 * LD], ALU.mult), ["lmi"], ["lpr1"])
        dve(lambda e: e.reduce_sum(lsum[:, :], lpr[:, :].rearrange("p (a d) -> p a d", d=64), AX.X), ["lpr0", "lpr1"], ["lsum"])
        act(lambda e: e.activation(lexp[:, :], lsum[:, :], AF.Exp), ["lsum"], ["lexp"])
        dve(lambda e: e.tensor_tensor(ldif[:, :], lexp[:, depth:2 * depth], lexp[:, 0:depth], ALU.subtract), ["lexp"], ["ldif"])
        for l in range(depth):
            li = lam_init_of(l)
            dve(lambda e, l=l, li=li: e.tensor_scalar_add(neglam[:, l:l + 1], ldif[:, l:l + 1], -li), ["ldif"], ["neglam"])
            dve(lambda e, l=l, li=li: e.tensor_scalar_mul(cdiff[:, l:l + 1], sbl[:, l:l + 1], (1.0 - li) * math.sqrt(128.0)), ["sbl"], ["cdiff"])
        dve(lambda e: e.tensor_scalar_mul(cret[:, :], rgn[:, :], 16.0), ["rgn"], ["cret"])
        dve(lambda e: e.tensor_scalar_mul(fnc[:, :], fnt[:, :], 32.0), ["fnt"], ["fnc"])
        nb = 0
        for l in range(depth):
            wv = w_ada[l].rearrange("(kc p) n -> p kc n", p=128)
            for j in range(8):
                buf = wa[nb % 2]
                rn = "wa%d" % (nb % 2)
                nb += 1
                for kc in range(8):
                    dma_sp(lambda e, buf=buf, kc=kc, j=j, wv=wv: e.dma_start(out=buf[:, kc, :], in_=wv[:, kc, j * 1152:(j + 1) * 1152]), [], [rn + "_%d" % kc])
                for m in range(9):
                    col = l * 72 + j * 9 + m
                    for kc in range(8):
                        pe(lambda e, buf=buf, kc=kc, m=m, col=col: e.matmul(ps[0][:, col:col + 1], buf[:, kc, m * 128:(m + 1) * 128], cnd2[:, kc:kc + 1], start=(kc == 0), stop=(kc == 7)),
                           [rn + "_%d" % kc, "cnd2"], ["psM"])
        dve(lambda e: e.tensor_tensor(MOD[:, :], ps[0][:, 0:depth * 72], bad[:, :], ALU.add), ["psM", "bad"], ["MOD"])
        for l in range(depth):
            for s in range(3):
                o = (l * 3 + s) * 8
                sc = l * 72 + s * 24 + 8
                gc = l * 72 + s * 24 + 16
                dve(lambda e, o=o, sc=sc: e.scalar_tensor_tensor(ACO[:, o:o + 8], MOD[:, sc:sc + 8], 1.0, nwt[:, o:o + 8], ALU.add, ALU.mult), ["MOD", "nwt"], ["ACO%d" % o])
                dve(lambda e, o=o: e.tensor_scalar_mul(ACO[:, o:o + 8], ACO[:, o:o + 8], 32.0), ["ACO%d" % o], ["ACO%d" % o])
                dve(lambda e, o=o, gc=gc, s=s: e.tensor_scalar_mul(GCO[:, o:o + 8], MOD[:, gc:gc + 8], 1.0 if s == 1 else 0.5), ["MOD"], ["GCO"])
        P.barrier()

    def emit_norm(xt, xr, ht, hr, fs, rstd, l, s, tag):
        for c in range(8):
            k = c % 3
            act(lambda e, c=c, k=k: e.activation(fs[k][:, :], xt[:, c, :], AF.Square), [xr], ["fs%d" % k])
            pe(lambda e, c=c, k=k: e.matmul(ps[0][:, :], ones_f[:, :], fs[k][:, :], start=(c == 0), stop=(c == 7)), ["fs%d" % k], ["ps0"])
        act(lambda e: e.activation(rstd[:, :], ps[0][:, :], AF.Ln, bias=epsc[:, 0:1], scale=1.0), ["ps0"], ["rstd"])
        act(lambda e: e.activation(rstd[:, :], rstd[:, :], AF.Exp, scale=-0.5), ["rstd"], ["rstd"])
        o = (l * 3 + s) * 8
        sh = l * 72 + s * 24
        for c in range(8):
            k = c % 3
            dve(lambda e, c=c, k=k: e.tensor_tensor(fs[k][:, :], xt[:, c, :], rstd[:, :], ALU.mult), [xr, "rstd"], ["fs%d" % k])
            act(lambda e, c=c, k=k: e.activation(ht[:, c, :], fs[k][:, :], AF.Identity, bias=MOD[:, sh + c:sh + c + 1], scale=ACO[:, o + c:o + c + 1]),
                ["fs%d" % k], [hr])

    def phase_ffn(l, fi, s, src_view):
        mem.reset()
        wup = mem.alloc([128, 8, 2 * DFF], BF16)
        wdn = mem.alloc([128, NF, D], BF16)
        xt = [mem.alloc([128, 8, T], F32) for _ in range(2)]
        ht = mem.alloc([128, 8, T], BF16)
        gt = mem.alloc([128, NF, T], BF16)
        fs = [mem.alloc([128, T], F32) for _ in range(3)]
        rstd = mem.alloc([128, T], F32)
        upv = w_up[l, fi].rearrange("(kc p) n -> p kc n", p=128)
        dnv = w_down[l, fi].rearrange("(fc p) n -> p fc n", p=128)
        for j in range(0, NF, 4):
            w_ = min(4, NF - j) * 128
            for part in range(2):
                c_ = part * DFF + j * 128
                dma_pool(lambda e, c_=c_, w_=w_: e.dma_start(out=wup[:, :, c_:c_ + w_], in_=upv[:, :, c_:c_ + w_]), [], ["wup%d_%d" % (part, j // 4)])
        for q in range(0, NF, 2):
            dma_pool(lambda e, q=q: e.dma_start(out=wdn[:, q:q + 2, :], in_=dnv[:, q:q + 2, :]), [], ["wdn%d" % q])
        wup_r = ["wup%d" % kc for kc in range(8)]
        go = (l * 3 + s) * 8

        def load_x(t):
            b = t % 2
            dma_sp(lambda e, t=t, b=b: e.dma_start(out=xt[b][:, :, :], in_=src_view[:, :, t * T:(t + 1) * T]), [], ["x%d" % b])

        def norm(t):
            b = t % 2
            emit_norm(xt[b], "x%d" % b, ht, "ht", fs, rstd, l, s, "f")

        def up(t):
            for f in range(NF):
                pa = ps[1 + f % 2]
                pb = ps[3 + f % 2]
                ra = "ps%d" % (1 + f % 2)
                rb = "ps%d" % (3 + f % 2)
                for kc in range(8):
                    pe(lambda e, f=f, kc=kc, pa=pa: e.matmul(pa[:, :], wup[:, kc, f * 128:(f + 1) * 128], ht[:, kc, :], start=(kc == 0), stop=(kc == 7)),
                       ["wup0_%d" % (f // 4), "ht"], [ra])
                for kc in range(8):
                    pe(lambda e, f=f, kc=kc, pb=pb: e.matmul(pb[:, :], wup[:, kc, DFF + f * 128:DFF + (f + 1) * 128], ht[:, kc, :], start=(kc == 0), stop=(kc == 7)),
                       ["wup1_%d" % (f // 4), "ht"], [rb])
                k = f % 2
                act(lambda e, pa=pa, k=k: e.activation(fs[k][:, :], pa[:, :], AF.Sigmoid), [ra], ["fs%d" % k])
                dve(lambda e, pa=pa, k=k: e.tensor_tensor(fs[k][:, :], fs[k][:, :], pa[:, :], ALU.mult), ["fs%d" % k, ra], ["fs%d" % k])
                dve(lambda e, pb=pb, k=k, f=f: e.tensor_tensor(gt[:, f, :], fs[k][:, :], pb[:, :], ALU.mult), ["fs%d" % k, rb], ["gt%d" % f])

        def down(t):
            b = t % 2
            for dc in range(8):
                py = ps[5 + dc % 2]
                ry = "ps%d" % (5 + dc % 2)
                for f in range(NF):
                    pe(lambda e, f=f, dc=dc, py=py: e.matmul(py[:, :], wdn[:, f, dc * 128:(dc + 1) * 128], gt[:, f, :], start=(f == 0), stop=(f == NF - 1)),
                       ["wdn%d" % (f // 2 * 2), "gt%d" % f], [ry])
                dve(lambda e, dc=dc, py=py, b=b: e.scalar_tensor_tensor(xt[b][:, dc, :], py[:, :], GCO[:, go + dc:go + dc + 1], xt[b][:, dc, :], ALU.mult, ALU.add),
                    [ry, "x%d" % b], ["x%d" % b])
            dma_sp(lambda e, t=t, b=b: e.dma_start(out=y_view[:, :, t * T:(t + 1) * T], in_=xt[b][:, :, :]), ["x%d" % b], [])

        load_x(0)
        norm(0)
        for t in range(NT):
            if t + 1 < NT:
                load_x(t + 1)
            up(t)
            if t + 1 < NT:
                norm(t + 1)
            down(t)
        P.barrier()

    def phase_m1(l):
        mem.reset()
        win = mem.alloc([128, 8, 8192], BF16)
        xt = mem.alloc([128, 8, T], F32)
        ht = [mem.alloc([128, 8, T], BF16) for _ in range(2)]
        stg = [mem.alloc([128, 2048], BF16) for _ in range(6)]
        fs = [mem.alloc([128, T], F32) for _ in range(3)]
        rstd = mem.alloc([128, T], F32)
        qdec = mem.alloc([128, 4, T], F32)
        wv = w_in[l].rearrange("(kc p) n -> p kc n", p=128)
        for blk in (0, 1, 2, 3, 10, 11, 4, 5, 6, 7, 8, 9, 12, 13, 14, 15):
            dma_pool(lambda e, blk=blk: e.dma_start(out=win[:, :, blk * 512:(blk + 1) * 512], in_=wv[:, :, blk * 512:(blk + 1) * 512]), [], ["win_%d" % blk])
        dma_sp(lambda e: e.dma_start(out=qdec[:, :, :], in_=qdecd[:, :, :]), [], ["qdec"])
        st = {"bank": 0, "slot": 0, "ev": 0}

        def nbank():
            b = 1 + st["bank"] % 6
            st["bank"] += 1
            return ps[b], "ps%d" % b

        def nslot():
            k = st["slot"] % 6
            st["slot"] += 1
            return stg[k], "stg%d" % k

        def load_x(t):
            dma_sp(lambda e, t=t: e.dma_start(out=xt[:, :, :], in_=y_view[:, :, t * T:(t + 1) * T]), [], ["x"])

        def norm(t):
            emit_norm(xt, "x", ht[t % 2], "ht%d" % (t % 2), fs, rstd, l, 1, "m")

        def fm_group(t, col, evac):
            h_ = ht[t % 2]
            hr = "ht%d" % (t % 2)
            pb, rb = nbank()
            hh = col // 4096
            for kc in range(8):
                pe(lambda e, kc=kc, pb=pb, col=col, h_=h_: e.matmul(pb[:, :], win[:, kc, col:col + 128], h_[:, kc, :], start=(kc == 0), stop=(kc == 7)),
                   ["win_%d" % (col // 512), hr], [rb])
            evac(pb, rb)

        def tm_group(t, j, col, evac):
            h_ = ht[t % 2]
            hr = "ht%d" % (t % 2)
            pb, rb = nbank()
            hh = col // 4096
            for kc in range(8):
                pe(lambda e, kc=kc, pb=pb, col=col, h_=h_, j=j: e.matmul(pb[:, :], h_[:, kc, j * 128:(j + 1) * 128], win[:, kc, col:col + 512], start=(kc == 0), stop=(kc == 7)),
                   ["win_%d" % (col // 512), hr], [rb])
            evac(pb, rb)

        def proj_a(t):
            t0 = t * T
            sq_, rq_ = nslot()
            sqd, rqd_ = nslot()
            for h in range(4):
                def ev(pb, rb, h=h):
                    act(lambda e: e.activation(sq_[:, h * T:(h + 1) * T], pb[:, :], AF.Identity), [rb], [rq_ + "a"])
                    dve(lambda e: e.tensor_tensor(sqd[:, h * T:(h + 1) * T], pb[:, :], qdec[:, h, :], ALU.mult), [rb, "qdec"], [rqd_ + "d"])
                fm_group(t, h * 128, ev)
            if DBG2 >= 2:
                dma_sp(lambda e: e.dma_start(out=rqT[:, :, t0:t0 + T].rearrange("h d s -> d h s"), in_=sq_[:, :].rearrange("p (h s) -> p h s", h=4)), [rq_ + "a", rq_ + "d"], [])
                dma_sp(lambda e: e.dma_start(out=rqdT[:, :, t0:t0 + T].rearrange("h d s -> d h s"), in_=sqd[:, :].rearrange("p (h s) -> p h s", h=4)), [rqd_ + "a", rqd_ + "d"], [])
            if DBG2 <= 2:
                return None
            sk, rk_ = nslot()
            for h in range(4):
                def ev(pb, rb, h=h):
                    act(lambda e: e.activation(sk[:, h * T:(h + 1) * T], pb[:, :], AF.Identity, scale=float(128.0 ** -0.5)), [rb], [rk_ + "a"])
                fm_group(t, 512 + h * 128, ev)
            dma_sp(lambda e: e.dma_start(out=rkT[:, :, t0:t0 + T].rearrange("h d s -> d h s"), in_=sk[:, :].rearrange("p (h s) -> p h s", h=4)), [rk_ + "a", rk_ + "d"], [])

            def chunked(colbase, dst, mode):
                dview = dst.rearrange("(c p) s -> p c s", p=128)
                for c0 in range(0, 8, 4):
                    sl, rs = nslot()
                    for cc in range(4):
                        c = c0 + cc

                        def ev(pb, rb, cc=cc):
                            o = sl[:, cc * T:(cc + 1) * T]
                            if mode == "silu":
                                k = st["ev"] % 3
                                st["ev"] += 1
                                act(lambda e: e.activation(fs[k][:, :], pb[:, :], AF.Sigmoid), [rb], ["fs%d" % k])
                                dve(lambda e: e.tensor_tensor(o, fs[k][:, :], pb[:, :], ALU.mult), [rb, "fs%d" % k], [rs + "d"])
                            elif mode == "sig":
                                act(lambda e: e.activation(o, pb[:, :], AF.Sigmoid), [rb], [rs + "a"])
                            elif mode == "q":
                                dve(lambda e: e.tensor_scalar_mul(o, pb[:, :], 0.125), [rb], [rs + "d"])
                            else:
                                dve(lambda e: e.tensor_copy(o, pb[:, :]), [rb], [rs + "d"])
                        fm_group(t, colbase + c * 128, ev)
                    dma_sp(lambda e, sl=sl, c0=c0: e.dma_start(out=dview[:, c0:c0 + 4, t0:t0 + T], in_=sl[:, :].rearrange("p (c s) -> p c s", c=4)), [rs + "a", rs + "d"], [])
            return chunked

        def proj_a2(t, chunked):
            chunked(2048, rgT, "silu")
            chunked(3072, dqT.rearrange("h r s -> (h r) s"), "q")
            chunked(4096, dkT.rearrange("h r s -> (h r) s"), "k")
            chunked(6144, sgrT, "sig")
            chunked(7168, sgdT, "sig")

        def proj_b(t):
            kt0 = t * 4
            skd, rkd_ = nslot()
            for j in range(4):
                def ev(pb, rb, j=j):
                    for h in range(4):
                        dve(lambda e, h=h: e.tensor_scalar_mul(skd[:, j * 512 + h * 128:j * 512 + (h + 1) * 128], pb[:, h * 128:(h + 1) * 128], kdec[:, h * 4 + j:h * 4 + j + 1]),
                            [rb, "kdec"], [rkd_ + "d"])
                tm_group(t, j, 512, ev)
            dma_sp(lambda e: e.dma_start(out=rkd[:, kt0:kt0 + 4, :], in_=skd[:, :].rearrange("p (j n) -> p j n", j=4)), [rkd_ + "a", rkd_ + "d"], [])
            for colbase, dst in ((1024, rv), (5120, dvs)):
                for jp in range(2):
                    sl, rs = nslot()
                    for jj in range(2):
                        j = jp * 2 + jj
                        for half in range(2):
                            def ev(pb, rb, jj=jj, half=half, sl=sl, rs=rs):
                                o = sl[:, jj * 1024 + half * 512:jj * 1024 + (half + 1) * 512]
                                if half == 0:
                                    act(lambda e: e.activation(o, pb[:, :], AF.Identity), [rb], [rs + "a"])
                                else:
                                    dve(lambda e: e.tensor_copy(o, pb[:, :]), [rb], [rs + "d"])
                            tm_group(t, j, colbase + half * 512, ev)
                    dma_sp(lambda e, sl=sl, jp=jp, dst=dst: e.dma_start(out=dst[:, kt0 + jp * 2:kt0 + jp * 2 + 2, :], in_=sl[:, :].rearrange("p (j n) -> p j n", j=2)), [rs + "a", rs + "d"], [])

        load_x(0)
        norm(0)
        for t in range(NT):
            if t + 1 < NT:
                load_x(t + 1)
            if DBG >= 2:
                ch = proj_a(t)
            if DBG >= 3:
                proj_b(t)
            if t + 1 < NT:
                norm(t + 1)
            if DBG >= 4:
                proj_a2(t, ch)
        P.barrier()

    def phase_m2d(l):
        mem.reset()
        KT = [[mem.alloc([128, S], BF16) for _ in range(2)] for _ in range(2)]
        VV = [mem.alloc([128, NKT, 128], BF16) for _ in range(2)]
        QT = [mem.alloc([128, 2, T], BF16) for _ in range(2)]
        PT = [mem.alloc([128, T], BF16) for _ in range(4)]
        C0 = mem.alloc([128, 4, T], F32)
        sfix = [mem.alloc([128, T], F32) for _ in range(2)]
        EP = [[mem.alloc([128, T], F32) for _ in range(2)] for _ in range(7)]
        ydst = [mem.alloc([128, T], BF16) for _ in range(2)]
        dma_sp(lambda e: e.dma_start(out=C0[:, :, :], in_=c0d[:, :, :]), [], ["C0"])
        cnt = {"i": 0, "q": 0, "o": 0}
        slopes = [2.0 ** (-(h + 1)) for h in range(8)]

        def kres(b):
            return ["K%d_%d%s" % (b, s, x) for s in range(2) for x in "ra"]

        def qres(qb):
            return ["Q%d_%d%s" % (qb, s, x) for s in range(2) for x in "ra"]

        def load_head(h):
            b = h % 2
            for s in range(2):
                dma_sp(lambda e, s=s: e.dma_start(out=KT[b][s][0:64, :], in_=dkT[h, s * 64:(s + 1) * 64, :]), [], ["K%d_%dr" % (b, s)])
                dma_pool(lambda e, s=s: e.dma_start(out=KT[b][s][64:68, :], in_=kaug[h, :, :]), [], ["K%d_%da" % (b, s)])
            dma_sp(lambda e: e.dma_start(out=VV[b][:, :, :], in_=dvs[:, :, h * 128:(h + 1) * 128]), [], ["V%d" % b])

        def load_q(h, qi):
            qb = cnt["q"] % 2
            cnt["q"] += 1
            q0 = qi * T
            for s in range(2):
                dma_sp(lambda e, s=s: e.dma_start(out=QT[qb][0:64, s, :], in_=dqT[h, s * 64:(s + 1) * 64, q0:q0 + T]), [], ["Q%d_%dr" % (qb, s)])
                dma_pool(lambda e, s=s: e.dma_start(out=QT[qb][64:68, s, :], in_=qaug[:, q0:q0 + T]), [], ["Q%d_%da" % (qb, s)])
            return qb

        def do_tile(h, qi, b, qb):
            nk = 4 * qi + 4
            lim = qi * T - 127 - SKIP_T / slopes[h]
            kt_lo = max(0, int(math.floor(lim / 128.0)) + 1) if lim >= 0 else 0
            steps = [(kt, s) for kt in range(kt_lo, nk) for s in range(2)]

            def c0_of(kt):
                m_ = kt - 4 * qi
                return 128 * m_ if m_ > 0 else 0
            base = cnt["i"]
            kr = kres(b)
            qr = qres(qb)
            slope = float(slopes[h])

            def s_mm(i):
                kt, s = steps[i]
                g = base + i
                pb = ps[1 + g % 3]
                c0 = c0_of(kt)
                pe(lambda e: e.matmul(pb[:, c0:T], KT[b][s][0:68, kt * 128:(kt + 1) * 128], QT[qb][0:68, s, c0:T], start=True, stop=True),
                   kr + qr, ["ps%d" % (1 + g % 3)])

            def step(i):
                kt, s = steps[i]
                g = base + i
                pb = ps[1 + g % 3]
                rb = "ps%d" % (1 + g % 3)
                pt = PT[g % 4]
                rp = "PT%d" % (g % 4)
                m = kt - 4 * qi
                c0 = c0_of(kt)
                if m >= 0:
                    sf = sfix[g % 2]
                    rsf = "sfix%d" % (g % 2)
                    dve(lambda e: e.scalar_tensor_tensor(sf[:, c0:T], C0[:, m, c0:T], slope, pb[:, c0:T], ALU.mult, ALU.add), ["C0", rb], [rsf])
                    act(lambda e: e.activation(pt[:, c0:T], sf[:, c0:T], AF.Exp), [rsf], [rp])
                else:
                    act(lambda e: e.activation(pt[:, c0:T], pb[:, c0:T], AF.Exp), [rb], [rp])
                po = ps[4 + s]
                pz = ps[6 + s]
                pe(lambda e: e.matmul(po[:, c0:T], VV[b][:, kt, :], pt[:, c0:T], start=(kt == kt_lo), stop=(kt == nk - 1)),
                   ["V%d" % b, rp], ["ps%d" % (4 + s)])
                pe(lambda e: e.matmul(pz[:, c0:T], ones_b[:, :], pt[:, c0:T], start=(kt == kt_lo), stop=(kt == nk - 1)),
                   [rp], ["ps%d" % (6 + s)])

            s_mm(0)
            s_mm(1)
            for i in range(len(steps)):
                if i + 2 < len(steps):
                    s_mm(i + 2)
                step(i)
            cnt["i"] += len(steps)
            ob = cnt["o"] % 2
            cnt["o"] += 1
            R0_, R1_, oo_, t1_, sq_, ln_, rs_ = [x[ob] for x in EP]
            sfx = "_%d" % ob
            act(lambda e: e.activation(ln_[:, :], ps[6][:, :], AF.Ln), ["ps6"], ["lnv" + sfx])
            act(lambda e: e.activation(rs_[:, :], ps[7][:, :], AF.Ln), ["ps7"], ["rstd" + sfx])
            act(lambda e: e.activation(R0_[:, :], ln_[:, :], AF.Exp, scale=-1.0), ["lnv" + sfx], ["R0" + sfx])
            act(lambda e: e.activation(R1_[:, :], rs_[:, :], AF.Exp, scale=-1.0), ["rstd" + sfx], ["R1" + sfx])
            dve(lambda e: e.tensor_tensor(oo_[:, :], ps[4][:, :], R0_[:, :], ALU.mult), ["ps4", "R0" + sfx], ["oo" + sfx])
            dve(lambda e: e.tensor_tensor(t1_[:, :], ps[5][:, :], R1_[:, :], ALU.mult), ["ps5", "R1" + sfx], ["t1" + sfx])
            dve(lambda e: e.scalar_tensor_tensor(oo_[:, :], t1_[:, :], neglam[:, l:l + 1], oo_[:, :], ALU.mult, ALU.add), ["t1" + sfx, "oo" + sfx], ["oo" + sfx])
            pool(lambda e: e.tensor_tensor(sq_[:, :], oo_[:, :], oo_[:, :], ALU.mult), ["oo" + sfx], ["sqo" + sfx])
            pe(lambda e: e.matmul(ps[0][:, :], ones_f[:, :], sq_[:, :], start=True, stop=True), ["sqo" + sfx], ["ps0"])
            act(lambda e: e.activation(ln_[:, :], ps[0][:, :], AF.Ln, bias=epsc[:, 1:2], scale=1.0), ["ps0"], ["lnv" + sfx])
            act(lambda e: e.activation(rs_[:, :], ln_[:, :], AF.Exp, scale=-0.5), ["lnv" + sfx], ["rstd" + sfx])
            dve(lambda e: e.scalar_tensor_tensor(ydst[ob][:, :], oo_[:, :], cdiff[:, l:l + 1], rs_[:, :], ALU.mult, ALU.mult), ["oo" + sfx, "rstd" + sfx], ["yd%d" % ob])
            q0 = qi * T
            dma_pool(lambda e: e.dma_start(out=ydT[h * 128:(h + 1) * 128, q0:q0 + T], in_=ydst[ob][:, :]), ["yd%d" % ob], [])

        seq = [(h, qi) for h in range(8) for qi in range(NT)]
        load_head(0)
        qb_next = load_q(0, 0)
        for idx, (h, qi) in enumerate(seq):
            if qi == 0 and h + 1 < 8:
                load_head(h + 1)
            qb = qb_next
            if idx + 1 < len(seq):
                qb_next = load_q(*seq[idx + 1])
            do_tile(h, qi, h % 2, qb)
        P.barrier()

    def phase_m2r(l):
        mem.reset()
        NB = S // T
        Dt = mem.alloc([128, 16, T], F32)
        qb_ = [mem.alloc([128, 4, T], BF16) for _ in range(2)]
        qdb = [mem.alloc([128, 4, T], BF16) for _ in range(2)]
        kb = [mem.alloc([128, 4, T], BF16) for _ in range(2)]
        kdb = [mem.alloc([128, 4, 512], BF16) for _ in range(2)]
        vb = [mem.alloc([128, 4, 1024], BF16) for _ in range(2)]
        rgb = [mem.alloc([128, 8, T], BF16) for _ in range(2)]
        Pm = [mem.alloc([128, 4, T], BF16) for _ in range(2)]
        st32 = mem.alloc([128, 4, 256], F32)
        stb = mem.alloc([128, 4, 256], BF16)
        y32 = [mem.alloc([128, 2, T], F32) for _ in range(2)]
        sqv = mem.alloc([128, 2, T], F32)
        rstd = mem.alloc([128, T], F32)
        tmp = mem.alloc([128, T], F32)
        ost = [mem.alloc([128, 8, T], BF16) for _ in range(2)]
        dma_sp(lambda e: e.dma_start(out=Dt[:, :, :], in_=dtd.rearrange("p h m s -> p (h m) s")), [], ["Dt"])
        gam = [1.0 - 2.0 ** (-5.0 - h) for h in range(4)]
        g512 = [float(np.float32(np.exp(np.float64(T) * np.log(np.float64(np.float32(g)))))) for g in gam]
        yr_view = yrT.rearrange("(c p) s -> p c s", p=128)
        rg_view = rgT.rearrange("(c p) s -> p c s", p=128)

        def load_blk(bi):
            b = bi % 2
            t0 = bi * T
            dma_sp(lambda e: e.dma_start(out=qb_[b][:, :, :], in_=rqT[:, :, t0:t0 + T].rearrange("h d s -> d h s")), [], ["q%d" % b])
            dma_sp(lambda e: e.dma_start(out=qdb[b][:, :, :], in_=rqdT[:, :, t0:t0 + T].rearrange("h d s -> d h s")), [], ["qd%d" % b])
            dma_sp(lambda e: e.dma_start(out=kb[b][:, :, :], in_=rkT[:, :, t0:t0 + T].rearrange("h d s -> d h s")), [], ["k%d" % b])
            dma_sp(lambda e: e.dma_start(out=kdb[b][:, :, :], in_=rkd[:, bi * 4:bi * 4 + 4, :]), [], ["kd%d" % b])
            dma_sp(lambda e: e.dma_start(out=vb[b][:, :, :], in_=rv[:, bi * 4:bi * 4 + 4, :]), [], ["v%d" % b])
            dma_sp(lambda e: e.dma_start(out=rgb[b][:, :, :], in_=rg_view[:, :, t0:t0 + T]), [], ["rg%d" % b])

        def do_head(bi, b, h, gi):
            pmb = Pm[gi % 2]
            rpm = "Pm%d" % (gi % 2)
            yb = y32[gi % 2]
            ryb = "y32_%d" % (gi % 2)
            osb = ost[b]
            ros = "ost%d" % b

            def smm(m):
                pb = ps[1 + m % 2]
                rb = "ps%d" % (1 + m % 2)
                pe(lambda e: e.matmul(pb[:, :], kb[b][:, h, m * 128:(m + 1) * 128], qb_[b][:, h, :], start=True, stop=True), ["k%d" % b, "q%d" % b], [rb])
                dve(lambda e: e.tensor_tensor(pmb[:, m, :], pb[:, :], Dt[:, h * 4 + m, :], ALU.mult), [rb, "Dt"], [rpm + "_%d" % m])
            for m in range(4):
                smm(m)

            def ymm(vc):
                py = ps[3 + vc]
                ry = "ps%d" % (3 + vc)
                for m in range(4):
                    pe(lambda e, m=m: e.matmul(py[:, :], vb[b][:, m, h * 256 + vc * 128:h * 256 + (vc + 1) * 128], pmb[:, m, :], start=(m == 0), stop=(m == 3 and bi == 0)),
                       ["v%d" % b, rpm + "_%d" % m], [ry])
                if bi > 0:
                    pe(lambda e: e.matmul(py[:, :], stb[:, h, vc * 128:(vc + 1) * 128], qdb[b][:, h, :], start=False, stop=True), ["stb%d" % h, "qd%d" % b], [ry])
            for vc in range(2):
                ymm(vc)
            for m in range(4):
                pe(lambda e, m=m: e.matmul(ps[5][:, 0:256], kdb[b][:, m, h * 128:(h + 1) * 128], vb[b][:, m, h * 256:(h + 1) * 256], start=(m == 0), stop=(m == 3)),
                   ["kd%d" % b, "v%d" % b], ["ps5"])
            if bi == 0:
                dve(lambda e: e.tensor_copy(st32[:, h, :], ps[5][:, 0:256]), ["ps5"], ["st32_%d" % h])
            else:
                dve(lambda e: e.scalar_tensor_tensor(st32[:, h, :], st32[:, h, :], float(g512[h]), ps[5][:, 0:256], ALU.mult, ALU.add), ["ps5", "st32_%d" % h], ["st32_%d" % h])
            if bi + 1 < NB:
                pool(lambda e: e.tensor_copy(stb[:, h, :], st32[:, h, :]), ["st32_%d" % h], ["stb%d" % h])

            def gn(vc):
                py = ps[3 + vc]
                ry = "ps%d" % (3 + vc)
                act(lambda e: e.activation(sqv[:, vc, :], py[:, :], AF.Square), [ry], ["sqv%d" % vc])
                act(lambda e: e.activation(yb[:, vc, :], py[:, :], AF.Identity), [ry], [ryb + "_%d" % vc])
                pe(lambda e: e.matmul(ps[0][:, :], ones_f[:, :], sqv[:, vc, :], start=(vc == 0), stop=(vc == 1)), ["sqv%d" % vc], ["ps0"])
            for vc in range(2):
                gn(vc)
            act(lambda e: e.activation(rstd[:, :], ps[0][:, :], AF.Ln, bias=epsc[:, 2:3], scale=1.0), ["ps0"], ["rstd"])
            act(lambda e: e.activation(rstd[:, :], rstd[:, :], AF.Exp, scale=-0.5), ["rstd"], ["rstd"])

            def fin(vc):
                c = h * 2 + vc
                dve(lambda e: e.scalar_tensor_tensor(tmp[:, :], yb[:, vc, :], cret[:, l * 8 + c:l * 8 + c + 1], rstd[:, :], ALU.mult, ALU.mult), [ryb + "_%d" % vc, "rstd"], ["tmp"])
                pool(lambda e: e.tensor_tensor(osb[:, c, :], tmp[:, :], rgb[b][:, c, :], ALU.mult), ["tmp", "rg%d" % b], [ros])
            for vc in range(2):
                fin(vc)

        def store_blk(bi, b):
            t0 = bi * T
            dma_pool(lambda e: e.dma_start(out=yr_view[:, :, t0:t0 + T], in_=ost[b][:, :, :]), ["ost%d" % b], [])

        load_blk(0)
        gi = 0
        for bi in range(NB):
            if bi + 1 < NB:
                load_blk(bi + 1)
            for h in range(4):
                do_head(bi, bi % 2, h, gi)
                gi += 1
            store_blk(bi, bi % 2)
        P.barrier()

    def phase_m3(l):
        mem.reset()
        wr = mem.alloc([128, 8, D], BF16); wd = mem.alloc([128, 8, D], BF16); wo = mem.alloc([128, 8, D], BF16)
        xt = [mem.alloc([128, 8, T], F32) for _ in range(2)]
        yr = [mem.alloc([128, 8, T], BF16) for _ in range(2)]
        yd = [mem.alloc([128, 8, T], BF16) for _ in range(2)]
        sr = [mem.alloc([128, 8, T], BF16) for _ in range(2)]
        sd = [mem.alloc([128, 8, T], BF16) for _ in range(2)]
        mg = mem.alloc([128, 8, T], BF16)
        ta = [mem.alloc([128, T], F32) for _ in range(2)]
        tb = [mem.alloc([128, T], F32) for _ in range(2)]

        def loadw(w_s, w_d, nm):
            v = w_d[l].rearrange("(kc p) n -> p kc n", p=128)
            for hh in range(2):
                dma_pool(lambda e, hh=hh: e.dma_start(out=w_s[:, hh * 4:(hh + 1) * 4, :], in_=v[:, hh * 4:(hh + 1) * 4, :]), [], ["%s%d" % (nm, hh)])
        loadw(wr, w_rb, "wr"); loadw(wd, w_db, "wd"); loadw(wo, w_out, "wo")
        views = [t_.rearrange("(c p) s -> p c s", p=128) for t_ in (yrT, ydT, sgrT, sgdT)]
        go = (l * 3 + 1) * 8

        def load(t):
            b = t % 2
            t0 = t * T
            dma_sp(lambda e: e.dma_start(out=xt[b][:, :, :], in_=y_view[:, :, t0:t0 + T]), [], ["x%d" % b])
            for buf, v, nm in ((yr, views[0], "yr"), (yd, views[1], "yd"), (sr, views[2], "sr"), (sd, views[3], "sd")):
                dma_sp(lambda e, buf=buf, v=v: e.dma_start(out=buf[b][:, :, :], in_=v[:, :, t0:t0 + T]), [], ["%s%d" % (nm, b)])

        def do_tile(t, b):
            def branch(dc):
                pr = ps[1 + dc % 2]; rr = "ps%d" % (1 + dc % 2)
                pd = ps[3 + dc % 2]; rd = "ps%d" % (3 + dc % 2)
                for kc in range(8):
                    pe(lambda e, kc=kc: e.matmul(pr[:, :], wr[:, kc, dc * 128:(dc + 1) * 128], yr[b][:, kc, :], start=(kc == 0), stop=(kc == 7)), ["wr%d" % (kc // 4), "yr%d" % b], [rr])
                for kc in range(8):
                    pe(lambda e, kc=kc: e.matmul(pd[:, :], wd[:, kc, dc * 128:(dc + 1) * 128], yd[b][:, kc, :], start=(kc == 0), stop=(kc == 7)), ["wd%d" % (kc // 4), "yd%d" % b], [rd])
                k = dc % 2
                dve(lambda e: e.tensor_tensor(ta[k][:, :], pr[:, :], sr[b][:, dc, :], ALU.mult), [rr, "sr%d" % b], ["ta%d" % k])
                dve(lambda e: e.tensor_tensor(tb[k][:, :], pd[:, :], sd[b][:, dc, :], ALU.mult), [rd, "sd%d" % b], ["tb%d" % k])
                pool(lambda e: e.tensor_tensor(mg[:, dc, :], ta[k][:, :], tb[k][:, :], ALU.add), ["ta%d" % k, "tb%d" % k], ["mg%d" % dc])
            for dc in range(8):
                branch(dc)

            def outp(dc):
                po = ps[5 + dc % 2]; ro = "ps%d" % (5 + dc % 2)
                for kc in range(8):
                    pe(lambda e, kc=kc: e.matmul(po[:, :], wo[:, kc, dc * 128:(dc + 1) * 128], mg[:, kc, :], start=(kc == 0), stop=(kc == 7)), ["wo%d" % (kc // 4), "mg%d" % kc], [ro])
                dve(lambda e: e.scalar_tensor_tensor(xt[b][:, dc, :], po[:, :], GCO[:, go + dc:go + dc + 1], xt[b][:, dc, :], ALU.mult, ALU.add), [ro, "x%d" % b], ["x%d" % b])
            for dc in range(8):
                outp(dc)
            t0 = t * T
            dma_sp(lambda e: e.dma_start(out=y_view[:, :, t0:t0 + T], in_=xt[b][:, :, :]), ["x%d" % b], [])

        load(0)
        for t in range(NT):
            if t + 1 < NT:
                load(t + 1)
            do_tile(t, t % 2)
        P.barrier()

    def phase_fin():
        mem.reset()
        xt = [mem.alloc([128, 8, T], F32) for _ in range(2)]
        fs = [mem.alloc([128, T], F32) for _ in range(3)]
        rstd = mem.alloc([128, T], F32)

        def load(t):
            b = t % 2
            dma_sp(lambda e: e.dma_start(out=xt[b][:, :, :], in_=y_view[:, :, t * T:(t + 1) * T]), [], ["x%d" % b])

        def do_tile(t, b):
            def st(c):
                k = c % 3
                act(lambda e: e.activation(fs[k][:, :], xt[b][:, c, :], AF.Square), ["x%d" % b], ["fs%d" % k])
                pe(lambda e: e.matmul(ps[0][:, :], ones_f[:, :], fs[k][:, :], start=(c == 0), stop=(c == 7)), ["fs%d" % k], ["ps0"])
            for c in range(8):
                st(c)
            act(lambda e: e.activation(rstd[:, :], ps[0][:, :], AF.Ln, bias=epsc[:, 0:1], scale=1.0), ["ps0"], ["rstd"])
            act(lambda e: e.activation(rstd[:, :], rstd[:, :], AF.Exp, scale=-0.5), ["rstd"], ["rstd"])

            def sc(c):
                dve(lambda e: e.scalar_tensor_tensor(xt[b][:, c, :], xt[b][:, c, :], fnc[:, c:c + 1], rstd[:, :], ALU.mult, ALU.mult), ["x%d" % b, "rstd"], ["x%d" % b])
            for c in range(8):
                sc(c)
            dma_sp(lambda e: e.dma_start(out=y_view[:, :, t * T:(t + 1) * T], in_=xt[b][:, :, :]), ["x%d" % b], [])

        load(0)
        for t in range(NT):
            if t + 1 < NT:
                load(t + 1)
            do_tile(t, t % 2)
        P.barrier()

    plist = [phase_pre]
    for l in range(depth):
        plist.append(lambda l=l: phase_ffn(l, 0, 0, x_view if l == 0 else y_view))
        plist.append(lambda l=l: phase_m1(l))
        plist.append(lambda l=l: phase_m2d(l))
        plist.append(lambda l=l: phase_m2r(l))
        plist.append(lambda l=l: phase_m3(l))
        plist.append(lambda l=l: phase_ffn(l, 1, 2, y_view))
    plist.append(phase_fin)
    for pf in plist[:upto]:
        pf()
    P.finalize()

    from contextlib import ExitStack
    sems = {}
    with ExitStack() as es:
        for k in ("pe", "act", "dve", "pool", "bar"):
            sems[k] = es.enter_context(nc.semaphore("s_" + k))
        for q in ("sp", "pool"):
            for i in range(KDMA):
                sems["%s_d%d" % (q, i)] = es.enter_context(nc.semaphore("s_%s_d%d" % (q, i)))
        with nc.Block() as block:
            @block.tensor
            def _(e):
                P.run_stream("pe", e, sems)

            @block.scalar
            def _(e):
                P.run_stream("act", e, sems)

            @block.vector
            def _(e):
                P.run_stream("dve", e, sems)

            @block.gpsimd
            def _(e):
                P.run_stream("pool", e, sems)

            @block.sync
            def _(e):
                P.run_stream("sp", e, sems)
    return nc


def make_consts(S):
    pos = np.arange(S)
    a = (pos // 128).astype(np.float64)
    b_ = (pos % 128).astype(np.float64)
    qaug = np.stack([-128.0 * a, -b_, np.ones(S), np.ones(S)]).astype(np.float32)
    kaug = np.zeros((8, 4, S), np.float32)
    for h in range(8):
        sl = 2.0 ** (-(h + 1))
        kaug[h, 0] = sl
        kaug[h, 1] = sl
        kaug[h, 2] = sl * 128.0 * a
        kaug[h, 3] = sl * b_
    i = np.arange(128)[:, None, None]
    m = np.arange(4)[None, :, None]
    j = np.arange(512)[None, None, :]
    kp = 128 * m + i
    allowed = (kp // 64) <= (j // 64)
    c0 = np.where(allowed, np.where(kp > j, -2.0 * (kp - j), 0.0), -1.0e6).astype(np.float32)
    dtd = np.zeros((128, 4, 4, 512), np.float32)
    qdec = np.zeros((128, 4, 512), np.float32)
    kdec = np.zeros((128, 16), np.float32)
    for h in range(4):
        g = np.float64(np.float32(1.0 - 2.0 ** (-5.0 - h)))
        lg = np.log(g)
        dd = np.where(allowed, np.exp(lg * np.abs(j - kp)), 0.0)
        dtd[:, h, :, :] = dd.astype(np.float32)
        qdec[:, h, :] = np.exp(lg * np.arange(512))[None, :].astype(np.float32)
        for jj in range(4):
            r = 128 * jj + np.arange(128)
            kdec[:, h * 4 + jj] = (np.exp(lg * (512 - r)) * (128.0 ** -0.5)).astype(np.float32)
    return dict(qaug=qaug, kaug=kaug, c0d=c0, dtd=dtd, qdecd=qdec, kdecd=kdec)


_CACHE = {}


def run(inputs, S, depth, n_cores, upto=99):
    key = (S, depth, upto)
    if key not in _CACHE:
        _CACHE[key] = build_program(S, depth, upto)
    nc = _CACHE[key]
    f = lambda a: np.ascontiguousarray(np.asarray(a, dtype=np.float32))
    x = f(inputs["x"]); c = f(inputs["c"])
    shared = dict(
        w_ada=f(inputs["w_ada"]),
        b_adaT=f(f(inputs["b_ada"]).reshape(depth, 72, 128).transpose(2, 0, 1).reshape(128, depth * 72)),
        norm_wT=f(f(inputs["norm_w"]).reshape(depth, 3, 8, 128).transpose(3, 0, 1, 2).reshape(128, depth * 24)),
        w_up=f(inputs["w_ffn_up"]), w_down=f(inputs["w_ffn_down"]), w_in=f(inputs["w_in"]),
        ret_gnT=f(f(inputs["ret_gn"]).reshape(depth, 8, 128).transpose(2, 0, 1).reshape(128, depth * 8)),
        lam_in=f(np.broadcast_to(np.stack([f(inputs["lambda_q1"]), f(inputs["lambda_k1"]), f(inputs["lambda_q2"]), f(inputs["lambda_k2"])]).reshape(1, -1), (128, 4 * depth * 64))),
        sublnT=f(f(inputs["diff_subln"]).T),
        w_rb=f(inputs["w_ret_branch"]), w_db=f(inputs["w_diff_branch"]), w_out=f(inputs["w_out"]),
        fnT=f(f(inputs["final_norm"]).reshape(8, 128).T),
    )
    shared.update(make_consts(S))
    in_maps = []
    for b in range(n_cores):
        m = dict(shared)
        m["xT"] = f(x[b].T)
        m["cT"] = f(c[b].reshape(8, 128).T)
        in_maps.append(m)
    res = run_bass_kernel_spmd(nc, in_maps, core_ids=list(range(n_cores)))
    out = np.stack([np.ascontiguousarray(np.asarray(r["yT"]).T) for r in res.results])
    return out.astype(np.float32)


def kernel(**inputs):
    return run(inputs, SEQ, NLAYERS, 8)
```

```python
import math
import numpy as np
import concourse.bass as bass
import concourse.mybir as mybir
from concourse.bass_utils import run_bass_kernel_spmd

F32 = mybir.dt.float32
BF16 = mybir.dt.bfloat16
AF = mybir.ActivationFunctionType
ALU = mybir.AluOpType
AX = mybir.AxisListType

D = 1024
KC = 8
T = 512
DFF = 2816
NF = 22
EPS = 1e-6
KDMA = 8
SKIP_T = 128.0
NLAYERS = 4
import os
DBG = int(os.environ.get('KDBG', '9'))
DBG2 = int(os.environ.get('KDBG2', '9'))
SEQ = 8192


class Op:
    __slots__ = ("eng", "fn", "deps", "dma", "sig", "cnt", "semkey", "qn", "bar")


class Prog:
    def __init__(self):
        self.streams = {e: [] for e in ("pe", "act", "dve", "pool", "sp")}
        self.lastw = {}
        self.readers = {}
        self.nbar = 0

    def add(self, eng, fn, reads=(), writes=(), dma=False):
        op = Op()
        op.eng = eng; op.fn = fn; op.dma = dma; op.sig = False; op.bar = 0
        op.cnt = 0; op.semkey = None; op.qn = 0
        deps = {}
        for r in reads:
            w = self.lastw.get(r)
            if w is not None:
                deps[id(w)] = (w, 0)
            if r.startswith("ps"):
                rd = self.readers.get(r)
                if rd:
                    for k, o in rd.items():
                        if k != "dma" and k != eng and id(o) not in deps:
                            deps[id(o)] = (o, 3)
        for r in writes:
            w = self.lastw.get(r)
            if w is not None and id(w) not in deps:
                deps[id(w)] = (w, 1)
            rd = self.readers.get(r)
            if rd:
                for k, o in rd.items():
                    if k == "dma":
                        for oo in o:
                            if id(oo) not in deps:
                                deps[id(oo)] = (oo, 2)
                    elif id(o) not in deps:
                        deps[id(o)] = (o, 2)
        dl = []
        for w, kind in deps.values():
            if (not w.dma) and (not dma) and w.eng == eng:
                if eng == "pe":
                    continue
            dl.append(w)
            w.sig = True
        op.deps = dl
        for r in writes:
            self.lastw[r] = op
            self.readers[r] = {}
        for r in reads:
            rd = self.readers.setdefault(r, {})
            if dma:
                rd.setdefault("dma", []).append(op)
            else:
                rd[eng] = op
        self.streams[eng].append(op)
        return op

    def barrier(self):
        self.nbar += 1
        for e, st in self.streams.items():
            for o in reversed(st):
                if o.bar:
                    break
                if not o.dma:
                    o.sig = True
                    break
            op = Op()
            op.eng = e; op.fn = None; op.dma = False; op.sig = False; op.bar = self.nbar
            op.deps = []; op.cnt = 0; op.semkey = None; op.qn = 0
            st.append(op)
        self.lastw = {}
        self.readers = {}

    def finalize(self):
        for e, st in self.streams.items():
            c = 0
            q = 0
            for op in st:
                if op.bar:
                    continue
                if op.dma:
                    op.semkey = "%s_d%d" % (e, q % KDMA)
                    op.cnt = 16 * (q // KDMA + 1)
                    op.qn = q
                    q += 1
                elif op.sig:
                    c += 1
                    op.cnt = c
                    op.semkey = e

    def run_stream(self, ename, eng, sems):
        waited = {}

        def wait(key, val):
            if waited.get(key, 0) < val:
                eng.wait_ge(sems[key], val)
                waited[key] = val

        own = 0
        dtot = {}
        for op in self.streams[ename]:
            if op.bar:
                if ename != "sp" and own > 0:
                    wait(ename, own)
                for k, v in dtot.items():
                    wait(k, v)
                eng.sem_inc(sems["bar"], 1)
                wait("bar", 5 * op.bar)
                continue
            for d in op.deps:
                wait(d.semkey, d.cnt)
            if op.dma and op.qn >= KDMA:
                wait(op.semkey, op.cnt - 16)
            ins = op.fn(eng)
            if op.dma:
                ins.then_inc(sems[op.semkey], 16)
                dtot[op.semkey] = op.cnt
            elif op.sig:
                ins.then_inc(sems[ename], 1)
                own = op.cnt


class Mem:
    def __init__(self, nc):
        self.nc = nc
        self.off = 16576
        self.n = 0
        self.base = 16576

    def alloc(self, shape, dtype):
        nb = 1
        for s in shape[1:]:
            nb *= s
        nb *= 4 if dtype == F32 else 2
        nb = (nb + 63) // 64 * 64
        h = self.nc.alloc_sbuf_tensor_at("sb%d" % self.n, list(shape), dtype, offset=self.off)
        self.n += 1
        self.off += nb
        assert self.off <= 229376, self.off
        return h

    def set_base(self):
        self.base = self.off

    def reset(self):
        self.off = self.base


def lam_init_of(l):
    return 0.8 - 0.6 * math.exp(-0.3 * l)


def build_program(S, depth, upto=99):
    NT = S // T
    NKT = S // 128
    nc = bass.Bass("TRN2", target_bir_lowering=False)
    P = Prog()
    mem = Mem(nc)

    def din(name, shape, dt=F32):
        return nc.dram_tensor(name, list(shape), dt, kind="ExternalInput")

    xT = din("xT", [D, S])
    cT = din("cT", [128, 8])
    w_ada = din("w_ada", [depth, D, 9216])
    b_adaT = din("b_adaT", [128, depth * 72])
    norm_wT = din("norm_wT", [128, depth * 24])
    w_up = din("w_up", [depth, 2, D, 2 * DFF])
    w_down = din("w_down", [depth, 2, DFF, D])
    w_in = din("w_in", [depth, D, 8192])
    ret_gnT = din("ret_gnT", [128, depth * 8])
    lam_in = din("lam_in", [128, 4 * depth * 64])
    sublnT = din("sublnT", [128, depth])
    w_rb = din("w_rb", [depth, D, D])
    w_db = din("w_db", [depth, D, D])
    w_out = din("w_out", [depth, D, D])
    fnT = din("fnT", [128, 8])
    qaug = din("qaug", [4, S])
    kaug = din("kaug", [8, 4, S])
    c0d = din("c0d", [128, 4, 512])
    dtd = din("dtd", [128, 4, 4, 512])
    qdecd = din("qdecd", [128, 4, 512])
    kdecd = din("kdecd", [128, 16])
    yT = nc.dram_tensor("yT", [D, S], F32, kind="ExternalOutput")

    def dscr(name, shape):
        return nc.dram_tensor(name, list(shape), BF16, kind="Internal")

    rqT = dscr("rqT", [4, 128, S]); rqdT = dscr("rqdT", [4, 128, S]); rkT = dscr("rkT", [4, 128, S])
    rkd = dscr("rkd", [128, NKT, 512]); rv = dscr("rv", [128, NKT, 1024])
    rgT = dscr("rgT", [D, S]); dqT = dscr("dqT", [8, 128, S]); dkT = dscr("dkT", [8, 128, S])
    dvs = dscr("dvs", [128, NKT, 1024]); sgrT = dscr("sgrT", [D, S]); sgdT = dscr("sgdT", [D, S])
    yrT = dscr("yrT", [D, S]); ydT = dscr("ydT", [D, S])

    ps = [nc.alloc_psum_tensor("ps%d" % i, [128, 512], F32) for i in range(8)]

    MOD = mem.alloc([128, depth * 72], F32)
    ACO = mem.alloc([128, depth * 24], F32)
    GCO = mem.alloc([128, depth * 24], F32)
    ones_b = mem.alloc([128, 128], BF16)
    ones_f = mem.alloc([128, 128], F32)
    neglam = mem.alloc([128, depth], F32)
    cdiff = mem.alloc([128, depth], F32)
    cret = mem.alloc([128, depth * 8], F32)
    fnc = mem.alloc([128, 8], F32)
    kdec = mem.alloc([128, 16], F32)
    epsc = mem.alloc([128, 4], F32)
    mem.set_base()

    x_view = xT.rearrange("(kc p) s -> p kc s", p=128)
    y_view = yT.rearrange("(kc p) s -> p kc s", p=128)

    def act(fn, r, w):
        return P.add("act", fn, r, w)

    def dve(fn, r, w):
        return P.add("dve", fn, r, w)

    def pool(fn, r, w):
        return P.add("pool", fn, r, w)

    def pe(fn, r, w):
        return P.add("pe", fn, r, w)

    def dma_sp(fn, r, w):
        return P.add("sp", fn, r, w, dma=True)

    def dma_pool(fn, r, w):
        return P.add("pool", fn, r, w, dma=True)

    def phase_pre():
        mem.reset()
        cnd = mem.alloc([128, 8], F32)
        cnd2 = mem.alloc([128, 8], F32)
        bad = mem.alloc([128, depth * 72], F32)
        nwt = mem.alloc([128, depth * 24], F32)
        rgn = mem.alloc([128, depth * 8], F32)
        sbl = mem.alloc([128, depth], F32)
        fnt = mem.alloc([128, 8], F32)
        lmi = mem.alloc([128, 4 * depth * 64], F32)
        lpr = mem.alloc([128, 2 * depth * 64], F32)
        lsum = mem.alloc([128, 2 * depth], F32)
        lexp = mem.alloc([128, 2 * depth], F32)
        ldif = mem.alloc([128, depth], F32)
        wa = [mem.alloc([128, 8, 1152], F32) for _ in range(2)]

        dma_sp(lambda e: e.dma_start(out=cnd[:, :], in_=cT[:, :]), [], ["cnd"])
        dma_sp(lambda e: e.dma_start(out=bad[:, :], in_=b_adaT[:, :]), [], ["bad"])
        dma_sp(lambda e: e.dma_start(out=nwt[:, :], in_=norm_wT[:, :]), [], ["nwt"])
        dma_sp(lambda e: e.dma_start(out=rgn[:, :], in_=ret_gnT[:, :]), [], ["rgn"])
        dma_sp(lambda e: e.dma_start(out=sbl[:, :], in_=sublnT[:, :]), [], ["sbl"])
        dma_sp(lambda e: e.dma_start(out=fnt[:, :], in_=fnT[:, :]), [], ["fnt"])
        dma_sp(lambda e: e.dma_start(out=lmi[:, :], in_=lam_in[:, :]), [], ["lmi"])
        dma_sp(lambda e: e.dma_start(out=kdec[:, :], in_=kdecd[:, :]), [], ["kdec"])
        dve(lambda e: e.memset(ones_b[:, :], 1.0), [], ["ones_b"])
        dve(lambda e: e.memset(ones_f[:, :], 1.0), [], ["ones_f"])
        dve(lambda e: e.memset(epsc[:, 0:1], float(D * EPS)), [], ["epsc"])
        dve(lambda e: e.memset(epsc[:, 1:2], float(128 * EPS)), [], ["epsc"])
        dve(lambda e: e.memset(epsc[:, 2:3], float(256 * EPS)), [], ["epsc"])
        act(lambda e: e.activation(cnd2[:, :], cnd[:, :], AF.Sigmoid), ["cnd"], ["cnd2"])
        dve(lambda e: e.tensor_tensor(cnd2[:, :], cnd2[:, :], cnd[:, :], ALU.mult), ["cnd", "cnd2"], ["cnd2"])
        LD = depth * 64
        dve(lambda e: e.tensor_tensor(lpr[:, 0:LD], lmi[:, 0:LD], lmi[:, LD:2 * LD], ALU.mult), ["lmi"], ["lpr0"])
        dve(lambda e: e.tensor_tensor(lpr[:, LD:2 * LD], lmi[:, 2 * LD:3 * LD], lmi[:, 3 * LD:4 * LD], ALU.mult), ["lmi"], ["lpr1"])
        dve(lambda e: e.reduce_sum(lsum[:, :], lpr[:, :].rearrange("p (a d) -> p a d", d=64), AX.X), ["lpr0", "lpr1"], ["lsum"])
        act(lambda e: e.activation(lexp[:, :], lsum[:, :], AF.Exp), ["lsum"], ["lexp"])
        dve(lambda e: e.tensor_tensor(ldif[:, :], lexp[:, depth:2 * depth], lexp[:, 0:depth], ALU.subtract), ["lexp"], ["ldif"])
        for l in range(depth):
            li = lam_init_of(l)
            dve(lambda e, l=l, li=li: e.tensor_scalar_add(neglam[:, l:l + 1], ldif[:, l:l + 1], -li), ["ldif"], ["neglam"])
            dve(lambda e, l=l, li=li: e.tensor_scalar_mul(cdiff[:, l:l + 1], sbl[:, l:l + 1], (1.0 - li) * math.sqrt(128.0)), ["sbl"], ["cdiff"])
        dve(lambda e: e.tensor_scalar_mul(cret[:, :], rgn[:, :], 16.0), ["rgn"], ["cret"])
        dve(lambda e: e.tensor_scalar_mul(fnc[:, :], fnt[:, :], 32.0), ["fnt"], ["fnc"])
        nb = 0
        for l in range(depth):
            wv = w_ada[l].rearrange("(kc p) n -> p kc n", p=128)
            for j in range(8):
                buf = wa[nb % 2]
                rn = "wa%d" % (nb % 2)
                nb += 1
                for kc in range(8):
                    dma_sp(lambda e, buf=buf, kc=kc, j=j, wv=wv: e.dma_start(out=buf[:, kc, :], in_=wv[:, kc, j * 1152:(j + 1) * 1152]), [], [rn + "_%d" % kc])
                for m in range(9):
                    col = l * 72 + j * 9 + m
                    for kc in range(8):
                        pe(lambda e, buf=buf, kc=kc, m=m, col=col: e.matmul(ps[0][:, col:col + 1], buf[:, kc, m * 128:(m + 1) * 128], cnd2[:, kc:kc + 1], start=(kc == 0), stop=(kc == 7)),
                           [rn + "_%d" % kc, "cnd2"], ["psM"])
        dve(lambda e: e.tensor_tensor(MOD[:, :], ps[0][:, 0:depth * 72], bad[:, :], ALU.add), ["psM", "bad"], ["MOD"])
        for l in range(depth):
            for s in range(3):
                o = (l * 3 + s) * 8
                sc = l * 72 + s * 24 + 8
                gc = l * 72 + s * 24 + 16
                dve(lambda e, o=o, sc=sc: e.scalar_tensor_tensor(ACO[:, o:o + 8], MOD[:, sc:sc + 8], 1.0, nwt[:, o:o + 8], ALU.add, ALU.mult), ["MOD", "nwt"], ["ACO%d" % o])
                dve(lambda e, o=o: e.tensor_scalar_mul(ACO[:, o:o + 8], ACO[:, o:o + 8], 32.0), ["ACO%d" % o], ["ACO%d" % o])
                dve(lambda e, o=o, gc=gc, s=s: e.tensor_scalar_mul(GCO[:, o:o + 8], MOD[:, gc:gc + 8], 1.0 if s == 1 else 0.5), ["MOD"], ["GCO"])
        P.barrier()

    def emit_norm(xt, xr, ht, hr, fs, rstd, l, s, tag):
        for c in range(8):
            k = c % 3
            act(lambda e, c=c, k=k: e.activation(fs[k][:, :], xt[:, c, :], AF.Square), [xr], ["fs%d" % k])
            pe(lambda e, c=c, k=k: e.matmul(ps[0][:, :], ones_f[:, :], fs[k][:, :], start=(c == 0), stop=(c == 7)), ["fs%d" % k], ["ps0"])
        act(lambda e: e.activation(rstd[:, :], ps[0][:, :], AF.Ln, bias=epsc[:, 0:1], scale=1.0), ["ps0"], ["rstd"])
        act(lambda e: e.activation(rstd[:, :], rstd[:, :], AF.Exp, scale=-0.5), ["rstd"], ["rstd"])
        o = (l * 3 + s) * 8
        sh = l * 72 + s * 24
        for c in range(8):
            k = c % 3
            dve(lambda e, c=c, k=k: e.tensor_tensor(fs[k][:, :], xt[:, c, :], rstd[:, :], ALU.mult), [xr, "rstd"], ["fs%d" % k])
            act(lambda e, c=c, k=k: e.activation(ht[:, c, :], fs[k][:, :], AF.Identity, bias=MOD[:, sh + c:sh + c + 1], scale=ACO[:, o + c:o + c + 1]),
                ["fs%d" % k], [hr])

    def phase_ffn(l, fi, s, src_view):
        mem.reset()
        wup = mem.alloc([128, 8, 2 * DFF], BF16)
        wdn = mem.alloc([128, NF, D], BF16)
        xt = [mem.alloc([128, 8, T], F32) for _ in range(2)]
        ht = mem.alloc([128, 8, T], BF16)
        gt = mem.alloc([128, NF, T], BF16)
        fs = [mem.alloc([128, T], F32) for _ in range(3)]
        rstd = mem.alloc([128, T], F32)
        upv = w_up[l, fi].rearrange("(kc p) n -> p kc n", p=128)
        dnv = w_down[l, fi].rearrange("(fc p) n -> p fc n", p=128)
        for j in range(0, NF, 4):
            w_ = min(4, NF - j) * 128
            for part in range(2):
                c_ = part * DFF + j * 128
                dma_pool(lambda e, c_=c_, w_=w_: e.dma_start(out=wup[:, :, c_:c_ + w_], in_=upv[:, :, c_:c_ + w_]), [], ["wup%d_%d" % (part, j // 4)])
        for q in range(0, NF, 2):
            dma_pool(lambda e, q=q: e.dma_start(out=wdn[:, q:q + 2, :], in_=dnv[:, q:q + 2, :]), [], ["wdn%d" % q])
        wup_r = ["wup%d" % kc for kc in range(8)]
        go = (l * 3 + s) * 8

        def load_x(t):
            b = t % 2
            dma_sp(lambda e, t=t, b=b: e.dma_start(out=xt[b][:, :, :], in_=src_view[:, :, t * T:(t + 1) * T]), [], ["x%d" % b])

        def norm(t):
            b = t % 2
            emit_norm(xt[b], "x%d" % b, ht, "ht", fs, rstd, l, s, "f")

        def up(t):
            for f in range(NF):
                pa = ps[1 + f % 2]
                pb = ps[3 + f % 2]
                ra = "ps%d" % (1 + f % 2)
                rb = "ps%d" % (3 + f % 2)
                for kc in range(8):
                    pe(lambda e, f=f, kc=kc, pa=pa: e.matmul(pa[:, :], wup[:, kc, f * 128:(f + 1) * 128], ht[:, kc, :], start=(kc == 0), stop=(kc == 7)),
                       ["wup0_%d" % (f // 4), "ht"], [ra])
                for kc in range(8):
                    pe(lambda e, f=f, kc=kc, pb=pb: e.matmul(pb[:, :], wup[:, kc, DFF + f * 128:DFF + (f + 1) * 128], ht[:, kc, :], start=(kc == 0), stop=(kc == 7)),
                       ["wup1_%d" % (f // 4), "ht"], [rb])
                k = f % 2
                act(lambda e, pa=pa, k=k: e.activation(fs[k][:, :], pa[:, :], AF.Sigmoid), [ra], ["fs%d" % k])
                dve(lambda e, pa=pa, k=k: e.tensor_tensor(fs[k][:, :], fs[k][:, :], pa[:, :], ALU.mult), ["fs%d" % k, ra], ["fs%d" % k])
                dve(lambda e, pb=pb, k=k, f=f: e.tensor_tensor(gt[:, f, :], fs[k][:, :], pb[:, :], ALU.mult), ["fs%d" % k, rb], ["gt%d" % f])

        def down(t):
            b = t % 2
            for dc in range(8):
                py = ps[5 + dc % 2]
                ry = "ps%d" % (5 + dc % 2)
                for f in range(NF):
                    pe(lambda e, f=f, dc=dc, py=py: e.matmul(py[:, :], wdn[:, f, dc * 128:(dc + 1) * 128], gt[:, f, :], start=(f == 0), stop=(f == NF - 1)),
                       ["wdn%d" % (f // 2 * 2), "gt%d" % f], [ry])
                dve(lambda e, dc=dc, py=py, b=b: e.scalar_tensor_tensor(xt[b][:, dc, :], py[:, :], GCO[:, go + dc:go + dc + 1], xt[b][:, dc, :], ALU.mult, ALU.add),
                    [ry, "x%d" % b], ["x%d" % b])
            dma_sp(lambda e, t=t, b=b: e.dma_start(out=y_view[:, :, t * T:(t + 1) * T], in_=xt[b][:, :, :]), ["x%d" % b], [])

        load_x(0)
        norm(0)
        for t in range(NT):
            if t + 1 < NT:
                load_x(t + 1)
            up(t)
            if t + 1 < NT:
                norm(t + 1)
            down(t)
        P.barrier()

    def phase_m1(l):
        mem.reset()
        win = mem.alloc([128, 8, 8192], BF16)
        xt = mem.alloc([128, 8, T], F32)
        ht = [mem.alloc([128, 8, T], BF16) for _ in range(2)]
        stg = [mem.alloc([128, 2048], BF16) for _ in range(6)]
        fs = [mem.alloc([128, T], F32) for _ in range(3)]
        rstd = mem.alloc([128, T], F32)
        qdec = mem.alloc([128, 4, T], F32)
        wv = w_in[l].rearrange("(kc p) n -> p kc n", p=128)
        for blk in (0, 1, 2, 3, 10, 11, 4, 5, 6, 7, 8, 9, 12, 13, 14, 15):
            dma_pool(lambda e, blk=blk: e.dma_start(out=win[:, :, blk * 512:(blk + 1) * 512], in_=wv[:, :, blk * 512:(blk + 1) * 512]), [], ["win_%d" % blk])
        dma_sp(lambda e: e.dma_start(out=qdec[:, :, :], in_=qdecd[:, :, :]), [], ["qdec"])
        st = {"bank": 0, "slot": 0, "ev": 0}

        def nbank():
            b = 1 + st["bank"] % 6
            st["bank"] += 1
            return ps[b], "ps%d" % b

        def nslot():
            k = st["slot"] % 6
            st["slot"] += 1
            return stg[k], "stg%d" % k

        def load_x(t):
            dma_sp(lambda e, t=t: e.dma_start(out=xt[:, :, :], in_=y_view[:, :, t * T:(t + 1) * T]), [], ["x"])

        def norm(t):
            emit_norm(xt, "x", ht[t % 2], "ht%d" % (t % 2), fs, rstd, l, 1, "m")

        def fm_group(t, col, evac):
            h_ = ht[t % 2]
            hr = "ht%d" % (t % 2)
            pb, rb = nbank()
            hh = col // 4096
            for kc in range(8):
                pe(lambda e, kc=kc, pb=pb, col=col, h_=h_: e.matmul(pb[:, :], win[:, kc, col:col + 128], h_[:, kc, :], start=(kc == 0), stop=(kc == 7)),
                   ["win_%d" % (col // 512), hr], [rb])
            evac(pb, rb)

        def tm_group(t, j, col, evac):
            h_ = ht[t % 2]
            hr = "ht%d" % (t % 2)
            pb, rb = nbank()
            hh = col // 4096
            for kc in range(8):
                pe(lambda e, kc=kc, pb=pb, col=col, h_=h_, j=j: e.matmul(pb[:, :], h_[:, kc, j * 128:(j + 1) * 128], win[:, kc, col:col + 512], start=(kc == 0), stop=(kc == 7)),
                   ["win_%d" % (col // 512), hr], [rb])
            evac(pb, rb)

        def proj_a(t):
            t0 = t * T
            sq_, rq_ = nslot()
            sqd, rqd_ = nslot()
            for h in range(4):
                def ev(pb, rb, h=h):
                    act(lambda e: e.activation(sq_[:, h * T:(h + 1) * T], pb[:, :], AF.Identity), [rb], [rq_ + "a"])
                    dve(lambda e: e.tensor_tensor(sqd[:, h * T:(h + 1) * T], pb[:, :], qdec[:, h, :], ALU.mult), [rb, "qdec"], [rqd_ + "d"])
                fm_group(t, h * 128, ev)
            if DBG2 >= 2:
                dma_sp(lambda e: e.dma_start(out=rqT[:, :, t0:t0 + T].rearrange("h d s -> d h s"), in_=sq_[:, :].rearrange("p (h s) -> p h s", h=4)), [rq_ + "a", rq_ + "d"], [])
                dma_sp(lambda e: e.dma_start(out=rqdT[:, :, t0:t0 + T].rearrange("h d s -> d h s"), in_=sqd[:, :].rearrange("p (h s) -> p h s", h=4)), [rqd_ + "a", rqd_ + "d"], [])
            if DBG2 <= 2:
                return None
            sk, rk_ = nslot()
            for h in range(4):
                def ev(pb, rb, h=h):
                    act(lambda e: e.activation(sk[:, h * T:(h + 1) * T], pb[:, :], AF.Identity, scale=float(128.0 ** -0.5)), [rb], [rk_ + "a"])
                fm_group(t, 512 + h * 128, ev)
            dma_sp(lambda e: e.dma_start(out=rkT[:, :, t0:t0 + T].rearrange("h d s -> d h s"), in_=sk[:, :].rearrange("p (h s) -> p h s", h=4)), [rk_ + "a", rk_ + "d"], [])

            def chunked(colbase, dst, mode):
                dview = dst.rearrange("(c p) s -> p c s", p=128)
                for c0 in range(0, 8, 4):
                    sl, rs = nslot()
                    for cc in range(4):
                        c = c0 + cc

                        def ev(pb, rb, cc=cc):
                            o = sl[:, cc * T:(cc + 1) * T]
                            if mode == "silu":
                                k = st["ev"] % 3
                                st["ev"] += 1
                                act(lambda e: e.activation(fs[k][:, :], pb[:, :], AF.Sigmoid), [rb], ["fs%d" % k])
                                dve(lambda e: e.tensor_tensor(o, fs[k][:, :], pb[:, :], ALU.mult), [rb, "fs%d" % k], [rs + "d"])
                            elif mode == "sig":
                                act(lambda e: e.activation(o, pb[:, :], AF.Sigmoid), [rb], [rs + "a"])
                            elif mode == "q":
                                dve(lambda e: e.tensor_scalar_mul(o, pb[:, :], 0.125), [rb], [rs + "d"])
                            else:
                                dve(lambda e: e.tensor_copy(o, pb[:, :]), [rb], [rs + "d"])
                        fm_group(t, colbase + c * 128, ev)
                    dma_sp(lambda e, sl=sl, c0=c0: e.dma_start(out=dview[:, c0:c0 + 4, t0:t0 + T], in_=sl[:, :].rearrange("p (c s) -> p c s", c=4)), [rs + "a", rs + "d"], [])
            return chunked

        def proj_a2(t, chunked):
            chunked(2048, rgT, "silu")
            chunked(3072, dqT.rearrange("h r s -> (h r) s"), "q")
            chunked(4096, dkT.rearrange("h r s -> (h r) s"), "k")
            chunked(6144, sgrT, "sig")
            chunked(7168, sgdT, "sig")

        def proj_b(t):
            kt0 = t * 4
            skd, rkd_ = nslot()
            for j in range(4):
                def ev(pb, rb, j=j):
                    for h in range(4):
                        dve(lambda e, h=h: e.tensor_scalar_mul(skd[:, j * 512 + h * 128:j * 512 + (h + 1) * 128], pb[:, h * 128:(h + 1) * 128], kdec[:, h * 4 + j:h * 4 + j + 1]),
                            [rb, "kdec"], [rkd_ + "d"])
                tm_group(t, j, 512, ev)
            dma_sp(lambda e: e.dma_start(out=rkd[:, kt0:kt0 + 4, :], in_=skd[:, :].rearrange("p (j n) -> p j n", j=4)), [rkd_ + "a", rkd_ + "d"], [])
            for colbase, dst in ((1024, rv), (5120, dvs)):
                for jp in range(2):
                    sl, rs = nslot()
                    for jj in range(2):
                        j = jp * 2 + jj
                        for half in range(2):
                            def ev(pb, rb, jj=jj, half=half, sl=sl, rs=rs):
                                o = sl[:, jj * 1024 + half * 512:jj * 1024 + (half + 1) * 512]
                                if half == 0:
                                    act(lambda e: e.activation(o, pb[:, :], AF.Identity), [rb], [rs + "a"])
                                else:
                                    dve(lambda e: e.tensor_copy(o, pb[:, :]), [rb], [rs + "d"])
                            tm_group(t, j, colbase + half * 512, ev)
                    dma_sp(lambda e, sl=sl, jp=jp, dst=dst: e.dma_start(out=dst[:, kt0 + jp * 2:kt0 + jp * 2 + 2, :], in_=sl[:, :].rearrange("p (j n) -> p j n", j=2)), [rs + "a", rs + "d"], [])

        load_x(0)
        norm(0)
        for t in range(NT):
            if t + 1 < NT:
                load_x(t + 1)
            if DBG >= 2:
                ch = proj_a(t)
            if DBG >= 3:
                proj_b(t)
            if t + 1 < NT:
                norm(t + 1)
            if DBG >= 4:
                proj_a2(t, ch)
        P.barrier()

    def phase_m2d(l):
        mem.reset()
        KT = [[mem.alloc([128, S], BF16) for _ in range(2)] for _ in range(2)]
        VV = [mem.alloc([128, NKT, 128], BF16) for _ in range(2)]
        QT = [mem.alloc([128, 2, T], BF16) for _ in range(2)]
        PT = [mem.alloc([128, T], BF16) for _ in range(4)]
        C0 = mem.alloc([128, 4, T], F32)
        sfix = [mem.alloc([128, T], F32) for _ in range(2)]
        EP = [[mem.alloc([128, T], F32) for _ in range(2)] for _ in range(7)]
        ydst = [mem.alloc([128, T], BF16) for _ in range(2)]
        dma_sp(lambda e: e.dma_start(out=C0[:, :, :], in_=c0d[:, :, :]), [], ["C0"])
        cnt = {"i": 0, "q": 0, "o": 0}
        slopes = [2.0 ** (-(h + 1)) for h in range(8)]

        def kres(b):
            return ["K%d_%d%s" % (b, s, x) for s in range(2) for x in "ra"]

        def qres(qb):
            return ["Q%d_%d%s" % (qb, s, x) for s in range(2) for x in "ra"]

        def load_head(h):
            b = h % 2
            for s in range(2):
                dma_sp(lambda e, s=s: e.dma_start(out=KT[b][s][0:64, :], in_=dkT[h, s * 64:(s + 1) * 64, :]), [], ["K%d_%dr" % (b, s)])
                dma_pool(lambda e, s=s: e.dma_start(out=KT[b][s][64:68, :], in_=kaug[h, :, :]), [], ["K%d_%da" % (b, s)])
            dma_sp(lambda e: e.dma_start(out=VV[b][:, :, :], in_=dvs[:, :, h * 128:(h + 1) * 128]), [], ["V%d" % b])

        def load_q(h, qi):
            qb = cnt["q"] % 2
            cnt["q"] += 1
            q0 = qi * T
            for s in range(2):
                dma_sp(lambda e, s=s: e.dma_start(out=QT[qb][0:64, s, :], in_=dqT[h, s * 64:(s + 1) * 64, q0:q0 + T]), [], ["Q%d_%dr" % (qb, s)])
                dma_pool(lambda e, s=s: e.dma_start(out=QT[qb][64:68, s, :], in_=qaug[:, q0:q0 + T]), [], ["Q%d_%da" % (qb, s)])
            return qb

        def do_tile(h, qi, b, qb):
            nk = 4 * qi + 4
            lim = qi * T - 127 - SKIP_T / slopes[h]
            kt_lo = max(0, int(math.floor(lim / 128.0)) + 1) if lim >= 0 else 0
            steps = [(kt, s) for kt in range(kt_lo, nk) for s in range(2)]

            def c0_of(kt):
                m_ = kt - 4 * qi
                return 128 * m_ if m_ > 0 else 0
            base = cnt["i"]
            kr = kres(b)
            qr = qres(qb)
            slope = float(slopes[h])

            def s_mm(i):
                kt, s = steps[i]
                g = base + i
                pb = ps[1 + g % 3]
                c0 = c0_of(kt)
                pe(lambda e: e.matmul(pb[:, c0:T], KT[b][s][0:68, kt * 128:(kt + 1) * 128], QT[qb][0:68, s, c0:T], start=True, stop=True),
                   kr + qr, ["ps%d" % (1 + g % 3)])

            def step(i):
                kt, s = steps[i]
                g = base + i
                pb = ps[1 + g % 3]
                rb = "ps%d" % (1 + g % 3)
                pt = PT[g % 4]
                rp = "PT%d" % (g % 4)
                m = kt - 4 * qi
                c0 = c0_of(kt)
                if m >= 0:
                    sf = sfix[g % 2]
                    rsf = "sfix%d" % (g % 2)
                    dve(lambda e: e.scalar_tensor_tensor(sf[:, c0:T], C0[:, m, c0:T], slope, pb[:, c0:T], ALU.mult, ALU.add), ["C0", rb], [rsf])
                    act(lambda e: e.activation(pt[:, c0:T], sf[:, c0:T], AF.Exp), [rsf], [rp])
                else:
                    act(lambda e: e.activation(pt[:, c0:T], pb[:, c0:T], AF.Exp), [rb], [rp])
                po = ps[4 + s]
                pz = ps[6 + s]
                pe(lambda e: e.matmul(po[:, c0:T], VV[b][:, kt, :], pt[:, c0:T], start=(kt == kt_lo), stop=(kt == nk - 1)),
                   ["V%d" % b, rp], ["ps%d" % (4 + s)])
                pe(lambda e: e.matmul(pz[:, c0:T], ones_b[:, :], pt[:, c0:T], start=(kt == kt_lo), stop=(kt == nk - 1)),
                   [rp], ["ps%d" % (6 + s)])

            s_mm(0)
            s_mm(1)
            for i in range(len(steps)):
                if i + 2 < len(steps):
                    s_mm(i + 2)
                step(i)
            cnt["i"] += len(steps)
            ob = cnt["o"] % 2
            cnt["o"] += 1
            R0_, R1_, oo_, t1_, sq_, ln_, rs_ = [x[ob] for x in EP]
            sfx = "_%d" % ob
            dve(lambda e: e.tensor_copy(oo_[:, :], ps[4][:, :]), ["ps4"], ["oo" + sfx])
            dve(lambda e: e.tensor_copy(t1_[:, :], ps[5][:, :]), ["ps5"], ["t1" + sfx])
            act(lambda e: e.activation(ln_[:, :], ps[6][:, :], AF.Ln), ["ps6"], ["lnv" + sfx])
            act(lambda e: e.activation(rs_[:, :], ps[7][:, :], AF.Ln), ["ps7"], ["rstd" + sfx])
            act(lambda e: e.activation(R0_[:, :], ln_[:, :], AF.Exp, scale=-1.0), ["lnv" + sfx], ["R0" + sfx])
            act(lambda e: e.activation(R1_[:, :], rs_[:, :], AF.Exp, scale=-1.0), ["rstd" + sfx], ["R1" + sfx])
            dve(lambda e: e.tensor_tensor(oo_[:, :], oo_[:, :], R0_[:, :], ALU.mult), ["oo" + sfx, "R0" + sfx], ["oo" + sfx])
            dve(lambda e: e.tensor_tensor(t1_[:, :], t1_[:, :], R1_[:, :], ALU.mult), ["t1" + sfx, "R1" + sfx], ["t1" + sfx])
            dve(lambda e: e.scalar_tensor_tensor(oo_[:, :], t1_[:, :], neglam[:, l:l + 1], oo_[:, :], ALU.mult, ALU.add), ["t1" + sfx, "oo" + sfx], ["oo" + sfx])
            pool(lambda e: e.tensor_tensor(sq_[:, :], oo_[:, :], oo_[:, :], ALU.mult), ["oo" + sfx], ["sqo" + sfx])
            pe(lambda e: e.matmul(ps[0][:, :], ones_f[:, :], sq_[:, :], start=True, stop=True), ["sqo" + sfx], ["ps0"])
            act(lambda e: e.activation(ln_[:, :], ps[0][:, :], AF.Ln, bias=epsc[:, 1:2], scale=1.0), ["ps0"], ["lnv" + sfx])
            act(lambda e: e.activation(rs_[:, :], ln_[:, :], AF.Exp, scale=-0.5), ["lnv" + sfx], ["rstd" + sfx])
            dve(lambda e: e.scalar_tensor_tensor(ydst[ob][:, :], oo_[:, :], cdiff[:, l:l + 1], rs_[:, :], ALU.mult, ALU.mult), ["oo" + sfx, "rstd" + sfx], ["yd%d" % ob])
            q0 = qi * T
            dma_pool(lambda e: e.dma_start(out=ydT[h * 128:(h + 1) * 128, q0:q0 + T], in_=ydst[ob][:, :]), ["yd%d" % ob], [])

        seq = [(h, qi) for h in range(8) for qi in range(NT)]
        load_head(0)
        qb_next = load_q(0, 0)
        for idx, (h, qi) in enumerate(seq):
            if qi == 0 and h + 1 < 8:
                load_head(h + 1)
            qb = qb_next
            if idx + 1 < len(seq):
                qb_next = load_q(*seq[idx + 1])
            do_tile(h, qi, h % 2, qb)
        P.barrier()

    def phase_m2r(l):
        mem.reset()
        NB = S // T
        Dt = mem.alloc([128, 16, T], F32)
        qb_ = [mem.alloc([128, 4, T], BF16) for _ in range(2)]
        qdb = [mem.alloc([128, 4, T], BF16) for _ in range(2)]
        kb = [mem.alloc([128, 4, T], BF16) for _ in range(2)]
        kdb = [mem.alloc([128, 4, 512], BF16) for _ in range(2)]
        vb = [mem.alloc([128, 4, 1024], BF16) for _ in range(2)]
        rgb = [mem.alloc([128, 8, T], BF16) for _ in range(2)]
        Pm = [mem.alloc([128, 4, T], BF16) for _ in range(2)]
        st32 = mem.alloc([128, 4, 256], F32)
        stb = mem.alloc([128, 4, 256], BF16)
        y32 = [mem.alloc([128, 2, T], F32) for _ in range(2)]
        sqv = mem.alloc([128, 2, T], F32)
        rstd = mem.alloc([128, T], F32)
        tmp = mem.alloc([128, T], F32)
        ost = [mem.alloc([128, 8, T], BF16) for _ in range(2)]
        dma_sp(lambda e: e.dma_start(out=Dt[:, :, :], in_=dtd.rearrange("p h m s -> p (h m) s")), [], ["Dt"])
        gam = [1.0 - 2.0 ** (-5.0 - h) for h in range(4)]
        g512 = [float(np.float32(np.exp(np.float64(T) * np.log(np.float64(np.float32(g)))))) for g in gam]
        yr_view = yrT.rearrange("(c p) s -> p c s", p=128)
        rg_view = rgT.rearrange("(c p) s -> p c s", p=128)

        def load_blk(bi):
            b = bi % 2
            t0 = bi * T
            dma_sp(lambda e: e.dma_start(out=qb_[b][:, :, :], in_=rqT[:, :, t0:t0 + T].rearrange("h d s -> d h s")), [], ["q%d" % b])
            dma_sp(lambda e: e.dma_start(out=qdb[b][:, :, :], in_=rqdT[:, :, t0:t0 + T].rearrange("h d s -> d h s")), [], ["qd%d" % b])
            dma_sp(lambda e: e.dma_start(out=kb[b][:, :, :], in_=rkT[:, :, t0:t0 + T].rearrange("h d s -> d h s")), [], ["k%d" % b])
            dma_sp(lambda e: e.dma_start(out=kdb[b][:, :, :], in_=rkd[:, bi * 4:bi * 4 + 4, :]), [], ["kd%d" % b])
            dma_sp(lambda e: e.dma_start(out=vb[b][:, :, :], in_=rv[:, bi * 4:bi * 4 + 4, :]), [], ["v%d" % b])
            dma_sp(lambda e: e.dma_start(out=rgb[b][:, :, :], in_=rg_view[:, :, t0:t0 + T]), [], ["rg%d" % b])

        def do_head(bi, b, h, gi):
            pmb = Pm[gi % 2]
            rpm = "Pm%d" % (gi % 2)
            yb = y32[gi % 2]
            ryb = "y32_%d" % (gi % 2)
            osb = ost[b]
            ros = "ost%d" % b

            def smm(m):
                pb = ps[1 + m % 2]
                rb = "ps%d" % (1 + m % 2)
                pe(lambda e: e.matmul(pb[:, :], kb[b][:, h, m * 128:(m + 1) * 128], qb_[b][:, h, :], start=True, stop=True), ["k%d" % b, "q%d" % b], [rb])
                dve(lambda e: e.tensor_tensor(pmb[:, m, :], pb[:, :], Dt[:, h * 4 + m, :], ALU.mult), [rb, "Dt"], [rpm + "_%d" % m])
            for m in range(4):
                smm(m)

            def ymm(vc):
                py = ps[3 + vc]
                ry = "ps%d" % (3 + vc)
                for m in range(4):
                    pe(lambda e, m=m: e.matmul(py[:, :], vb[b][:, m, h * 256 + vc * 128:h * 256 + (vc + 1) * 128], pmb[:, m, :], start=(m == 0), stop=(m == 3 and bi == 0)),
                       ["v%d" % b, rpm + "_%d" % m], [ry])
                if bi > 0:
                    pe(lambda e: e.matmul(py[:, :], stb[:, h, vc * 128:(vc + 1) * 128], qdb[b][:, h, :], start=False, stop=True), ["stb%d" % h, "qd%d" % b], [ry])
            for vc in range(2):
                ymm(vc)
            for m in range(4):
                pe(lambda e, m=m: e.matmul(ps[5][:, 0:256], kdb[b][:, m, h * 128:(h + 1) * 128], vb[b][:, m, h * 256:(h + 1) * 256], start=(m == 0), stop=(m == 3)),
                   ["kd%d" % b, "v%d" % b], ["ps5"])
            if bi == 0:
                dve(lambda e: e.tensor_copy(st32[:, h, :], ps[5][:, 0:256]), ["ps5"], ["st32_%d" % h])
            else:
                dve(lambda e: e.scalar_tensor_tensor(st32[:, h, :], st32[:, h, :], float(g512[h]), ps[5][:, 0:256], ALU.mult, ALU.add), ["ps5", "st32_%d" % h], ["st32_%d" % h])
            if bi + 1 < NB:
                pool(lambda e: e.tensor_copy(stb[:, h, :], st32[:, h, :]), ["st32_%d" % h], ["stb%d" % h])

            def gn(vc):
                py = ps[3 + vc]
                ry = "ps%d" % (3 + vc)
                act(lambda e: e.activation(sqv[:, vc, :], py[:, :], AF.Square), [ry], ["sqv%d" % vc])
                act(lambda e: e.activation(yb[:, vc, :], py[:, :], AF.Identity), [ry], [ryb + "_%d" % vc])
                pe(lambda e: e.matmul(ps[0][:, :], ones_f[:, :], sqv[:, vc, :], start=(vc == 0), stop=(vc == 1)), ["sqv%d" % vc], ["ps0"])
            for vc in range(2):
                gn(vc)
            act(lambda e: e.activation(rstd[:, :], ps[0][:, :], AF.Ln, bias=epsc[:, 2:3], scale=1.0), ["ps0"], ["rstd"])
            act(lambda e: e.activation(rstd[:, :], rstd[:, :], AF.Exp, scale=-0.5), ["rstd"], ["rstd"])

            def fin(vc):
                c = h * 2 + vc
                dve(lambda e: e.scalar_tensor_tensor(tmp[:, :], yb[:, vc, :], cret[:, l * 8 + c:l * 8 + c + 1], rstd[:, :], ALU.mult, ALU.mult), [ryb + "_%d" % vc, "rstd"], ["tmp"])
                pool(lambda e: e.tensor_tensor(osb[:, c, :], tmp[:, :], rgb[b][:, c, :], ALU.mult), ["tmp", "rg%d" % b], [ros])
            for vc in range(2):
                fin(vc)

        def store_blk(bi, b):
            t0 = bi * T
            dma_pool(lambda e: e.dma_start(out=yr_view[:, :, t0:t0 + T], in_=ost[b][:, :, :]), ["ost%d" % b], [])

        load_blk(0)
        gi = 0
        for bi in range(NB):
            if bi + 1 < NB:
                load_blk(bi + 1)
            for h in range(4):
                do_head(bi, bi % 2, h, gi)
                gi += 1
            store_blk(bi, bi % 2)
        P.barrier()

    def phase_m3(l):
        mem.reset()
        wr = mem.alloc([128, 8, D], BF16); wd = mem.alloc([128, 8, D], BF16); wo = mem.alloc([128, 8, D], BF16)
        xt = [mem.alloc([128, 8, T], F32) for _ in range(2)]
        yr = [mem.alloc([128, 8, T], BF16) for _ in range(2)]
        yd = [mem.alloc([128, 8, T], BF16) for _ in range(2)]
        sr = [mem.alloc([128, 8, T], BF16) for _ in range(2)]
        sd = [mem.alloc([128, 8, T], BF16) for _ in range(2)]
        mg = mem.alloc([128, 8, T], BF16)
        ta = [mem.alloc([128, T], F32) for _ in range(2)]
        tb = [mem.alloc([128, T], F32) for _ in range(2)]

        def loadw(w_s, w_d, nm):
            v = w_d[l].rearrange("(kc p) n -> p kc n", p=128)
            for hh in range(2):
                dma_pool(lambda e, hh=hh: e.dma_start(out=w_s[:, hh * 4:(hh + 1) * 4, :], in_=v[:, hh * 4:(hh + 1) * 4, :]), [], ["%s%d" % (nm, hh)])
        loadw(wr, w_rb, "wr"); loadw(wd, w_db, "wd"); loadw(wo, w_out, "wo")
        views = [t_.rearrange("(c p) s -> p c s", p=128) for t_ in (yrT, ydT, sgrT, sgdT)]
        go = (l * 3 + 1) * 8

        def load(t):
            b = t % 2
            t0 = t * T
            dma_sp(lambda e: e.dma_start(out=xt[b][:, :, :], in_=y_view[:, :, t0:t0 + T]), [], ["x%d" % b])
            for buf, v, nm in ((yr, views[0], "yr"), (yd, views[1], "yd"), (sr, views[2], "sr"), (sd, views[3], "sd")):
                dma_sp(lambda e, buf=buf, v=v: e.dma_start(out=buf[b][:, :, :], in_=v[:, :, t0:t0 + T]), [], ["%s%d" % (nm, b)])

        def do_tile(t, b):
            def branch(dc):
                pr = ps[1 + dc % 2]; rr = "ps%d" % (1 + dc % 2)
                pd = ps[3 + dc % 2]; rd = "ps%d" % (3 + dc % 2)
                for kc in range(8):
                    pe(lambda e, kc=kc: e.matmul(pr[:, :], wr[:, kc, dc * 128:(dc + 1) * 128], yr[b][:, kc, :], start=(kc == 0), stop=(kc == 7)), ["wr%d" % (kc // 4), "yr%d" % b], [rr])
                for kc in range(8):
                    pe(lambda e, kc=kc: e.matmul(pd[:, :], wd[:, kc, dc * 128:(dc + 1) * 128], yd[b][:, kc, :], start=(kc == 0), stop=(kc == 7)), ["wd%d" % (kc // 4), "yd%d" % b], [rd])
                k = dc % 2
                dve(lambda e: e.tensor_tensor(ta[k][:, :], pr[:, :], sr[b][:, dc, :], ALU.mult), [rr, "sr%d" % b], ["ta%d" % k])
                dve(lambda e: e.tensor_tensor(tb[k][:, :], pd[:, :], sd[b][:, dc, :], ALU.mult), [rd, "sd%d" % b], ["tb%d" % k])
                pool(lambda e: e.tensor_tensor(mg[:, dc, :], ta[k][:, :], tb[k][:, :], ALU.add), ["ta%d" % k, "tb%d" % k], ["mg%d" % dc])
            for dc in range(8):
                branch(dc)

            def outp(dc):
                po = ps[5 + dc % 2]; ro = "ps%d" % (5 + dc % 2)
                for kc in range(8):
                    pe(lambda e, kc=kc: e.matmul(po[:, :], wo[:, kc, dc * 128:(dc + 1) * 128], mg[:, kc, :], start=(kc == 0), stop=(kc == 7)), ["wo%d" % (kc // 4), "mg%d" % kc], [ro])
                dve(lambda e: e.scalar_tensor_tensor(xt[b][:, dc, :], po[:, :], GCO[:, go + dc:go + dc + 1], xt[b][:, dc, :], ALU.mult, ALU.add), [ro, "x%d" % b], ["x%d" % b])
            for dc in range(8):
                outp(dc)
            t0 = t * T
            dma_sp(lambda e: e.dma_start(out=y_view[:, :, t0:t0 + T], in_=xt[b][:, :, :]), ["x%d" % b], [])

        load(0)
        for t in range(NT):
            if t + 1 < NT:
                load(t + 1)
            do_tile(t, t % 2)
        P.barrier()

    def phase_fin():
        mem.reset()
        xt = [mem.alloc([128, 8, T], F32) for _ in range(2)]
        fs = [mem.alloc([128, T], F32) for _ in range(3)]
        rstd = mem.alloc([128, T], F32)

        def load(t):
            b = t % 2
            dma_sp(lambda e: e.dma_start(out=xt[b][:, :, :], in_=y_view[:, :, t * T:(t + 1) * T]), [], ["x%d" % b])

        def do_tile(t, b):
            def st(c):
                k = c % 3
                act(lambda e: e.activation(fs[k][:, :], xt[b][:, c, :], AF.Square), ["x%d" % b], ["fs%d" % k])
                pe(lambda e: e.matmul(ps[0][:, :], ones_f[:, :], fs[k][:, :], start=(c == 0), stop=(c == 7)), ["fs%d" % k], ["ps0"])
            for c in range(8):
                st(c)
            act(lambda e: e.activation(rstd[:, :], ps[0][:, :], AF.Ln, bias=epsc[:, 0:1], scale=1.0), ["ps0"], ["rstd"])
            act(lambda e: e.activation(rstd[:, :], rstd[:, :], AF.Exp, scale=-0.5), ["rstd"], ["rstd"])

            def sc(c):
                dve(lambda e: e.scalar_tensor_tensor(xt[b][:, c, :], xt[b][:, c, :], fnc[:, c:c + 1], rstd[:, :], ALU.mult, ALU.mult), ["x%d" % b, "rstd"], ["x%d" % b])
            for c in range(8):
                sc(c)
            dma_sp(lambda e: e.dma_start(out=y_view[:, :, t * T:(t + 1) * T], in_=xt[b][:, :, :]), ["x%d" % b], [])

        load(0)
        for t in range(NT):
            if t + 1 < NT:
                load(t + 1)
            do_tile(t, t % 2)
        P.barrier()

    plist = [phase_pre]
    for l in range(depth):
        plist.append(lambda l=l: phase_ffn(l, 0, 0, x_view if l == 0 else y_view))
        plist.append(lambda l=l: phase_m1(l))
        plist.append(lambda l=l: phase_m2d(l))
        plist.append(lambda l=l: phase_m2r(l))
        plist.append(lambda l=l: phase_m3(l))
        plist.append(lambda l=l: phase_ffn(l, 1, 2, y_view))
    plist.append(phase_fin)
    for pf in plist[:upto]:
        pf()
    P.finalize()

    from contextlib import ExitStack
    sems = {}
    with ExitStack() as es:
        for k in ("pe", "act", "dve", "pool", "bar"):
            sems[k] = es.enter_context(nc.semaphore("s_" + k))
        for q in ("sp", "pool"):
            for i in range(KDMA):
                sems["%s_d%d" % (q, i)] = es.enter_context(nc.semaphore("s_%s_d%d" % (q, i)))
        with nc.Block() as block:
            @block.tensor
            def _(e):
                P.run_stream("pe", e, sems)

            @block.scalar
            def _(e):
                P.run_stream("act", e, sems)

            @block.vector
            def _(e):
                P.run_stream("dve", e, sems)

            @block.gpsimd
            def _(e):
                P.run_stream("pool", e, sems)

            @block.sync
            def _(e):
                P.run_stream("sp", e, sems)
    return nc


def make_consts(S):
    pos = np.arange(S)
    a = (pos // 128).astype(np.float64)
    b_ = (pos % 128).astype(np.float64)
    qaug = np.stack([-128.0 * a, -b_, np.ones(S), np.ones(S)]).astype(np.float32)
    kaug = np.zeros((8, 4, S), np.float32)
    for h in range(8):
        sl = 2.0 ** (-(h + 1))
        kaug[h, 0] = sl
        kaug[h, 1] = sl
        kaug[h, 2] = sl * 128.0 * a
        kaug[h, 3] = sl * b_
    i = np.arange(128)[:, None, None]
    m = np.arange(4)[None, :, None]
    j = np.arange(512)[None, None, :]
    kp = 128 * m + i
    allowed = (kp // 64) <= (j // 64)
    c0 = np.where(allowed, np.where(kp > j, -2.0 * (kp - j), 0.0), -1.0e6).astype(np.float32)
    dtd = np.zeros((128, 4, 4, 512), np.float32)
    qdec = np.zeros((128, 4, 512), np.float32)
    kdec = np.zeros((128, 16), np.float32)
    for h in range(4):
        g = np.float64(np.float32(1.0 - 2.0 ** (-5.0 - h)))
        lg = np.log(g)
        dd = np.where(allowed, np.exp(lg * np.abs(j - kp)), 0.0)
        dtd[:, h, :, :] = dd.astype(np.float32)
        qdec[:, h, :] = np.exp(lg * np.arange(512))[None, :].astype(np.float32)
        for jj in range(4):
            r = 128 * jj + np.arange(128)
            kdec[:, h * 4 + jj] = (np.exp(lg * (512 - r)) * (128.0 ** -0.5)).astype(np.float32)
    return dict(qaug=qaug, kaug=kaug, c0d=c0, dtd=dtd, qdecd=qdec, kdecd=kdec)


_CACHE = {}


def run(inputs, S, depth, n_cores, upto=99):
    key = (S, depth, upto)
    if key not in _CACHE:
        _CACHE[key] = build_program(S, depth, upto)
    nc = _CACHE[key]
    f = lambda a: np.ascontiguousarray(np.asarray(a, dtype=np.float32))
    x = f(inputs["x"]); c = f(inputs["c"])
    shared = dict(
        w_ada=f(inputs["w_ada"]),
        b_adaT=f(f(inputs["b_ada"]).reshape(depth, 72, 128).transpose(2, 0, 1).reshape(128, depth * 72)),
        norm_wT=f(f(inputs["norm_w"]).reshape(depth, 3, 8, 128).transpose(3, 0, 1, 2).reshape(128, depth * 24)),
        w_up=f(inputs["w_ffn_up"]), w_down=f(inputs["w_ffn_down"]), w_in=f(inputs["w_in"]),
        ret_gnT=f(f(inputs["ret_gn"]).reshape(depth, 8, 128).transpose(2, 0, 1).reshape(128, depth * 8)),
        lam_in=f(np.broadcast_to(np.stack([f(inputs["lambda_q1"]), f(inputs["lambda_k1"]), f(inputs["lambda_q2"]), f(inputs["lambda_k2"])]).reshape(1, -1), (128, 4 * depth * 64))),
        sublnT=f(f(inputs["diff_subln"]).T),
        w_rb=f(inputs["w_ret_branch"]), w_db=f(inputs["w_diff_branch"]), w_out=f(inputs["w_out"]),
        fnT=f(f(inputs["final_norm"]).reshape(8, 128).T),
    )
    shared.update(make_consts(S))
    in_maps = []
    for b in range(n_cores):
        m = dict(shared)
        m["xT"] = f(x[b].T)
        m["cT"] = f(c[b].reshape(8, 128).T)
        in_maps.append(m)
    res = run_bass_kernel_spmd(nc, in_maps, core_ids=list(range(n_cores)))
    out = np.stack([np.ascontiguousarray(np.asarray(r["yT"]).T) for r in res.results])
    return out.astype(np.float32)


def kernel(**inputs):
    return run(inputs, SEQ, NLAYERS, 8)
```

```python
import math
import numpy as np
import concourse.bass as bass
import concourse.mybir as mybir
from concourse.bass_utils import run_bass_kernel_spmd

F32 = mybir.dt.float32
BF16 = mybir.dt.bfloat16
AF = mybir.ActivationFunctionType
ALU = mybir.AluOpType
AX = mybir.AxisListType

D = 1024
KC = 8
T = 512
DFF = 2816
NF = 22
EPS = 1e-6
KDMA = 8
SKIP_T = 128.0
NLAYERS = 4
import os
DBG = int(os.environ.get('KDBG', '9'))
DBG2 = int(os.environ.get('KDBG2', '9'))
SEQ = 8192


class Op:
    __slots__ = ("eng", "fn", "deps", "dma", "sig", "cnt", "semkey", "qn", "bar")


class Prog:
    def __init__(self):
        self.streams = {e: [] for e in ("pe", "act", "dve", "pool", "sp")}
        self.lastw = {}
        self.readers = {}
        self.nbar = 0

    def add(self, eng, fn, reads=(), writes=(), dma=False):
        op = Op()
        op.eng = eng; op.fn = fn; op.dma = dma; op.sig = False; op.bar = 0
        op.cnt = 0; op.semkey = None; op.qn = 0
        deps = {}
        for r in reads:
            w = self.lastw.get(r)
            if w is not None:
                deps[id(w)] = (w, 0)
            if r.startswith("ps"):
                rd = self.readers.get(r)
                if rd:
                    for k, o in rd.items():
                        if k != "dma" and k != eng and id(o) not in deps:
                            deps[id(o)] = (o, 3)
        for r in writes:
            w = self.lastw.get(r)
            if w is not None and id(w) not in deps:
                deps[id(w)] = (w, 1)
            rd = self.readers.get(r)
            if rd:
                for k, o in rd.items():
                    if k == "dma":
                        for oo in o:
                            if id(oo) not in deps:
                                deps[id(oo)] = (oo, 2)
                    elif id(o) not in deps:
                        deps[id(o)] = (o, 2)
        dl = []
        for w, kind in deps.values():
            if (not w.dma) and (not dma) and w.eng == eng:
                if eng == "pe":
                    continue
            dl.append(w)
            w.sig = True
        op.deps = dl
        for r in writes:
            self.lastw[r] = op
            self.readers[r] = {}
        for r in reads:
            rd = self.readers.setdefault(r, {})
            if dma:
                rd.setdefault("dma", []).append(op)
            else:
                rd[eng] = op
        self.streams[eng].append(op)
        return op

    def barrier(self):
        self.nbar += 1
        for e, st in self.streams.items():
            for o in reversed(st):
                if o.bar:
                    break
                if not o.dma:
                    o.sig = True
                    break
            op = Op()
            op.eng = e; op.fn = None; op.dma = False; op.sig = False; op.bar = self.nbar
            op.deps = []; op.cnt = 0; op.semkey = None; op.qn = 0
            st.append(op)
        self.lastw = {}
        self.readers = {}

    def finalize(self):
        for e, st in self.streams.items():
            c = 0
            q = 0
            for op in st:
                if op.bar:
                    continue
                if op.dma:
                    op.semkey = "%s_d%d" % (e, q % KDMA)
                    op.cnt = 16 * (q // KDMA + 1)
                    op.qn = q
                    q += 1
                elif op.sig:
                    c += 1
                    op.cnt = c
                    op.semkey = e

    def run_stream(self, ename, eng, sems):
        waited = {}

        def wait(key, val):
            if waited.get(key, 0) < val:
                eng.wait_ge(sems[key], val)
                waited[key] = val

        own = 0
        dtot = {}
        for op in self.streams[ename]:
            if op.bar:
                if ename != "sp" and own > 0:
                    wait(ename, own)
                for k, v in dtot.items():
                    wait(k, v)
                eng.sem_inc(sems["bar"], 1)
                wait("bar", 5 * op.bar)
                continue
            for d in op.deps:
                wait(d.semkey, d.cnt)
            if op.dma and op.qn >= KDMA:
                wait(op.semkey, op.cnt - 16)
            ins = op.fn(eng)
            if op.dma:
                ins.then_inc(sems[op.semkey], 16)
                dtot[op.semkey] = op.cnt
            elif op.sig:
                ins.then_inc(sems[ename], 1)
                own = op.cnt


class Mem:
    def __init__(self, nc):
        self.nc = nc
        self.off = 16576
        self.n = 0
        self.base = 16576

    def alloc(self, shape, dtype):
        nb = 1
        for s in shape[1:]:
            nb *= s
        nb *= 4 if dtype == F32 else 2
        nb = (nb + 63) // 64 * 64
        h = self.nc.alloc_sbuf_tensor_at("sb%d" % self.n, list(shape), dtype, offset=self.off)
        self.n += 1
        self.off += nb
        assert self.off <= 229376, self.off
        return h

    def set_base(self):
        self.base = self.off

    def reset(self):
        self.off = self.base


def lam_init_of(l):
    return 0.8 - 0.6 * math.exp(-0.3 * l)


def build_program(S, depth, upto=99):
    NT = S // T
    NKT = S // 128
    nc = bass.Bass("TRN2", target_bir_lowering=False)
    P = Prog()
    mem = Mem(nc)

    def din(name, shape, dt=F32):
        return nc.dram_tensor(name, list(shape), dt, kind="ExternalInput")

    xT = din("xT", [D, S])
    cT = din("cT", [128, 8])
    w_ada = din("w_ada", [depth, D, 9216])
    b_adaT = din("b_adaT", [128, depth * 72])
    norm_wT = din("norm_wT", [128, depth * 24])
    w_up = din("w_up", [depth, 2, D, 2 * DFF])
    w_down = din("w_down", [depth, 2, DFF, D])
    w_in = din("w_in", [depth, D, 8192])
    ret_gnT = din("ret_gnT", [128, depth * 8])
    lam_in = din("lam_in", [128, 4 * depth * 64])
    sublnT = din("sublnT", [128, depth])
    w_rb = din("w_rb", [depth, D, D])
    w_db = din("w_db", [depth, D, D])
    w_out = din("w_out", [depth, D, D])
    fnT = din("fnT", [128, 8])
    qaug = din("qaug", [4, S])
    kaug = din("kaug", [8, 4, S])
    c0d = din("c0d", [128, 4, 512])
    dtd = din("dtd", [128, 4, 4, 512])
    qdecd = din("qdecd", [128, 4, 512])
    kdecd = din("kdecd", [128, 16])
    yT = nc.dram_tensor("yT", [D, S], F32, kind="ExternalOutput")

    def dscr(name, shape):
        return nc.dram_tensor(name, list(shape), BF16, kind="Internal")

    rqT = dscr("rqT", [4, 128, S]); rqdT = dscr("rqdT", [4, 128, S]); rkT = dscr("rkT", [4, 128, S])
    rkd = dscr("rkd", [128, NKT, 512]); rv = dscr("rv", [128, NKT, 1024])
    rgT = dscr("rgT", [D, S]); dqT = dscr("dqT", [8, 128, S]); dkT = dscr("dkT", [8, 128, S])
    dvs = dscr("dvs", [128, NKT, 1024]); sgrT = dscr("sgrT", [D, S]); sgdT = dscr("sgdT", [D, S])
    yrT = dscr("yrT", [D, S]); ydT = dscr("ydT", [D, S])

    ps = [nc.alloc_psum_tensor("ps%d" % i, [128, 512], F32) for i in range(8)]

    MOD = mem.alloc([128, depth * 72], F32)
    ACO = mem.alloc([128, depth * 24], F32)
    GCO = mem.alloc([128, depth * 24], F32)
    ones_b = mem.alloc([128, 128], BF16)
    ones_f = mem.alloc([128, 128], F32)
    neglam = mem.alloc([128, depth], F32)
    cdiff = mem.alloc([128, depth], F32)
    cret = mem.alloc([128, depth * 8], F32)
    fnc = mem.alloc([128, 8], F32)
    kdec = mem.alloc([128, 16], F32)
    epsc = mem.alloc([128, 4], F32)
    mem.set_base()

    x_view = xT.rearrange("(kc p) s -> p kc s", p=128)
    y_view = yT.rearrange("(kc p) s -> p kc s", p=128)

    def act(fn, r, w):
        return P.add("act", fn, r, w)

    def dve(fn, r, w):
        return P.add("dve", fn, r, w)

    def pool(fn, r, w):
        return P.add("pool", fn, r, w)

    def pe(fn, r, w):
        return P.add("pe", fn, r, w)

    def dma_sp(fn, r, w):
        return P.add("sp", fn, r, w, dma=True)

    def dma_pool(fn, r, w):
        return P.add("pool", fn, r, w, dma=True)

    def phase_pre():
        mem.reset()
        cnd = mem.alloc([128, 8], F32)
        cnd2 = mem.alloc([128, 8], F32)
        bad = mem.alloc([128, depth * 72], F32)
        nwt = mem.alloc([128, depth * 24], F32)
        rgn = mem.alloc([128, depth * 8], F32)
        sbl = mem.alloc([128, depth], F32)
        fnt = mem.alloc([128, 8], F32)
        lmi = mem.alloc([128, 4 * depth * 64], F32)
        lpr = mem.alloc([128, 2 * depth * 64], F32)
        lsum = mem.alloc([128, 2 * depth], F32)
        lexp = mem.alloc([128, 2 * depth], F32)
        ldif = mem.alloc([128, depth], F32)
        wa = [mem.alloc([128, 8, 1152], F32) for _ in range(2)]

        dma_sp(lambda e: e.dma_start(out=cnd[:, :], in_=cT[:, :]), [], ["cnd"])
        dma_sp(lambda e: e.dma_start(out=bad[:, :], in_=b_adaT[:, :]), [], ["bad"])
        dma_sp(lambda e: e.dma_start(out=nwt[:, :], in_=norm_wT[:, :]), [], ["nwt"])
        dma_sp(lambda e: e.dma_start(out=rgn[:, :], in_=ret_gnT[:, :]), [], ["rgn"])
        dma_sp(lambda e: e.dma_start(out=sbl[:, :], in_=sublnT[:, :]), [], ["sbl"])
        dma_sp(lambda e: e.dma_start(out=fnt[:, :], in_=fnT[:, :]), [], ["fnt"])
        dma_sp(lambda e: e.dma_start(out=lmi[:, :], in_=lam_in[:, :]), [], ["lmi"])
        dma_sp(lambda e: e.dma_start(out=kdec[:, :], in_=kdecd[:, :]), [], ["kdec"])
        dve(lambda e: e.memset(ones_b[:, :], 1.0), [], ["ones_b"])
        dve(lambda e: e.memset(ones_f[:, :], 1.0), [], ["ones_f"])
        dve(lambda e: e.memset(epsc[:, 0:1], float(D * EPS)), [], ["epsc"])
        dve(lambda e: e.memset(epsc[:, 1:2], float(128 * EPS)), [], ["epsc"])
        dve(lambda e: e.memset(epsc[:, 2:3], float(256 * EPS)), [], ["epsc"])
        act(lambda e: e.activation(cnd2[:, :], cnd[:, :], AF.Sigmoid), ["cnd"], ["cnd2"])
        dve(lambda e: e.tensor_tensor(cnd2[:, :], cnd2[:, :], cnd[:, :], ALU.mult), ["cnd", "cnd2"], ["cnd2"])
        LD = depth * 64
        dve(lambda e: e.tensor_tensor(lpr[:, 0:LD], lmi[:, 0:LD], lmi[:, LD:2 * LD], ALU.mult), ["lmi"], ["lpr0"])
        dve(lambda e: e.tensor_tensor(lpr[:, LD:2 * LD], lmi[:, 2 * LD:3 * LD], lmi[:, 3 * LD:4 * LD], ALU.mult), ["lmi"], ["lpr1"])
        dve(lambda e: e.reduce_sum(lsum[:, :], lpr[:, :].rearrange("p (a d) -> p a d", d=64), AX.X), ["lpr0", "lpr1"], ["lsum"])
        act(lambda e: e.activation(lexp[:, :], lsum[:, :], AF.Exp), ["lsum"], ["lexp"])
        dve(lambda e: e.tensor_tensor(ldif[:, :], lexp[:, depth:2 * depth], lexp[:, 0:depth], ALU.subtract), ["lexp"], ["ldif"])
        for l in range(depth):
            li = lam_init_of(l)
            dve(lambda e, l=l, li=li: e.tensor_scalar_add(neglam[:, l:l + 1], ldif[:, l:l + 1], -li), ["ldif"], ["neglam"])
            dve(lambda e, l=l, li=li: e.tensor_scalar_mul(cdiff[:, l:l + 1], sbl[:, l:l + 1], (1.0 - li) * math.sqrt(128.0)), ["sbl"], ["cdiff"])
        dve(lambda e: e.tensor_scalar_mul(cret[:, :], rgn[:, :], 16.0), ["rgn"], ["cret"])
        dve(lambda e: e.tensor_scalar_mul(fnc[:, :], fnt[:, :], 32.0), ["fnt"], ["fnc"])
        nb = 0
        for l in range(depth):
            wv = w_ada[l].rearrange("(kc p) n -> p kc n", p=128)
            for j in range(8):
                buf = wa[nb % 2]
                rn = "wa%d" % (nb % 2)
                nb += 1
                for kc in range(8):
                    dma_sp(lambda e, buf=buf, kc=kc, j=j, wv=wv: e.dma_start(out=buf[:, kc, :], in_=wv[:, kc, j * 1152:(j + 1) * 1152]), [], [rn + "_%d" % kc])
                for m in range(9):
                    col = l * 72 + j * 9 + m
                    for kc in range(8):
                        pe(lambda e, buf=buf, kc=kc, m=m, col=col: e.matmul(ps[0][:, col:col + 1], buf[:, kc, m * 128:(m + 1) * 128], cnd2[:, kc:kc + 1], start=(kc == 0), stop=(kc == 7)),
                           [rn + "_%d" % kc, "cnd2"], ["psM"])
        dve(lambda e: e.tensor_tensor(MOD[:, :], ps[0][:, 0:depth * 72], bad[:, :], ALU.add), ["psM", "bad"], ["MOD"])
        for l in range(depth):
            for s in range(3):
                o = (l * 3 + s) * 8
                sc = l * 72 + s * 24 + 8
                gc = l * 72 + s * 24 + 16
                dve(lambda e, o=o, sc=sc: e.scalar_tensor_tensor(ACO[:, o:o + 8], MOD[:, sc:sc + 8], 1.0, nwt[:, o:o + 8], ALU.add, ALU.mult), ["MOD", "nwt"], ["ACO%d" % o])
                dve(lambda e, o=o: e.tensor_scalar_mul(ACO[:, o:o + 8], ACO[:, o:o + 8], 32.0), ["ACO%d" % o], ["ACO%d" % o])
                dve(lambda e, o=o, gc=gc, s=s: e.tensor_scalar_mul(GCO[:, o:o + 8], MOD[:, gc:gc + 8], 1.0 if s == 1 else 0.5), ["MOD"], ["GCO"])
        P.barrier()

    def emit_norm(xt, xr, ht, hr, fs, rstd, l, s, tag):
        for c in range(8):
            k = c % 3
            act(lambda e, c=c, k=k: e.activation(fs[k][:, :], xt[:, c, :], AF.Square), [xr], ["fs%d" % k])
            pe(lambda e, c=c, k=k: e.matmul(ps[0][:, :], ones_f[:, :], fs[k][:, :], start=(c == 0), stop=(c == 7)), ["fs%d" % k], ["ps0"])
        act(lambda e: e.activation(rstd[:, :], ps[0][:, :], AF.Ln, bias=epsc[:, 0:1], scale=1.0), ["ps0"], ["rstd"])
        act(lambda e: e.activation(rstd[:, :], rstd[:, :], AF.Exp, scale=-0.5), ["rstd"], ["rstd"])
        o = (l * 3 + s) * 8
        sh = l * 72 + s * 24
        for c in range(8):
            k = c % 3
            dve(lambda e, c=c, k=k: e.tensor_tensor(fs[k][:, :], xt[:, c, :], rstd[:, :], ALU.mult), [xr, "rstd"], ["fs%d" % k])
            act(lambda e, c=c, k=k: e.activation(ht[:, c, :], fs[k][:, :], AF.Identity, bias=MOD[:, sh + c:sh + c + 1], scale=ACO[:, o + c:o + c + 1]),
                ["fs%d" % k], [hr])

    def phase_ffn(l, fi, s, src_view):
        mem.reset()
        wup = mem.alloc([128, 8, 2 * DFF], BF16)
        wdn = mem.alloc([128, NF, D], BF16)
        xt = [mem.alloc([128, 8, T], F32) for _ in range(2)]
        ht = mem.alloc([128, 8, T], BF16)
        gt = mem.alloc([128, NF, T], BF16)
        fs = [mem.alloc([128, T], F32) for _ in range(3)]
        rstd = mem.alloc([128, T], F32)
        upv = w_up[l, fi].rearrange("(kc p) n -> p kc n", p=128)
        dnv = w_down[l, fi].rearrange("(fc p) n -> p fc n", p=128)
        for j in range(0, NF, 4):
            w_ = min(4, NF - j) * 128
            for part in range(2):
                c_ = part * DFF + j * 128
                dma_pool(lambda e, c_=c_, w_=w_: e.dma_start(out=wup[:, :, c_:c_ + w_], in_=upv[:, :, c_:c_ + w_]), [], ["wup%d_%d" % (part, j // 4)])
        for q in range(0, NF, 2):
            dma_pool(lambda e, q=q: e.dma_start(out=wdn[:, q:q + 2, :], in_=dnv[:, q:q + 2, :]), [], ["wdn%d" % q])
        wup_r = ["wup%d" % kc for kc in range(8)]
        go = (l * 3 + s) * 8

        def load_x(t):
            b = t % 2
            dma_sp(lambda e, t=t, b=b: e.dma_start(out=xt[b][:, :, :], in_=src_view[:, :, t * T:(t + 1) * T]), [], ["x%d" % b])

        def norm(t):
            b = t % 2
            emit_norm(xt[b], "x%d" % b, ht, "ht", fs, rstd, l, s, "f")

        def up(t):
            for f in range(NF):
                pa = ps[1 + f % 2]
                pb = ps[3 + f % 2]
                ra = "ps%d" % (1 + f % 2)
                rb = "ps%d" % (3 + f % 2)
                for kc in range(8):
                    pe(lambda e, f=f, kc=kc, pa=pa: e.matmul(pa[:, :], wup[:, kc, f * 128:(f + 1) * 128], ht[:, kc, :], start=(kc == 0), stop=(kc == 7)),
                       ["wup0_%d" % (f // 4), "ht"], [ra])
                for kc in range(8):
                    pe(lambda e, f=f, kc=kc, pb=pb: e.matmul(pb[:, :], wup[:, kc, DFF + f * 128:DFF + (f + 1) * 128], ht[:, kc, :], start=(kc == 0), stop=(kc == 7)),
                       ["wup1_%d" % (f // 4), "ht"], [rb])
                k = f % 2
                act(lambda e, pa=pa, k=k: e.activation(fs[k][:, :], pa[:, :], AF.Sigmoid), [ra], ["fs%d" % k])
                dve(lambda e, pa=pa, k=k: e.tensor_tensor(fs[k][:, :], fs[k][:, :], pa[:, :], ALU.mult), ["fs%d" % k, ra], ["fs%d" % k])
                dve(lambda e, pb=pb, k=k, f=f: e.tensor_tensor(gt[:, f, :], fs[k][:, :], pb[:, :], ALU.mult), ["fs%d" % k, rb], ["gt%d" % f])

        def down(t):
            b = t % 2
            for dc in range(8):
                py = ps[5 + dc % 2]
                ry = "ps%d" % (5 + dc % 2)
                for f in range(NF):
                    pe(lambda e, f=f, dc=dc, py=py: e.matmul(py[:, :], wdn[:, f, dc * 128:(dc + 1) * 128], gt[:, f, :], start=(f == 0), stop=(f == NF - 1)),
                       ["wdn%d" % (f // 2 * 2), "gt%d" % f], [ry])
                dve(lambda e, dc=dc, py=py, b=b: e.scalar_tensor_tensor(xt[b][:, dc, :], py[:, :], GCO[:, go + dc:go + dc + 1], xt[b][:, dc, :], ALU.mult, ALU.add),
                    [ry, "x%d" % b], ["x%d" % b])
            dma_sp(lambda e, t=t, b=b: e.dma_start(out=y_view[:, :, t * T:(t + 1) * T], in_=xt[b][:, :, :]), ["x%d" % b], [])

        load_x(0)
        norm(0)
        for t in range(NT):
            if t + 1 < NT:
                load_x(t + 1)
            up(t)
            if t + 1 < NT:
                norm(t + 1)
            down(t)
        P.barrier()

    def phase_m1(l):
        mem.reset()
        win = mem.alloc([128, 8, 8192], BF16)
        xt = mem.alloc([128, 8, T], F32)
        ht = [mem.alloc([128, 8, T], BF16) for _ in range(2)]
        stg = [mem.alloc([128, 2048], BF16) for _ in range(6)]
        fs = [mem.alloc([128, T], F32) for _ in range(3)]
        rstd = mem.alloc([128, T], F32)
        qdec = mem.alloc([128, 4, T], F32)
        wv = w_in[l].rearrange("(kc p) n -> p kc n", p=128)
        for blk in (0, 1, 2, 3, 10, 11, 4, 5, 6, 7, 8, 9, 12, 13, 14, 15):
            dma_pool(lambda e, blk=blk: e.dma_start(out=win[:, :, blk * 512:(blk + 1) * 512], in_=wv[:, :, blk * 512:(blk + 1) * 512]), [], ["win_%d" % blk])
        dma_sp(lambda e: e.dma_start(out=qdec[:, :, :], in_=qdecd[:, :, :]), [], ["qdec"])
        st = {"bank": 0, "slot": 0, "ev": 0}

        def nbank():
            b = 1 + st["bank"] % 6
            st["bank"] += 1
            return ps[b], "ps%d" % b

        def nslot():
            k = st["slot"] % 6
            st["slot"] += 1
            return stg[k], "stg%d" % k

        def load_x(t):
            dma_sp(lambda e, t=t: e.dma_start(out=xt[:, :, :], in_=y_view[:, :, t * T:(t + 1) * T]), [], ["x"])

        def norm(t):
            emit_norm(xt, "x", ht[t % 2], "ht%d" % (t % 2), fs, rstd, l, 1, "m")

        def fm_group(t, col, evac):
            h_ = ht[t % 2]
            hr = "ht%d" % (t % 2)
            pb, rb = nbank()
            hh = col // 4096
            for kc in range(8):
                pe(lambda e, kc=kc, pb=pb, col=col, h_=h_: e.matmul(pb[:, :], win[:, kc, col:col + 128], h_[:, kc, :], start=(kc == 0), stop=(kc == 7)),
                   ["win_%d" % (col // 512), hr], [rb])
            evac(pb, rb)

        def tm_group(t, j, col, evac):
            h_ = ht[t % 2]
            hr = "ht%d" % (t % 2)
            pb, rb = nbank()
            hh = col // 4096
            for kc in range(8):
                pe(lambda e, kc=kc, pb=pb, col=col, h_=h_, j=j: e.matmul(pb[:, :], h_[:, kc, j * 128:(j + 1) * 128], win[:, kc, col:col + 512], start=(kc == 0), stop=(kc == 7)),
                   ["win_%d" % (col // 512), hr], [rb])
            evac(pb, rb)

        def proj_a(t):
            t0 = t * T
            sq_, rq_ = nslot()
            sqd, rqd_ = nslot()
            for h in range(4):
                def ev(pb, rb, h=h):
                    act(lambda e: e.activation(sq_[:, h * T:(h + 1) * T], pb[:, :], AF.Identity), [rb], [rq_ + "a"])
                    dve(lambda e: e.tensor_tensor(sqd[:, h * T:(h + 1) * T], pb[:, :], qdec[:, h, :], ALU.mult), [rb, "qdec"], [rqd_ + "d"])
                fm_group(t, h * 128, ev)
            if DBG2 >= 2:
                dma_sp(lambda e: e.dma_start(out=rqT[:, :, t0:t0 + T].rearrange("h d s -> d h s"), in_=sq_[:, :].rearrange("p (h s) -> p h s", h=4)), [rq_ + "a", rq_ + "d"], [])
                dma_sp(lambda e: e.dma_start(out=rqdT[:, :, t0:t0 + T].rearrange("h d s -> d h s"), in_=sqd[:, :].rearrange("p (h s) -> p h s", h=4)), [rqd_ + "a", rqd_ + "d"], [])
            if DBG2 <= 2:
                return None
            sk, rk_ = nslot()
            for h in range(4):
                def ev(pb, rb, h=h):
                    act(lambda e: e.activation(sk[:, h * T:(h + 1) * T], pb[:, :], AF.Identity, scale=float(128.0 ** -0.5)), [rb], [rk_ + "a"])
                fm_group(t, 512 + h * 128, ev)
            dma_sp(lambda e: e.dma_start(out=rkT[:, :, t0:t0 + T].rearrange("h d s -> d h s"), in_=sk[:, :].rearrange("p (h s) -> p h s", h=4)), [rk_ + "a", rk_ + "d"], [])

            def chunked(colbase, dst, mode):
                dview = dst.rearrange("(c p) s -> p c s", p=128)
                for c0 in range(0, 8, 4):
                    sl, rs = nslot()
                    for cc in range(4):
                        c = c0 + cc

                        def ev(pb, rb, cc=cc):
                            o = sl[:, cc * T:(cc + 1) * T]
                            if mode == "silu":
                                k = st["ev"] % 3
                                st["ev"] += 1
                                act(lambda e: e.activation(fs[k][:, :], pb[:, :], AF.Sigmoid), [rb], ["fs%d" % k])
                                dve(lambda e: e.tensor_tensor(o, fs[k][:, :], pb[:, :], ALU.mult), [rb, "fs%d" % k], [rs + "d"])
                            elif mode == "sig":
                                act(lambda e: e.activation(o, pb[:, :], AF.Sigmoid), [rb], [rs + "a"])
                            elif mode == "q":
                                dve(lambda e: e.tensor_scalar_mul(o, pb[:, :], 0.125), [rb], [rs + "d"])
                            else:
                                dve(lambda e: e.tensor_copy(o, pb[:, :]), [rb], [rs + "d"])
                        fm_group(t, colbase + c * 128, ev)
                    dma_sp(lambda e, sl=sl, c0=c0: e.dma_start(out=dview[:, c0:c0 + 4, t0:t0 + T], in_=sl[:, :].rearrange("p (c s) -> p c s", c=4)), [rs + "a", rs + "d"], [])
            return chunked

        def proj_a2(t, chunked):
            chunked(2048, rgT, "silu")
            chunked(3072, dqT.rearrange("h r s -> (h r) s"), "q")
            chunked(4096, dkT.rearrange("h r s -> (h r) s"), "k")
            chunked(6144, sgrT, "sig")
            chunked(7168, sgdT, "sig")

        def proj_b(t):
            kt0 = t * 4
            skd, rkd_ = nslot()
            for j in range(4):
                def ev(pb, rb, j=j):
                    for h in range(4):
                        dve(lambda e, h=h: e.tensor_scalar_mul(skd[:, j * 512 + h * 128:j * 512 + (h + 1) * 128], pb[:, h * 128:(h + 1) * 128], kdec[:, h * 4 + j:h * 4 + j + 1]),
                            [rb, "kdec"], [rkd_ + "d"])
                tm_group(t, j, 512, ev)
            dma_sp(lambda e: e.dma_start(out=rkd[:, kt0:kt0 + 4, :], in_=skd[:, :].rearrange("p (j n) -> p j n", j=4)), [rkd_ + "a", rkd_ + "d"], [])
            for colbase, dst in ((1024, rv), (5120, dvs)):
                for jp in range(2):
                    sl, rs = nslot()
                    for jj in range(2):
                        j = jp * 2 + jj
                        for half in range(2):
                            def ev(pb, rb, jj=jj, half=half, sl=sl, rs=rs):
                                o = sl[:, jj * 1024 + half * 512:jj * 1024 + (half + 1) * 512]
                                if half == 0:
                                    act(lambda e: e.activation(o, pb[:, :], AF.Identity), [rb], [rs + "a"])
                                else:
                                    dve(lambda e: e.tensor_copy(o, pb[:, :]), [rb], [rs + "d"])
                            tm_group(t, j, colbase + half * 512, ev)
                    dma_sp(lambda e, sl=sl, jp=jp, dst=dst: e.dma_start(out=dst[:, kt0 + jp * 2:kt0 + jp * 2 + 2, :], in_=sl[:, :].rearrange("p (j n) -> p j n", j=2)), [rs + "a", rs + "d"], [])

        load_x(0)
        norm(0)
        for t in range(NT):
            if t + 1 < NT:
                load_x(t + 1)
            if DBG >= 2:
                ch = proj_a(t)
            if DBG >= 3:
                proj_b(t)
            if t + 1 < NT:
                norm(t + 1)
            if DBG >= 4:
                proj_a2(t, ch)
        P.barrier()

    def phase_m2d(l):
        mem.reset()
        KT = [[mem.alloc([128, S], BF16) for _ in range(2)] for _ in range(2)]
        VV = [mem.alloc([128, NKT, 128], BF16) for _ in range(2)]
        QT = [mem.alloc([128, 2, T], BF16) for _ in range(2)]
        PT = [mem.alloc([128, T], BF16) for _ in range(4)]
        C0 = mem.alloc([128, 4, T], F32)
        sfix = [mem.alloc([128, T], F32) for _ in range(2)]
        EP = [[mem.alloc([128, T], F32) for _ in range(2)] for _ in range(7)]
        ydst = [mem.alloc([128, T], BF16) for _ in range(2)]
        dma_sp(lambda e: e.dma_start(out=C0[:, :, :], in_=c0d[:, :, :]), [], ["C0"])
        cnt = {"i": 0, "q": 0, "o": 0}
        slopes = [2.0 ** (-(h + 1)) for h in range(8)]

        def kres(b):
            return ["K%d_%d%s" % (b, s, x) for s in range(2) for x in "ra"]

        def qres(qb):
            return ["Q%d_%d%s" % (qb, s, x) for s in range(2) for x in "ra"]

        def load_head(h):
            b = h % 2
            for s in range(2):
                dma_sp(lambda e, s=s: e.dma_start(out=KT[b][s][0:64, :], in_=dkT[h, s * 64:(s + 1) * 64, :]), [], ["K%d_%dr" % (b, s)])
                dma_pool(lambda e, s=s: e.dma_start(out=KT[b][s][64:68, :], in_=kaug[h, :, :]), [], ["K%d_%da" % (b, s)])
            dma_sp(lambda e: e.dma_start(out=VV[b][:, :, :], in_=dvs[:, :, h * 128:(h + 1) * 128]), [], ["V%d" % b])

        def load_q(h, qi):
            qb = cnt["q"] % 2
            cnt["q"] += 1
            q0 = qi * T
            for s in range(2):
                dma_sp(lambda e, s=s: e.dma_start(out=QT[qb][0:64, s, :], in_=dqT[h, s * 64:(s + 1) * 64, q0:q0 + T]), [], ["Q%d_%dr" % (qb, s)])
                dma_pool(lambda e, s=s: e.dma_start(out=QT[qb][64:68, s, :], in_=qaug[:, q0:q0 + T]), [], ["Q%d_%da" % (qb, s)])
            return qb

        def do_tile(h, qi, b, qb):
            nk = 4 * qi + 4
            lim = qi * T - 127 - SKIP_T / slopes[h]
            kt_lo = max(0, int(math.floor(lim / 128.0)) + 1) if lim >= 0 else 0
            steps = [(kt, s) for kt in range(kt_lo, nk) for s in range(2)]

            def c0_of(kt):
                m_ = kt - 4 * qi
                return 128 * m_ if m_ > 0 else 0
            base = cnt["i"]
            kr = kres(b)
            qr = qres(qb)
            slope = float(slopes[h])

            def s_mm(i):
                kt, s = steps[i]
                g = base + i
                pb = ps[1 + g % 3]
                c0 = c0_of(kt)
                pe(lambda e: e.matmul(pb[:, c0:T], KT[b][s][0:68, kt * 128:(kt + 1) * 128], QT[qb][0:68, s, c0:T], start=True, stop=True),
                   kr + qr, ["ps%d" % (1 + g % 3)])

            def step(i):
                kt, s = steps[i]
                g = base + i
                pb = ps[1 + g % 3]
                rb = "ps%d" % (1 + g % 3)
                pt = PT[g % 4]
                rp = "PT%d" % (g % 4)
                m = kt - 4 * qi
                c0 = c0_of(kt)
                if m >= 0:
                    sf = sfix[g % 2]
                    rsf = "sfix%d" % (g % 2)
                    dve(lambda e: e.scalar_tensor_tensor(sf[:, c0:T], C0[:, m, c0:T], slope, pb[:, c0:T], ALU.mult, ALU.add), ["C0", rb], [rsf])
                    act(lambda e: e.activation(pt[:, c0:T], sf[:, c0:T], AF.Exp), [rsf], [rp])
                else:
                    act(lambda e: e.activation(pt[:, c0:T], pb[:, c0:T], AF.Exp), [rb], [rp])
                po = ps[4 + s]
                pz = ps[6 + s]
                pe(lambda e: e.matmul(po[:, c0:T], VV[b][:, kt, :], pt[:, c0:T], start=(kt == kt_lo), stop=(kt == nk - 1)),
                   ["V%d" % b, rp], ["ps%d" % (4 + s)])
                pe(lambda e: e.matmul(pz[:, c0:T], ones_b[:, :], pt[:, c0:T], start=(kt == kt_lo), stop=(kt == nk - 1)),
                   [rp], ["ps%d" % (6 + s)])

            s_mm(0)
            s_mm(1)
            for i in range(len(steps)):
                if i + 2 < len(steps):
                    s_mm(i + 2)
                step(i)
            cnt["i"] += len(steps)
            ob = cnt["o"] % 2
            cnt["o"] += 1
            R0_, R1_, oo_, t1_, sq_, ln_, rs_ = [x[ob] for x in EP]
            sfx = "_%d" % ob
            dve(lambda e: e.tensor_copy(oo_[:, :], ps[4][:, :]), ["ps4"], ["oo" + sfx])
            dve(lambda e: e.tensor_copy(t1_[:, :], ps[5][:, :]), ["ps5"], ["t1" + sfx])
            act(lambda e: e.activation(ln_[:, :], ps[6][:, :], AF.Ln), ["ps6"], ["lnv" + sfx])
            act(lambda e: e.activation(rs_[:, :], ps[7][:, :], AF.Ln), ["ps7"], ["rstd" + sfx])
            act(lambda e: e.activation(R0_[:, :], ln_[:, :], AF.Exp, scale=-1.0), ["lnv" + sfx], ["R0" + sfx])
            act(lambda e: e.activation(R1_[:, :], rs_[:, :], AF.Exp, scale=-1.0), ["rstd" + sfx], ["R1" + sfx])
            dve(lambda e: e.tensor_tensor(oo_[:, :], oo_[:, :], R0_[:, :], ALU.mult), ["oo" + sfx, "R0" + sfx], ["oo" + sfx])
            dve(lambda e: e.tensor_tensor(t1_[:, :], t1_[:, :], R1_[:, :], ALU.mult), ["t1" + sfx, "R1" + sfx], ["t1" + sfx])
            dve(lambda e: e.scalar_tensor_tensor(oo_[:, :], t1_[:, :], neglam[:, l:l + 1], oo_[:, :], ALU.mult, ALU.add), ["t1" + sfx, "oo" + sfx], ["oo" + sfx])
            pool(lambda e: e.tensor_tensor(sq_[:, :], oo_[:, :], oo_[:, :], ALU.mult), ["oo" + sfx], ["sqo" + sfx])
            pe(lambda e: e.matmul(ps[0][:, :], ones_f[:, :], sq_[:, :], start=True, stop=True), ["sqo" + sfx], ["ps0"])
            act(lambda e: e.activation(ln_[:, :], ps[0][:, :], AF.Ln, bias=epsc[:, 1:2], scale=1.0), ["ps0"], ["lnv" + sfx])
            act(lambda e: e.activation(rs_[:, :], ln_[:, :], AF.Exp, scale=-0.5), ["lnv" + sfx], ["rstd" + sfx])
            dve(lambda e: e.scalar_tensor_tensor(ydst[ob][:, :], oo_[:, :], cdiff[:, l:l + 1], rs_[:, :], ALU.mult, ALU.mult), ["oo" + sfx, "rstd" + sfx], ["yd%d" % ob])
            q0 = qi * T
            dma_pool(lambda e: e.dma_start(out=ydT[h * 128:(h + 1) * 128, q0:q0 + T], in_=ydst[ob][:, :]), ["yd%d" % ob], [])

        seq = [(h, qi) for h in range(8) for qi in range(NT)]
        load_head(0)
        qb_next = load_q(0, 0)
        for idx, (h, qi) in enumerate(seq):
            if qi == 0 and h + 1 < 8:
                load_head(h + 1)
            qb = qb_next
            if idx + 1 < len(seq):
                qb_next = load_q(*seq[idx + 1])
            do_tile(h, qi, h % 2, qb)
        P.barrier()

    def phase_m2r(l):
        mem.reset()
        NB = S // T
        Dt = mem.alloc([128, 16, T], F32)
        qb_ = [mem.alloc([128, 4, T], BF16) for _ in range(2)]
        qdb = [mem.alloc([128, 4, T], BF16) for _ in range(2)]
        kb = [mem.alloc([128, 4, T], BF16) for _ in range(2)]
        kdb = [mem.alloc([128, 4, 512], BF16) for _ in range(2)]
        vb = [mem.alloc([128, 4, 1024], BF16) for _ in range(2)]
        rgb = [mem.alloc([128, 8, T], BF16) for _ in range(2)]
        Pm = [mem.alloc([128, 4, T], BF16) for _ in range(2)]
        st32 = mem.alloc([128, 4, 256], F32)
        stb = mem.alloc([128, 4, 256], BF16)
        y32 = [mem.alloc([128, 2, T], F32) for _ in range(2)]
        sqv = [mem.alloc([128, 2, T], F32) for _ in range(2)]
        rstd = mem.alloc([128, T], F32)
        tmp = mem.alloc([128, T], F32)
        ost = [mem.alloc([128, 8, T], BF16) for _ in range(2)]
        dma_sp(lambda e: e.dma_start(out=Dt[:, :, :], in_=dtd.rearrange("p h m s -> p (h m) s")), [], ["Dt"])
        gam = [1.0 - 2.0 ** (-5.0 - h) for h in range(4)]
        g512 = [float(np.float32(np.exp(np.float64(T) * np.log(np.float64(np.float32(g)))))) for g in gam]
        yr_view = yrT.rearrange("(c p) s -> p c s", p=128)
        rg_view = rgT.rearrange("(c p) s -> p c s", p=128)

        def load_blk(bi):
            b = bi % 2
            t0 = bi * T
            dma_sp(lambda e: e.dma_start(out=qb_[b][:, :, :], in_=rqT[:, :, t0:t0 + T].rearrange("h d s -> d h s")), [], ["q%d" % b])
            dma_sp(lambda e: e.dma_start(out=qdb[b][:, :, :], in_=rqdT[:, :, t0:t0 + T].rearrange("h d s -> d h s")), [], ["qd%d" % b])
            dma_sp(lambda e: e.dma_start(out=kb[b][:, :, :], in_=rkT[:, :, t0:t0 + T].rearrange("h d s -> d h s")), [], ["k%d" % b])
            dma_sp(lambda e: e.dma_start(out=kdb[b][:, :, :], in_=rkd[:, bi * 4:bi * 4 + 4, :]), [], ["kd%d" % b])
            dma_sp(lambda e: e.dma_start(out=vb[b][:, :, :], in_=rv[:, bi * 4:bi * 4 + 4, :]), [], ["v%d" % b])
            dma_sp(lambda e: e.dma_start(out=rgb[b][:, :, :], in_=rg_view[:, :, t0:t0 + T]), [], ["rg%d" % b])

        def do_head(bi, b, h, gi):
            pmb = Pm[gi % 2]
            rpm = "Pm%d" % (gi % 2)
            yb = y32[gi % 2]
            ryb = "y32_%d" % (gi % 2)
            osb = ost[b]
            ros = "ost%d" % b
            ybase = 3 if gi % 2 == 0 else 6
            sqb = sqv[gi % 2]
            rsq = "sqv%d_" % (gi % 2)

            def smm(m):
                pb = ps[1 + m % 2]
                rb = "ps%d" % (1 + m % 2)
                pe(lambda e: e.matmul(pb[:, :], kb[b][:, h, m * 128:(m + 1) * 128], qb_[b][:, h, :], start=True, stop=True), ["k%d" % b, "q%d" % b], [rb])
                dve(lambda e: e.tensor_tensor(pmb[:, m, :], pb[:, :], Dt[:, h * 4 + m, :], ALU.mult), [rb, "Dt"], [rpm + "_%d" % m])
            for m in range(4):
                smm(m)

            def ymm(vc):
                py = ps[ybase + vc]
                ry = "ps%d" % (ybase + vc)
                for m in range(4):
                    pe(lambda e, m=m: e.matmul(py[:, :], vb[b][:, m, h * 256 + vc * 128:h * 256 + (vc + 1) * 128], pmb[:, m, :], start=(m == 0), stop=(m == 3 and bi == 0)),
                       ["v%d" % b, rpm + "_%d" % m], [ry])
                if bi > 0:
                    pe(lambda e: e.matmul(py[:, :], stb[:, h, vc * 128:(vc + 1) * 128], qdb[b][:, h, :], start=False, stop=True), ["stb%d" % h, "qd%d" % b], [ry])
            for vc in range(2):
                ymm(vc)
            for m in range(4):
                pe(lambda e, m=m: e.matmul(ps[5][:, 0:256], kdb[b][:, m, h * 128:(h + 1) * 128], vb[b][:, m, h * 256:(h + 1) * 256], start=(m == 0), stop=(m == 3)),
                   ["kd%d" % b, "v%d" % b], ["ps5"])
            if bi == 0:
                dve(lambda e: e.tensor_copy(st32[:, h, :], ps[5][:, 0:256]), ["ps5"], ["st32_%d" % h])
            else:
                dve(lambda e: e.scalar_tensor_tensor(st32[:, h, :], st32[:, h, :], float(g512[h]), ps[5][:, 0:256], ALU.mult, ALU.add), ["ps5", "st32_%d" % h], ["st32_%d" % h])
            if bi + 1 < NB:
                pool(lambda e: e.tensor_copy(stb[:, h, :], st32[:, h, :]), ["st32_%d" % h], ["stb%d" % h])

            def gn(vc):
                py = ps[ybase + vc]
                ry = "ps%d" % (ybase + vc)
                act(lambda e: e.activation(sqb[:, vc, :], py[:, :], AF.Square), [ry], [rsq + "%d" % vc])
                act(lambda e: e.activation(yb[:, vc, :], py[:, :], AF.Identity), [ry], [ryb + "_%d" % vc])
                pe(lambda e: e.matmul(ps[0][:, :], ones_f[:, :], sqb[:, vc, :], start=(vc == 0), stop=(vc == 1)), [rsq + "%d" % vc], ["ps0"])
            for vc in range(2):
                gn(vc)
            act(lambda e: e.activation(rstd[:, :], ps[0][:, :], AF.Ln, bias=epsc[:, 2:3], scale=1.0), ["ps0"], ["rstd"])
            act(lambda e: e.activation(rstd[:, :], rstd[:, :], AF.Exp, scale=-0.5), ["rstd"], ["rstd"])

            def fin(vc):
                c = h * 2 + vc
                dve(lambda e: e.scalar_tensor_tensor(tmp[:, :], yb[:, vc, :], cret[:, l * 8 + c:l * 8 + c + 1], rstd[:, :], ALU.mult, ALU.mult), [ryb + "_%d" % vc, "rstd"], ["tmp"])
                pool(lambda e: e.tensor_tensor(osb[:, c, :], tmp[:, :], rgb[b][:, c, :], ALU.mult), ["tmp", "rg%d" % b], [ros])
            for vc in range(2):
                fin(vc)

        def store_blk(bi, b):
            t0 = bi * T
            dma_pool(lambda e: e.dma_start(out=yr_view[:, :, t0:t0 + T], in_=ost[b][:, :, :]), ["ost%d" % b], [])

        load_blk(0)
        gi = 0
        for bi in range(NB):
            if bi + 1 < NB:
                load_blk(bi + 1)
            for h in range(4):
                do_head(bi, bi % 2, h, gi)
                gi += 1
            store_blk(bi, bi % 2)
        P.barrier()

    def phase_m3(l):
        mem.reset()
        wr = mem.alloc([128, 8, D], BF16); wd = mem.alloc([128, 8, D], BF16); wo = mem.alloc([128, 8, D], BF16)
        xt = [mem.alloc([128, 8, T], F32) for _ in range(2)]
        yr = [mem.alloc([128, 8, T], BF16) for _ in range(2)]
        yd = [mem.alloc([128, 8, T], BF16) for _ in range(2)]
        sr = [mem.alloc([128, 8, T], BF16) for _ in range(2)]
        sd = [mem.alloc([128, 8, T], BF16) for _ in range(2)]
        mg = mem.alloc([128, 8, T], BF16)
        ta = [mem.alloc([128, T], F32) for _ in range(2)]
        tb = [mem.alloc([128, T], F32) for _ in range(2)]

        def loadw(w_s, w_d, nm):
            v = w_d[l].rearrange("(kc p) n -> p kc n", p=128)
            for hh in range(2):
                dma_pool(lambda e, hh=hh: e.dma_start(out=w_s[:, hh * 4:(hh + 1) * 4, :], in_=v[:, hh * 4:(hh + 1) * 4, :]), [], ["%s%d" % (nm, hh)])
        loadw(wr, w_rb, "wr"); loadw(wd, w_db, "wd"); loadw(wo, w_out, "wo")
        views = [t_.rearrange("(c p) s -> p c s", p=128) for t_ in (yrT, ydT, sgrT, sgdT)]
        go = (l * 3 + 1) * 8

        def load(t):
            b = t % 2
            t0 = t * T
            dma_sp(lambda e: e.dma_start(out=xt[b][:, :, :], in_=y_view[:, :, t0:t0 + T]), [], ["x%d" % b])
            for buf, v, nm in ((yr, views[0], "yr"), (yd, views[1], "yd"), (sr, views[2], "sr"), (sd, views[3], "sd")):
                dma_sp(lambda e, buf=buf, v=v: e.dma_start(out=buf[b][:, :, :], in_=v[:, :, t0:t0 + T]), [], ["%s%d" % (nm, b)])

        def do_tile(t, b):
            def branch(dc):
                pr = ps[1 + dc % 2]; rr = "ps%d" % (1 + dc % 2)
                pd = ps[3 + dc % 2]; rd = "ps%d" % (3 + dc % 2)
                for kc in range(8):
                    pe(lambda e, kc=kc: e.matmul(pr[:, :], wr[:, kc, dc * 128:(dc + 1) * 128], yr[b][:, kc, :], start=(kc == 0), stop=(kc == 7)), ["wr%d" % (kc // 4), "yr%d" % b], [rr])
                for kc in range(8):
                    pe(lambda e, kc=kc: e.matmul(pd[:, :], wd[:, kc, dc * 128:(dc + 1) * 128], yd[b][:, kc, :], start=(kc == 0), stop=(kc == 7)), ["wd%d" % (kc // 4), "yd%d" % b], [rd])
                k = dc % 2
                dve(lambda e: e.tensor_tensor(ta[k][:, :], pr[:, :], sr[b][:, dc, :], ALU.mult), [rr, "sr%d" % b], ["ta%d" % k])
                dve(lambda e: e.tensor_tensor(tb[k][:, :], pd[:, :], sd[b][:, dc, :], ALU.mult), [rd, "sd%d" % b], ["tb%d" % k])
                pool(lambda e: e.tensor_tensor(mg[:, dc, :], ta[k][:, :], tb[k][:, :], ALU.add), ["ta%d" % k, "tb%d" % k], ["mg%d" % dc])
            for dc in range(8):
                branch(dc)

            def outp(dc):
                po = ps[5 + dc % 2]; ro = "ps%d" % (5 + dc % 2)
                for kc in range(8):
                    pe(lambda e, kc=kc: e.matmul(po[:, :], wo[:, kc, dc * 128:(dc + 1) * 128], mg[:, kc, :], start=(kc == 0), stop=(kc == 7)), ["wo%d" % (kc // 4), "mg%d" % kc], [ro])
                dve(lambda e: e.scalar_tensor_tensor(xt[b][:, dc, :], po[:, :], GCO[:, go + dc:go + dc + 1], xt[b][:, dc, :], ALU.mult, ALU.add), [ro, "x%d" % b], ["x%d" % b])
            for dc in range(8):
                outp(dc)
            t0 = t * T
            dma_sp(lambda e: e.dma_start(out=y_view[:, :, t0:t0 + T], in_=xt[b][:, :, :]), ["x%d" % b], [])

        load(0)
        for t in range(NT):
            if t + 1 < NT:
                load(t + 1)
            do_tile(t, t % 2)
        P.barrier()

    def phase_fin():
        mem.reset()
        xt = [mem.alloc([128, 8, T], F32) for _ in range(2)]
        fs = [mem.alloc([128, T], F32) for _ in range(3)]
        rstd = mem.alloc([128, T], F32)

        def load(t):
            b = t % 2
            dma_sp(lambda e: e.dma_start(out=xt[b][:, :, :], in_=y_view[:, :, t * T:(t + 1) * T]), [], ["x%d" % b])

        def do_tile(t, b):
            def st(c):
                k = c % 3
                act(lambda e: e.activation(fs[k][:, :], xt[b][:, c, :], AF.Square), ["x%d" % b], ["fs%d" % k])
                pe(lambda e: e.matmul(ps[0][:, :], ones_f[:, :], fs[k][:, :], start=(c == 0), stop=(c == 7)), ["fs%d" % k], ["ps0"])
            for c in range(8):
                st(c)
            act(lambda e: e.activation(rstd[:, :], ps[0][:, :], AF.Ln, bias=epsc[:, 0:1], scale=1.0), ["ps0"], ["rstd"])
            act(lambda e: e.activation(rstd[:, :], rstd[:, :], AF.Exp, scale=-0.5), ["rstd"], ["rstd"])

            def sc(c):
                dve(lambda e: e.scalar_tensor_tensor(xt[b][:, c, :], xt[b][:, c, :], fnc[:, c:c + 1], rstd[:, :], ALU.mult, ALU.mult), ["x%d" % b, "rstd"], ["x%d" % b])
            for c in range(8):
                sc(c)
            dma_sp(lambda e: e.dma_start(out=y_view[:, :, t * T:(t + 1) * T], in_=xt[b][:, :, :]), ["x%d" % b], [])

        load(0)
        for t in range(NT):
            if t + 1 < NT:
                load(t + 1)
            do_tile(t, t % 2)
        P.barrier()

    plist = [phase_pre]
    for l in range(depth):
        plist.append(lambda l=l: phase_ffn(l, 0, 0, x_view if l == 0 else y_view))
        plist.append(lambda l=l: phase_m1(l))
        plist.append(lambda l=l: phase_m2d(l))
        plist.append(lambda l=l: phase_m2r(l))
        plist.append(lambda l=l: phase_m3(l))
        plist.append(lambda l=l: phase_ffn(l, 1, 2, y_view))
    plist.append(phase_fin)
    for pf in plist[:upto]:
        pf()
    P.finalize()

    from contextlib import ExitStack
    sems = {}
    with ExitStack() as es:
        for k in ("pe", "act", "dve", "pool", "bar"):
            sems[k] = es.enter_context(nc.semaphore("s_" + k))
        for q in ("sp", "pool"):
            for i in range(KDMA):
                sems["%s_d%d" % (q, i)] = es.enter_context(nc.semaphore("s_%s_d%d" % (q, i)))
        with nc.Block() as block:
            @block.tensor
            def _(e):
                P.run_stream("pe", e, sems)

            @block.scalar
            def _(e):
                P.run_stream("act", e, sems)

            @block.vector
            def _(e):
                P.run_stream("dve", e, sems)

            @block.gpsimd
            def _(e):
                P.run_stream("pool", e, sems)

            @block.sync
            def _(e):
                P.run_stream("sp", e, sems)
    return nc


def make_consts(S):
    pos = np.arange(S)
    a = (pos // 128).astype(np.float64)
    b_ = (pos % 128).astype(np.float64)
    qaug = np.stack([-128.0 * a, -b_, np.ones(S), np.ones(S)]).astype(np.float32)
    kaug = np.zeros((8, 4, S), np.float32)
    for h in range(8):
        sl = 2.0 ** (-(h + 1))
        kaug[h, 0] = sl
        kaug[h, 1] = sl
        kaug[h, 2] = sl * 128.0 * a
        kaug[h, 3] = sl * b_
    i = np.arange(128)[:, None, None]
    m = np.arange(4)[None, :, None]
    j = np.arange(512)[None, None, :]
    kp = 128 * m + i
    allowed = (kp // 64) <= (j // 64)
    c0 = np.where(allowed, np.where(kp > j, -2.0 * (kp - j), 0.0), -1.0e6).astype(np.float32)
    dtd = np.zeros((128, 4, 4, 512), np.float32)
    qdec = np.zeros((128, 4, 512), np.float32)
    kdec = np.zeros((128, 16), np.float32)
    for h in range(4):
        g = np.float64(np.float32(1.0 - 2.0 ** (-5.0 - h)))
        lg = np.log(g)
        dd = np.where(allowed, np.exp(lg * np.abs(j - kp)), 0.0)
        dtd[:, h, :, :] = dd.astype(np.float32)
        qdec[:, h, :] = np.exp(lg * np.arange(512))[None, :].astype(np.float32)
        for jj in range(4):
            r = 128 * jj + np.arange(128)
            kdec[:, h * 4 + jj] = (np.exp(lg * (512 - r)) * (128.0 ** -0.5)).astype(np.float32)
    return dict(qaug=qaug, kaug=kaug, c0d=c0, dtd=dtd, qdecd=qdec, kdecd=kdec)


_CACHE = {}


def run(inputs, S, depth, n_cores, upto=99):
    key = (S, depth, upto)
    if key not in _CACHE:
        _CACHE[key] = build_program(S, depth, upto)
    nc = _CACHE[key]
    f = lambda a: np.ascontiguousarray(np.asarray(a, dtype=np.float32))
    x = f(inputs["x"]); c = f(inputs["c"])
    shared = dict(
        w_ada=f(inputs["w_ada"]),
        b_adaT=f(f(inputs["b_ada"]).reshape(depth, 72, 128).transpose(2, 0, 1).reshape(128, depth * 72)),
        norm_wT=f(f(inputs["norm_w"]).reshape(depth, 3, 8, 128).transpose(3, 0, 1, 2).reshape(128, depth * 24)),
        w_up=f(inputs["w_ffn_up"]), w_down=f(inputs["w_ffn_down"]), w_in=f(inputs["w_in"]),
        ret_gnT=f(f(inputs["ret_gn"]).reshape(depth, 8, 128).transpose(2, 0, 1).reshape(128, depth * 8)),
        lam_in=f(np.broadcast_to(np.stack([f(inputs["lambda_q1"]), f(inputs["lambda_k1"]), f(inputs["lambda_q2"]), f(inputs["lambda_k2"])]).reshape(1, -1), (128, 4 * depth * 64))),
        sublnT=f(f(inputs["diff_subln"]).T),
        w_rb=f(inputs["w_ret_branch"]), w_db=f(inputs["w_diff_branch"]), w_out=f(inputs["w_out"]),
        fnT=f(f(inputs["final_norm"]).reshape(8, 128).T),
    )
    shared.update(make_consts(S))
    in_maps = []
    for b in range(n_cores):
        m = dict(shared)
        m["xT"] = f(x[b].T)
        m["cT"] = f(c[b].reshape(8, 128).T)
        in_maps.append(m)
    res = run_bass_kernel_spmd(nc, in_maps, core_ids=list(range(n_cores)))
    out = np.stack([np.ascontiguousarray(np.asarray(r["yT"]).T) for r in res.results])
    return out.astype(np.float32)


def kernel(**inputs):
    return run(inputs, SEQ, NLAYERS, 8)
```
